# Optimizing a Trainium2 kernel written in Bass

```python
import math
import jax
import jax.numpy as jnp
from jax import lax
import numpy as np

D_MODEL = 1024
BATCH = 16
SEQ = 256
DEPTH = 4
DEC_BATCH = 2
DEC_SEQ = 2048
PAST_LEN = 512

GRID_W = 64

N_MIXERS = 3
N_HYENA = (DEPTH + 2) // 3
N_MLA = (DEPTH + 1) // 3
N_RWKV = DEPTH // 3

DEEPNORM_ALPHA = (2.0 * DEPTH) ** 0.25
DEEPNORM_BETA = (8.0 * DEPTH) ** -0.25
LN_EPS = 1e-5
RMS_EPS = 1e-6

HY_WIDTH = D_MODEL
HY_BANDS = 16
HY_EMB = 1 + 2 * HY_BANDS
HY_FFN = 64
HY_DECAY_MIN = -math.log(1e-2) / 1.5
HY_DECAY_MAX = -math.log(1e-2) / 0.3

MLA_HEADS = 16
MLA_Q_RANK = 256
MLA_KV_RANK = 128
MLA_NOPE_DIM = 64
MLA_ROPE_DIM = 32
MLA_V_DIM = 64
ROPE_BASE = 10000.0
Q_BLOCK = 128

RW_HEAD_DIM = 64
RW_HEADS = D_MODEL // RW_HEAD_DIM
RW_DECAY_LORA = 64
RW_A_LORA = 64
RW_GN_EPS = 64e-5

kernel_name = 'hybrid_hyena_mla_rwkv7_diffusion_step'


def _layer_norm(x, g, b):
    xf = x.astype(jnp.float32)
    mu = jnp.mean(xf, -1, keepdims=True)
    var = jnp.mean(jnp.square(xf - mu), -1, keepdims=True)
    return ((xf - mu) * lax.rsqrt(var + LN_EPS) * g + b).astype(x.dtype)


def _rms_norm(x, g):
    xf = x.astype(jnp.float32)
    return (xf * lax.rsqrt(jnp.mean(xf * xf, -1, keepdims=True) + RMS_EPS) * g).astype(x.dtype)


def _neighbours(u):
    up = jnp.pad(u, ((0, 0), (1, 1), (0, 0)))
    return up[:, :-2], up[:, 2:]


def _modulation(cond, w, b):
    m = jax.nn.silu(cond) @ w + b
    return jnp.split(m, 3, axis=-1)


def _hyena_filter(L, w1, b1, w2, b2, w3, b3, freq, decay):
    pos = jnp.arange(L, dtype=jnp.float32)
    t = jnp.linspace(0.0, 1.0, L, dtype=jnp.float32)
    bands = jnp.linspace(1e-4, HY_BANDS - 1, HY_BANDS, dtype=jnp.float32)
    ang = (2.0 * math.pi / L) * pos[:, None] * bands[None, :]
    feat = jnp.concatenate([t[:, None], jnp.cos(ang), -jnp.sin(ang)], -1)
    hdn = jnp.sin(freq[0] * (feat @ w1 + b1))
    hdn = jnp.sin(freq[1] * (hdn @ w2 + b2))
    hf = (hdn @ w3 + b3).astype(jnp.float32).reshape(L, 2, HY_WIDTH)
    window = jnp.exp(-t[:, None, None] * jnp.abs(decay.astype(jnp.float32))[None])
    h = hf * window
    zero = jnp.zeros((1, HY_WIDTH), jnp.float32)
    return jnp.concatenate([h[:, 0], zero, h[:0:-1, 1]], 0)


def _hyena_mixer(h, w_in, conv_w, conv_b, w1, b1, w2, b2, w3, b3, freq, decay, skip, w_out):
    B, L, _ = h.shape
    u = h @ w_in
    u3, z = u[..., :3 * HY_WIDTH], u[..., 3 * HY_WIDTH:]
    prev, nxt = _neighbours(u3)
    u3 = prev * conv_w[0] + u3 * conv_w[1] + nxt * conv_w[2] + conv_b
    x0, x1, v = jnp.split(u3, 3, axis=-1)
    filt = _hyena_filter(L, w1, b1, w2, b2, w3, b3, freq, decay)
    v = v * x1
    vf = jnp.fft.rfft(v.astype(jnp.float32), n=2 * L, axis=1)
    kf = jnp.fft.rfft(filt, axis=0)
    y = jnp.fft.irfft(vf * kf[None], n=2 * L, axis=1)[:, :L].astype(h.dtype) + v * skip
    y = y * x0
    return (y * jax.nn.silu(z)) @ w_out


def _axial_rope_tables(rows):
    half = MLA_ROPE_DIM // 2
    inv = ROPE_BASE ** (-jnp.arange(0, half, 2, dtype=jnp.float32) / half)
    r = jnp.repeat(jnp.arange(rows, dtype=jnp.float32), GRID_W)
    col = jnp.tile(jnp.arange(GRID_W, dtype=jnp.float32), rows)
    ar, ac = r[:, None] * inv, col[:, None] * inv
    ang = jnp.concatenate([ar, ar, ac, ac], -1)
    return jnp.cos(ang), jnp.sin(ang)


def _rotate_half(x):
    a, b = jnp.split(x, 2, axis=-1)
    return jnp.concatenate([-b, a], -1)


def _apply_axial_rope(x, cos, sin):
    half = MLA_ROPE_DIM // 2
    rot = jnp.concatenate([_rotate_half(x[..., :half]), _rotate_half(x[..., half:])], -1)
    return (x.astype(jnp.float32) * cos + rot.astype(jnp.float32) * sin).astype(x.dtype)


def _mla_project(h, w_in, q_norm, kv_norm, w_q_up):
    B, L, _ = h.shape
    u = h @ w_in
    q_c, kv_c, k_pe, z = jnp.split(
        u, [MLA_Q_RANK, MLA_Q_RANK + MLA_KV_RANK, MLA_Q_RANK + MLA_KV_RANK + MLA_ROPE_DIM], axis=-1)
    q = (_rms_norm(q_c, q_norm) @ w_q_up).reshape(B, L, MLA_HEADS, MLA_NOPE_DIM + MLA_ROPE_DIM)
    ckv = _rms_norm(kv_c, kv_norm)
    return q[..., :MLA_NOPE_DIM], q[..., MLA_NOPE_DIM:], ckv, k_pe, z


def _mla_expand(ckv, w_kv_up):
    B, L, _ = ckv.shape
    kv = (ckv @ w_kv_up).reshape(B, L, MLA_HEADS, MLA_NOPE_DIM + MLA_V_DIM)
    return kv[..., :MLA_NOPE_DIM], kv[..., MLA_NOPE_DIM:]


def _mla_attend(q_nope, q_pe, k_nope, k_pe, v):
    B, Lq, H, _ = q_nope.shape
    nb = Lq // Q_BLOCK
    scale = (MLA_NOPE_DIM + MLA_ROPE_DIM) ** -0.5

    def blocks(a):
        return a.reshape(B, nb, Q_BLOCK, *a.shape[2:]).swapaxes(0, 1)

    def one(qb):
        qn, qp = qb
        s = jnp.einsum('bqhd,bkhd->bhqk', qn, k_nope) + jnp.einsum('bqhr,bkr->bhqk', qp, k_pe)
        p = jax.nn.softmax(s.astype(jnp.float32) * scale, axis=-1).astype(v.dtype)
        return jnp.einsum('bhqk,bkhd->bqhd', p, v)

    o = lax.map(one, (blocks(q_nope), blocks(q_pe)))
    return o.swapaxes(0, 1).reshape(B, Lq, H * MLA_V_DIM)


def _mla_context(h, w_in, q_norm, kv_norm, w_q_up, w_kv_up, w_out):
    q_nope, q_pe, ckv, k_pe, z = _mla_project(h, w_in, q_norm, kv_norm, w_q_up)
    k_nope, v = _mla_expand(ckv, w_kv_up)
    o = _mla_attend(q_nope, q_pe, k_nope, k_pe, v)
    return (o * jax.nn.silu(z)) @ w_out, ckv, k_pe


def _mla_latent(h, ctx_ckv, ctx_kpe, cos, sin, w_in, q_norm, kv_norm, w_q_up, w_kv_up, w_out):
    q_nope, q_pe, ckv, k_pe, z = _mla_project(h, w_in, q_norm, kv_norm, w_q_up)
    q_pe = _apply_axial_rope(q_pe, cos[:, None, :], sin[:, None, :])
    k_pe = _apply_axial_rope(k_pe, cos, sin)
    ckv_all = jnp.concatenate([ctx_ckv.astype(ckv.dtype), ckv], 1)
    kpe_all = jnp.concatenate([ctx_kpe.astype(k_pe.dtype), k_pe], 1)
    k_nope, v = _mla_expand(ckv_all, w_kv_up)
    o = _mla_attend(q_nope, q_pe, k_nope, kpe_all, v)
    return (o * jax.nn.silu(z)) @ w_out


def _rwkv_mixer(h, s0, mu, w_in, w0, w1, w2, a0, a1, a2, k_k, k_a, r_k, gn_g, gn_b, w_out):
    B, L, D = h.shape
    H, N = RW_HEADS, RW_HEAD_DIM
    f32 = jnp.float32
    prev, nxt = _neighbours(h)
    d = 0.5 * (prev + nxt) - h
    xs = h[None] + d[None] * mu[:, None, None, :]
    xr, xw, xk, xv, xa, xg = xs[0], xs[1], xs[2], xs[3], xs[4], xs[5]
    r, k, v, z = jnp.einsum('pbld,pde->pble', jnp.stack([xr, xk, xv, xg]), w_in)
    wl = w0[:, None, None, :] + jnp.einsum(
        'nblr,nrd->nbld', jnp.tanh(jnp.einsum('bld,ndr->nblr', xw, w1)), w2)
    decay = jnp.exp(-jnp.exp(-jax.nn.softplus(-wl.astype(f32)) - 0.5))
    a = jax.nn.sigmoid((a0[:, None, None, :] + jnp.einsum(
        'nblr,nrd->nbld', jnp.einsum('bld,ndr->nblr', xa, a1), a2)).astype(f32))

    def heads(t):
        return t.reshape(*t.shape[:-1], H, N)

    kf = k.astype(f32)
    kk = heads(kf * k_k)
    kk = kk * lax.rsqrt(jnp.sum(kk * kk, -1, keepdims=True) + 1e-12)
    kd = heads(kf[None] * (1.0 + (a - 1.0) * k_a))
    bd = kk[None] * heads(a)
    rh, vh = heads(r.astype(f32)), heads(v.astype(f32))

    def orient(t):
        return jnp.stack([t[0], t[1][:, ::-1]], 2).transpose(1, 0, 2, 3, 4)

    def both(t):
        return orient(jnp.broadcast_to(t, (2,) + t.shape))

    seq = (both(rh), orient(heads(decay)), orient(kd), both(vh), both(-kk), orient(bd))

    def step(S, inp):
        r_t, w_t, k_t, v_t, a_t, b_t = inp
        sa = jnp.einsum('bdhij,bdhj->bdhi', S, a_t)
        S = S * w_t[..., None, :] + sa[..., None] * b_t[..., None, :] + v_t[..., None] * k_t[..., None, :]
        return S, jnp.einsum('bdhij,bdhj->bdhi', S, r_t)

    s_fin, y = lax.scan(step, s0.astype(f32), seq)
    y = y.transpose(1, 0, 2, 3, 4)
    y = y[:, :, 0] + y[:, ::-1, 1]
    m = jnp.mean(y, -1, keepdims=True)
    var = jnp.mean(jnp.square(y - m), -1, keepdims=True)
    yn = (y - m) * lax.rsqrt(var + RW_GN_EPS) * gn_g.reshape(H, N) + gn_b.reshape(H, N)
    bonus = jnp.sum(jnp.sum(rh[None] * kd * r_k, -1, keepdims=True), 0) * vh
    o = (yn + bonus).reshape(B, L, D).astype(h.dtype)
    return (o * jax.nn.silu(z)) @ w_out, s_fin


def setup_inputs(seed: int = 0) -> dict:
    key = jax.random.key(seed)
    ks = iter(jax.random.split(key, 64))

    def nrm(shape, scale=1.0):
        return jax.random.normal(next(ks), shape, jnp.float32) * scale

    D = D_MODEL
    hqk = MLA_NOPE_DIM + MLA_ROPE_DIM
    mla_in = MLA_Q_RANK + MLA_KV_RANK + MLA_ROPE_DIM + MLA_HEADS * MLA_V_DIM
    hy_dec = jnp.linspace(HY_DECAY_MIN, HY_DECAY_MAX, HY_WIDTH, dtype=jnp.float32)
    return {
        'x_prompt': nrm((BATCH, SEQ, D)),
        'x_sample': nrm((DEC_BATCH, DEC_SEQ, D)),
        'cache_mla_ckv': nrm((DEC_BATCH, N_MLA, PAST_LEN, MLA_KV_RANK)),
        'cache_mla_kpe': nrm((DEC_BATCH, N_MLA, PAST_LEN, MLA_ROPE_DIM)),
        'state_rwkv': nrm((DEC_BATCH, N_RWKV, 2, RW_HEADS, RW_HEAD_DIM, RW_HEAD_DIM), 0.5),
        'c': nrm((DEC_BATCH, D)),
        'c_ctx': nrm((D,)),
        'mod_w': nrm((DEPTH, D, 3 * D), 0.5 * D ** -0.5),
        'mod_b': nrm((DEPTH, 3 * D), 0.02),
        'ln_g': 1.0 + nrm((DEPTH, D), 0.02),
        'ln_b': nrm((DEPTH, D), 0.02),
        'hy_w_in': nrm((N_HYENA, D, 4 * HY_WIDTH), D ** -0.5),
        'hy_conv_w': nrm((N_HYENA, 3, 3 * HY_WIDTH), 3 ** -0.5),
        'hy_conv_b': nrm((N_HYENA, 3 * HY_WIDTH), 0.02),
        'hy_ffn_w1': nrm((N_HYENA, HY_EMB, HY_FFN), HY_EMB ** -0.5),
        'hy_ffn_b1': nrm((N_HYENA, HY_FFN), 0.02),
        'hy_ffn_w2': nrm((N_HYENA, HY_FFN, HY_FFN), HY_FFN ** -0.5),
        'hy_ffn_b2': nrm((N_HYENA, HY_FFN), 0.02),
        'hy_ffn_w3': nrm((N_HYENA, HY_FFN, 2 * HY_WIDTH), HY_FFN ** -0.5),
        'hy_ffn_b3': nrm((N_HYENA, 2 * HY_WIDTH), 0.02),
        'hy_freq': 1.0 + nrm((N_HYENA, 2, HY_FFN), 0.1),
        'hy_decay': hy_dec[None, None, :] * (1.0 + nrm((N_HYENA, 2, HY_WIDTH), 0.05)),
        'hy_skip': nrm((N_HYENA, HY_WIDTH)),
        'hy_w_out': nrm((N_HYENA, HY_WIDTH, D), HY_WIDTH ** -0.5 * DEEPNORM_BETA),
        'mla_w_in': nrm((N_MLA, D, mla_in), D ** -0.5),
        'mla_q_norm': 1.0 + nrm((N_MLA, MLA_Q_RANK), 0.02),
        'mla_kv_norm': 1.0 + nrm((N_MLA, MLA_KV_RANK), 0.02),
        'mla_w_q_up': nrm((N_MLA, MLA_Q_RANK, MLA_HEADS * hqk), MLA_Q_RANK ** -0.5),
        'mla_w_kv_up': nrm((N_MLA, MLA_KV_RANK, MLA_HEADS * (MLA_NOPE_DIM + MLA_V_DIM)), MLA_KV_RANK ** -0.5),
        'mla_w_out': nrm((N_MLA, MLA_HEADS * MLA_V_DIM, D), (MLA_HEADS * MLA_V_DIM) ** -0.5 * DEEPNORM_BETA),
        'rw_mu': jax.random.uniform(next(ks), (N_RWKV, 6, D), jnp.float32),
        'rw_w_in': nrm((N_RWKV, 4, D, D), D ** -0.5),
        'rw_w0': jnp.linspace(-6.0, 1.0, D, dtype=jnp.float32)[None, None, :] + nrm((N_RWKV, 2, D), 0.1),
        'rw_w1': nrm((N_RWKV, 2, D, RW_DECAY_LORA), D ** -0.5),
        'rw_w2': nrm((N_RWKV, 2, RW_DECAY_LORA, D), 0.1 * RW_DECAY_LORA ** -0.5),
        'rw_a0': nrm((N_RWKV, 2, D), 0.1),
        'rw_a1': nrm((N_RWKV, 2, D, RW_A_LORA), D ** -0.5),
        'rw_a2': nrm((N_RWKV, 2, RW_A_LORA, D), 0.1 * RW_A_LORA ** -0.5),
        'rw_k_k': 0.85 + nrm((N_RWKV, D), 0.05),
        'rw_k_a': 1.0 + nrm((N_RWKV, D), 0.05),
        'rw_r_k': nrm((N_RWKV, RW_HEADS, RW_HEAD_DIM), 0.1),
        'rw_gn_g': 1.0 + nrm((N_RWKV, D), 0.02),
        'rw_gn_b': nrm((N_RWKV, D), 0.02),
        'rw_w_out': nrm((N_RWKV, D, D), D ** -0.5 * DEEPNORM_BETA),
    }


def reference(x_prompt, x_sample, cache_mla_ckv, cache_mla_kpe, state_rwkv, c, c_ctx,
              mod_w, mod_b, ln_g, ln_b,
              hy_w_in, hy_conv_w, hy_conv_b, hy_ffn_w1, hy_ffn_b1, hy_ffn_w2, hy_ffn_b2,
              hy_ffn_w3, hy_ffn_b3, hy_freq, hy_decay, hy_skip, hy_w_out,
              mla_w_in, mla_q_norm, mla_kv_norm, mla_w_q_up, mla_w_kv_up, mla_w_out,
              rw_mu, rw_w_in, rw_w0, rw_w1, rw_w2, rw_a0, rw_a1, rw_a2, rw_k_k, rw_k_a,
              rw_r_k, rw_gn_g, rw_gn_b, rw_w_out):
    rows = x_sample.shape[1] // GRID_W
    cos, sin = _axial_rope_tables(rows)
    xp, xs = x_prompt, x_sample
    new_ckv, new_kpe, new_rw = [], [], []
    for i in range(DEPTH):
        kind, j = i % N_MIXERS, i // N_MIXERS
        sh_p, sc_p, g_p = _modulation(c_ctx[None, None, :], mod_w[i], mod_b[i])
        sh_s, sc_s, g_s = _modulation(c[:, None, :], mod_w[i], mod_b[i])
        hp = xp * (1.0 + sc_p) + sh_p
        hs = xs * (1.0 + sc_s) + sh_s
        if kind == 0:
            hyp = (hy_w_in[j], hy_conv_w[j], hy_conv_b[j], hy_ffn_w1[j], hy_ffn_b1[j],
                   hy_ffn_w2[j], hy_ffn_b2[j], hy_ffn_w3[j], hy_ffn_b3[j], hy_freq[j],
                   hy_decay[j], hy_skip[j], hy_w_out[j])
            op = _hyena_mixer(hp, *hyp)
            os_ = _hyena_mixer(hs, *hyp)
        elif kind == 1:
            mp = (mla_w_in[j], mla_q_norm[j], mla_kv_norm[j], mla_w_q_up[j], mla_w_kv_up[j], mla_w_out[j])
            op, ckv, kpe = _mla_context(hp, *mp)
            os_ = _mla_latent(hs, cache_mla_ckv[:, j], cache_mla_kpe[:, j], cos, sin, *mp)
            new_ckv.append(ckv)
            new_kpe.append(kpe)
        else:
            rp = (rw_mu[j], rw_w_in[j], rw_w0[j], rw_w1[j], rw_w2[j], rw_a0[j], rw_a1[j], rw_a2[j],
                  rw_k_k[j], rw_k_a[j], rw_r_k[j], rw_gn_g[j], rw_gn_b[j], rw_w_out[j])
            s_zero = jnp.zeros((xp.shape[0], 2, RW_HEADS, RW_HEAD_DIM, RW_HEAD_DIM), jnp.float32)
            op, st = _rwkv_mixer(hp, s_zero, *rp)
            os_, _ = _rwkv_mixer(hs, state_rwkv[:, j], *rp)
            new_rw.append(st)
        xp = _layer_norm(DEEPNORM_ALPHA * xp + g_p * op, ln_g[i], ln_b[i])
        xs = _layer_norm(DEEPNORM_ALPHA * xs + g_s * os_, ln_g[i], ln_b[i])
    return (xp, xs, jnp.stack(new_ckv, 1), jnp.stack(new_kpe, 1), jnp.stack(new_rw, 1))
```

```python
import numpy as np
import ml_dtypes
from contextlib import ExitStack
import concourse.bass as bass
import concourse.mybir as mybir
from concourse.bass_utils import run_bass_kernel_spmd

F32 = mybir.dt.float32
BF16 = mybir.dt.bfloat16
I32 = mybir.dt.int32
F32R = mybir.dt.float32r
AF = mybir.ActivationFunctionType
ALU = mybir.AluOpType
NPBF = ml_dtypes.bfloat16

D = 1024
KC = 8
SEQ = 256
TP = 512
TS = 2048
DEPTH = 4
ALPHA = (2.0 * DEPTH) ** 0.25
LN_EPS = 1e-5
RMS_EPS = 1e-6
TWO_PI = float(2.0 * np.pi)
SEM_ROT = 3000


class Tile:
    __slots__ = ("h", "name", "w", "r", "dsem", "dval", "excl", "lastrg")

    def __init__(self, h, name=""):
        self.h = h
        self.name = name
        self.excl = False
        self.lastrg = None
        self.w = []
        self.r = []
        self.dsem = None
        self.dval = 0

    def __getitem__(self, idx):
        return self.h[idx]


class Eng:
    def __init__(self, name):
        self.name = name
        self.ops = []
        self.n = 0
        self.seen = {}
        self.maxblk = -1


class KB:
    NPOOL = 110

    def __init__(self, nc):
        self.nc = nc
        self.tiles = []
        self.blockno = 0
        self.E = {n: Eng(n) for n in ("pe", "act", "dve", "pool", "sp")}
        self.dma_latest = {}
        self.pool_vals = [0] * self.NPOOL
        self.pool_next = 0
        self.ncoll = 0
        self.semh = {}
        self.semstack = None

    def track(self, t):
        self.tiles.append(t)
        return t

    def _engsem(self, e, n):
        return ("E", e.name, n // SEM_ROT), (n % SEM_ROT) + 1

    def _deps(self, e, reads, writes, acc, skip_self=False):
        deps = {}

        def add(tok):
            s, v = tok
            if skip_self and s[0] == "E" and s[1] == e.name:
                return
            if s[0] == "D":
                v = max(v, self.dma_latest.get(s, v))
            if deps.get(s, 0) < v:
                deps[s] = v
        for t in reads:
            for tok in t.w:
                add(tok)
            if t.excl:
                for tok in t.r:
                    if not (tok[0][0] == "E" and tok[0][1] == e.name):
                        add(tok)
        if not acc:
            for t in writes:
                for tok in t.w:
                    add(tok)
                for tok in t.r:
                    add(tok)
        waits = []
        for s, v in deps.items():
            if e.seen.get(s, 0) >= v:
                continue
            if s[0] == "E":
                mb = e.seen.get(("MB", s[1]), -1)
                if mb > s[2]:
                    continue
                if s[2] > mb:
                    e.seen[("MB", s[1])] = s[2]
            e.seen[s] = v
            waits.append((s, v))
        return waits

    def _commit(self, tok, reads, writes, acc):
        for t in reads:
            t.r.append(tok)
        for t in writes:
            if acc:
                t.w.append(tok)
            else:
                t.w = [tok]
                t.r = []

    def op(self, eng, emit, reads=(), writes=(), acc=False, skip_self=False):
        e = self.E[eng]
        waits = self._deps(e, reads, writes, acc, skip_self)
        tok = self._engsem(e, e.n)
        e.n += 1
        e.ops.append((waits, emit, tok, 1))
        self._commit(tok, reads, writes, acc)

    def dma(self, eng, emit, reads=(), writes=()):
        e = self.E[eng]
        waits = self._deps(e, reads, writes, False)
        st = writes[0] if writes else reads[0]
        if st.dsem is None:
            assert self.pool_next < self.NPOOL, "out of DMA semaphores in this block"
            idx = self.pool_next
            self.pool_next += 1
            st.dsem = ("D", idx)
            st.dval = self.pool_vals[idx]
        st.dval += 16
        self.pool_vals[st.dsem[1]] = st.dval
        tok = (st.dsem, st.dval)
        self.dma_latest[st.dsem] = st.dval
        e.ops.append((waits, emit, tok, 16))
        self._commit(tok, reads, writes, False)

    def special(self, eng, emit, reads=(), writes=()):
        e = self.E[eng]
        waits = self._deps(e, reads, writes, False)
        tok = (("C", self.ncoll), 1)
        self.ncoll += 1
        e.ops.append((waits, emit, tok, 1))
        self._commit(tok, reads, writes, False)

    def end_block(self):
        nc = self.nc
        sp = self.E["sp"]
        waits = []
        for s, v in self.dma_latest.items():
            if sp.seen.get(s, 0) < v:
                sp.seen[s] = v
                waits.append((s, v))
        for t in self.tiles:
            for tok in t.w + t.r:
                if tok[0][0] == "C" and sp.seen.get(tok[0], 0) < 1:
                    sp.seen[tok[0]] = 1
                    waits.append(tok)
        sp.ops.append((waits, None, None, 0))
        keys = []
        for e in self.E.values():
            for waits_, emit, tok, inc in e.ops:
                for s, v in waits_:
                    if s not in keys:
                        keys.append(s)
                if tok is not None and tok[0] not in keys:
                    keys.append(tok[0])
        semh = self.semh
        for kk in keys:
            if kk not in semh:
                semh[kk] = self.semstack.enter_context(nc.semaphore("sem%d" % len(semh)))
        with nc.Block() as block:
            def run(e):
                def body(h):
                    for waits_, emit, tok, inc in e.ops:
                        for s, v in waits_:
                            h.wait_ge(semh[s], v)
                        if emit is None:
                            continue
                        ins = emit(h)
                        if tok is not None:
                            ins.then_inc(semh[tok[0]], inc)
                return body
            block.tensor(run(self.E["pe"]))
            block.scalar(run(self.E["act"]))
            block.vector(run(self.E["dve"]))
            block.gpsimd(run(self.E["pool"]))
            block.sync(run(self.E["sp"]))
        nops = {n: len(e.ops) for n, e in self.E.items()}
        for t in self.tiles:
            t.w = []
            t.r = []
            t.dsem = None
            t.dval = 0
        for e in self.E.values():
            e.ops = []
        self.pool_next = 0
        self.blockno += 1
        return nops, len(semh)


def _dft_tables(L):
    n2 = 2 * L
    l = np.arange(L, dtype=np.float64)
    ang = 2.0 * np.pi * np.outer(l, l) / n2
    C = np.cos(ang)
    S = np.sin(ang)
    wf = np.full(L, 2.0 / n2)
    wf[0] = 1.0 / n2
    Gc = wf[:, None] * np.cos(ang)
    Gs = (2.0 / n2) * np.sin(ang)
    nyq = (-1.0) ** l
    return C, S, Gc, Gs, nyq, nyq / n2


def _feat_tables(L):
    pos = np.arange(L, dtype=np.float32)
    t = np.linspace(0.0, 1.0, L, dtype=np.float32)
    bands = np.linspace(1e-4, 16 - 1, 16, dtype=np.float32)
    ang = (np.float32(2.0 * np.pi / L) * pos[:, None] * bands[None, :]).astype(np.float32)
    feat = np.concatenate([t[:, None], np.cos(ang), -np.sin(ang)], -1).astype(np.float32)
    return np.ascontiguousarray(feat.T), t


_CONST = {}


def _constants():
    if _CONST:
        return _CONST
    c = {}
    C, S, Gc, Gs, nyq, nyqrow = _dft_tables(TS)
    fw = np.stack([C, S], 0)
    fw = fw.reshape(2, 16, 128, 16, 128)
    c["fwd_s"] = np.ascontiguousarray(fw.transpose(3, 2, 1, 0, 4)).astype(NPBF)
    iv = np.stack([Gc, Gs], 0).reshape(2, 16, 128, 4, 512)
    c["inv_s"] = np.ascontiguousarray(iv.transpose(3, 1, 2, 0, 4)).astype(NPBF)
    c["nyq_s"] = np.ascontiguousarray(nyq.reshape(16, 128).T).astype(NPBF)
    c["nyqrow_s"] = nyqrow.reshape(1, TS).astype(NPBF)
    ft, t = _feat_tables(TS)
    c["feat_s"] = ft
    c["negt_s"] = np.ascontiguousarray((-t).reshape(16, 128).T).astype(np.float32)
    C, S, Gc, Gs, nyq, nyqrow = _dft_tables(SEQ)
    fw = np.stack([C, S], 0).reshape(2, 2, 128, 256)
    c["fwd_p"] = np.ascontiguousarray(fw.transpose(2, 1, 0, 3)).astype(NPBF)
    iv = np.stack([Gc, Gs], 0).reshape(2, 2, 128, 256)
    c["inv_p"] = np.ascontiguousarray(iv.transpose(2, 1, 0, 3)).astype(NPBF)
    c["nyq_p"] = np.ascontiguousarray(nyq.reshape(2, 128).T).astype(NPBF)
    c["nyqrow_p"] = nyqrow.reshape(1, SEQ).astype(NPBF)
    ft, t = _feat_tables(SEQ)
    c["feat_p"] = ft
    c["negt_p"] = np.ascontiguousarray((-t).reshape(2, 128).T).astype(np.float32)
    c["ident"] = np.eye(128, dtype=np.float32)
    _CONST.update(c)
    return _CONST


def _fm(v):
    v = np.asarray(v, np.float32)
    lead = v.shape[:-1]
    n = v.shape[-1] // 128
    return np.ascontiguousarray(np.moveaxis(v.reshape(*lead, n, 128), -1, 0))


def _own_cols(q, nblk, width=1024):
    idx = []
    for jc in range(2):
        for w in range(nblk):
            s = w * width + q * 256 + jc * 128
            idx.extend(range(s, s + 128))
    return np.array(idx)


def _perm_cols(nblk, width=1024):
    idx = []
    for jc in range(8):
        for w in range(nblk):
            s = w * width + jc * 128
            idx.extend(range(s, s + 128))
    return np.array(idx)


def prep_inputs(inp):
    cst = _constants()
    f = lambda a: np.ascontiguousarray(np.asarray(a, np.float32))
    shared = {}
    shared["mod_w"] = f(inp["mod_w"])
    shared["mod_bT"] = _fm(inp["mod_b"])
    shared["ln_gT"] = _fm(inp["ln_g"])
    shared["ln_bT"] = _fm(inp["ln_b"])
    pc = _perm_cols(4)
    shared["hy_win"] = f(np.asarray(inp["hy_w_in"])[:, :, pc])
    cw = np.asarray(inp["hy_conv_w"], np.float32)
    cb = np.asarray(inp["hy_conv_b"], np.float32)
    cv = np.concatenate([cw, cb[:, None, :]], 1)
    cvT = cv.reshape(2, 4, 3, 8, 128).transpose(4, 0, 3, 2, 1)
    shared["hy_cvT"] = f(cvT)
    shared["hy_skipT"] = _fm(inp["hy_skip"])
    shared["hy_w1"] = f(inp["hy_ffn_w1"])
    shared["hy_b12"] = f(np.stack([np.asarray(inp["hy_ffn_b1"]), np.asarray(inp["hy_ffn_b2"])], -1))
    shared["hy_w2"] = f(inp["hy_ffn_w2"])
    w3a = np.concatenate([np.asarray(inp["hy_ffn_w3"]), np.asarray(inp["hy_ffn_b3"])[:, None, :]], 1)
    shared["hy_w3a"] = f(w3a)
    shared["hy_freqT"] = f(np.asarray(inp["hy_freq"]).transpose(0, 2, 1))
    shared["hy_decay"] = f(np.asarray(inp["hy_decay"]).reshape(2, 1, 2048))
    shared["hy_wout"] = f(inp["hy_w_out"])
    for k_ in ("fwd_s", "inv_s", "nyq_s", "nyqrow_s", "feat_s", "negt_s", "fwd_p", "inv_p", "nyq_p",
               "nyqrow_p", "feat_p", "negt_p", "ident"):
        shared[k_] = cst[k_]
    wi = np.asarray(inp["mla_w_in"], np.float32)[0]
    shared["mla_wh"] = f(wi[:, 0:416])
    shared["mla_wz"] = f(wi[:, 416:1440])
    shared["mla_nrm"] = f(np.concatenate([_fm(np.asarray(inp["mla_q_norm"])[0]), _fm(np.asarray(inp["mla_kv_norm"])[0])], 1))
    wq = np.asarray(inp["mla_w_q_up"], np.float32)[0]
    shared["mla_wq"] = f(wq)
    wkv = np.asarray(inp["mla_w_kv_up"], np.float32)[0].reshape(128, 16, 2, 64)
    shared["mla_wk"] = f(wkv[:, :, 0, :].reshape(128, 1024))
    shared["mla_wv"] = f(wkv[:, :, 1, :].reshape(128, 1024))
    shared["mla_wout"] = f(inp["mla_w_out"])
    half = 16
    inv = (10000.0 ** (-np.arange(0, half, 2, dtype=np.float32) / half)).astype(np.float32)
    rr = np.repeat(np.arange(32, dtype=np.float32), 64)
    col = np.tile(np.arange(64, dtype=np.float32), 32)
    ar, ac = rr[:, None] * inv, col[:, None] * inv
    ang = np.concatenate([ar, ar, ac, ac], -1).astype(np.float32)
    cosT, sinT = np.cos(ang).T.astype(np.float32), np.sin(ang).T.astype(np.float32)
    shared["rope_cs32"] = f(np.stack([cosT, sinT], 1))
    cs96 = np.zeros((96, 2, TS), np.float32)
    cs96[0:64, 0] = 1.0
    cs96[64:96, 0] = cosT
    cs96[64:96, 1] = sinT
    shared["rope_cs96"] = cs96
    R = np.zeros((32, 32), np.float32)
    for i_ in range(8):
        R[8 + i_, i_] = -1.0
        R[i_, 8 + i_] = 1.0
        R[24 + i_, 16 + i_] = -1.0
        R[16 + i_, 24 + i_] = 1.0
    shared["rope_R32"] = R
    R96 = np.zeros((96, 96), np.float32)
    R96[64:96, 64:96] = R
    shared["rope_R96"] = R96
    shared["rw_muT"] = _fm(np.asarray(inp["rw_mu"])[0])
    rwin = np.asarray(inp["rw_w_in"], np.float32)[0]
    rwin_cat = np.concatenate([rwin[0], rwin[1], rwin[2], rwin[3]], 1)
    shared["rw_win"] = f(rwin_cat[:, pc])
    shared["rw_w1"] = f(np.asarray(inp["rw_w1"])[0])
    shared["rw_a1"] = f(np.asarray(inp["rw_a1"])[0])
    shared["rw_w2"] = f(np.asarray(inp["rw_w2"])[0])
    shared["rw_a2"] = f(np.asarray(inp["rw_a2"])[0])
    vecs = np.stack([np.asarray(inp["rw_w0"])[0, 0], np.asarray(inp["rw_w0"])[0, 1],
                     np.asarray(inp["rw_a0"])[0, 0], np.asarray(inp["rw_a0"])[0, 1],
                     np.asarray(inp["rw_k_k"])[0], np.asarray(inp["rw_k_a"])[0],
                     np.asarray(inp["rw_r_k"])[0].reshape(-1), np.asarray(inp["rw_gn_g"])[0],
                     np.asarray(inp["rw_gn_b"])[0]], 0).astype(np.float32)
    vecT = np.ascontiguousarray(vecs.reshape(9, 8, 128).transpose(2, 1, 0))
    shared["rw_vec"] = vecT
    shared["rw_wout"] = f(inp["rw_w_out"])
    tau = np.arange(64)
    m1 = np.zeros((2, 64, 192), np.float32)
    m2 = np.zeros((2, 64, 192), np.float32)
    mf = np.ones((2, 128, 128), np.float32)
    for dr in range(2):
        if dr == 0:
            ms = (tau[:, None] < tau[None, :]).astype(np.float32)
            mi = (tau[:, None] <= tau[None, :]).astype(np.float32)
        else:
            ms = (tau[:, None] > tau[None, :]).astype(np.float32)
            mi = (tau[:, None] >= tau[None, :]).astype(np.float32)
        m1[dr, :, 0:64] = 1.0
        m1[dr, :, 64:128] = mi
        m1[dr, :, 128:192] = ms
        m2[dr, :, 0:64] = 1.0
        m2[dr, :, 64:128] = ms.T
        m2[dr, :, 128:192] = ms.T
        mf[dr, 64:128, 64:128] = mi
    shared["rw_mask1"] = m1
    shared["rw_mask2"] = m2
    shared["rw_maskf"] = mf
    rst = np.ones((128, 512), np.float32)
    rst[:, 0::64] = 0.0
    shared["rw_reset"] = rst
    shared["rw_ii2"] = np.concatenate([np.eye(64, dtype=np.float32)] * 2, 0)
    bd = np.zeros((128, 128), np.float32)
    bd[0:64, 0:64] = 1.0
    bd[64:128, 64:128] = 1.0
    shared["rw_bd"] = bd
    maps = []
    xp = np.asarray(inp["x_prompt"], np.float32)
    xs = np.asarray(inp["x_sample"], np.float32)
    cc = np.asarray(inp["c"], np.float32)
    cctx = np.asarray(inp["c_ctx"], np.float32)
    for c in range(8):
        b, q = c // 4, c % 4
        m = dict(shared)
        m["xp"] = f(xp[2 * c:2 * c + 2].reshape(TP, D).T)
        m["xs"] = f(xs[b].T)
        m["condT"] = f(np.stack([cctx, cc[b]], 0).reshape(2, 8, 128).transpose(2, 1, 0))
        oc = _own_cols(q, 4)
        m["hy_win_own"] = f(np.asarray(inp["hy_w_in"])[:, :, oc])
        cvo = cv.reshape(2, 4, 3, 8, 128)[:, :, :, 2 * q:2 * q + 2, :].transpose(4, 0, 3, 2, 1)
        m["hy_cv_own"] = f(cvo)
        m["hy_skip_own"] = f(_fm(inp["hy_skip"])[:, :, 2 * q:2 * q + 2])
        oc2 = np.concatenate([np.arange(256 * q, 256 * q + 256), 1024 + np.arange(256 * q, 256 * q + 256)])
        m["hy_w3a_own"] = f(w3a[:, :, oc2])
        m["hy_decay_own"] = f(np.asarray(inp["hy_decay"]).reshape(2, 2048)[:, oc2].reshape(2, 1, 512))
        m["mla_wz_own"] = f(wi[:, 416 + 256 * q:416 + 256 * q + 256])
        m["mla_wq_own"] = f(wq[:, 384 * q:384 * q + 384])
        m["mla_wk_own"] = f(wkv[:, 4 * q:4 * q + 4, 0, :].reshape(128, 256))
        m["mla_wv_own"] = f(wkv[:, 4 * q:4 * q + 4, 1, :].reshape(128, 256))
        oc4 = _own_cols(q, 4)
        m["rw_win_own"] = f(rwin_cat[:, oc4])
        m["rw_w2_own"] = f(np.asarray(inp["rw_w2"])[0][:, :, 256 * q:256 * q + 256])
        m["rw_a2_own"] = f(np.asarray(inp["rw_a2"])[0][:, :, 256 * q:256 * q + 256])
        m["rw_vec_own"] = f(vecT[:, 2 * q:2 * q + 2, :])
        stt_ = np.asarray(inp["state_rwkv"], np.float32)[b, 0][:, 4 * q:4 * q + 4]
        m["rw_state"] = f(stt_.transpose(0, 1, 3, 2))
        m["cache_ckvT"] = f(np.asarray(inp["cache_mla_ckv"], np.float32)[b, 0].T)
        m["cache_kpeT"] = f(np.asarray(inp["cache_mla_kpe"], np.float32)[b, 0].T)
        maps.append(m)
    return maps


class Scope(ExitStack):
    def __init__(self, prog):
        super().__init__()
        self.prog = prog
        self.tiles = []

    def __exit__(self, *a):
        pend = self.prog.pending
        for t in self.tiles:
            for tok in t.w + t.r:
                if tok not in pend:
                    pend.append(tok)
            self.prog.k.tiles.remove(t)
        return super().__exit__(*a)


class Prog:
    def __init__(self, nl=DEPTH, debug=False):
        self.pending = []
        self.nl = nl
        self.debug = debug
        self.nc = bass.Bass("TRN2", target_bir_lowering=False)
        self.k = KB(self.nc)
        self.din = {}
        self.dout = {}
        self.stats = []

    def inp(self, name, shape, dt=F32):
        t = self.k.track(Tile(self.nc.dram_tensor(name, list(shape), dt, kind="ExternalInput"), name))
        self.din[name] = t
        return t

    def outp(self, name, shape, dt=F32):
        t = self.k.track(Tile(self.nc.dram_tensor(name, list(shape), dt, kind="ExternalOutput"), name))
        self.dout[name] = t
        return t

    def scratch(self, name, shape, dt):
        return self.k.track(Tile(self.nc.dram_tensor(name, list(shape), dt), name))

    def sb(self, st, name, shape, dt):
        self.uid = getattr(self, "uid", 0) + 1
        name = "%s_%d" % (name, self.uid)
        t = self.k.track(Tile(st.enter_context(self.nc.sbuf_tensor(name, list(shape), dt)), name))
        t.w = list(self.pending)
        if hasattr(st, "tiles"):
            st.tiles.append(t)
        return t

    def scope(self):
        return Scope(self)

    def P(self, bank, out, lhsT, rhs, start, stop, reads, serial=False, fast=False):
        b0 = lhsT.base_partition()
        rg = (b0, b0 + lhsT.partition_size())
        last = bank.lastrg
        disjoint = last is not None and (rg[1] <= last[0] or last[1] <= rg[0])
        bank.lastrg = rg
        serial = serial or disjoint
        self.k.op("pe", lambda e: e.matmul(out, lhsT=lhsT, rhs=rhs, start=start, stop=stop),
                  reads=reads, writes=[bank], acc=(not start) and (not serial), skip_self=not serial)

    def A(self, writes, reads, fn):
        self.k.op("act", fn, reads=reads, writes=writes)

    def V(self, writes, reads, fn):
        self.k.op("dve", fn, reads=reads, writes=writes)

    def G(self, writes, reads, fn):
        self.k.op("pool", fn, reads=reads, writes=writes)

    def dsp(self, writes, reads, out, in_):
        self.k.dma("sp", lambda e: e.dma_start(out=out, in_=in_), reads=reads, writes=writes)

    def dpool(self, writes, reads, out, in_):
        self.k.dma("pool", lambda e: e.dma_start(out=out, in_=in_), reads=reads, writes=writes)

    def act(self, wt, out, rt, in_, func, scale=None, bias=None, extra_reads=()):
        kw = {}
        if scale is not None:
            kw["scale"] = scale
        if bias is not None:
            kw["bias"] = bias
        self.A([wt], [rt] + list(extra_reads), lambda e: e.activation(out=out, in_=in_, func=func, **kw))

    def tt(self, wt, out, r0, in0, r1, in1, op):
        self.V([wt], [r0, r1], lambda e: e.tensor_tensor(out=out, in0=in0, in1=in1, op=op))

    def stt(self, wt, out, r0, in0, scalar, r1, in1, op0, op1, extra_reads=()):
        self.V([wt], [r0, r1] + list(extra_reads),
               lambda e: e.scalar_tensor_tensor(out=out, in0=in0, scalar=scalar, in1=in1, op0=op0, op1=op1))

    def ts(self, wt, out, r0, in0, s1, s2, op0, op1=None, extra_reads=()):
        if op1 is None:
            self.V([wt], [r0] + list(extra_reads),
                   lambda e: e.tensor_scalar(out=out, in0=in0, scalar1=s1, scalar2=None, op0=op0))
        else:
            self.V([wt], [r0] + list(extra_reads),
                   lambda e: e.tensor_scalar(out=out, in0=in0, scalar1=s1, scalar2=s2, op0=op0, op1=op1))

    def memset(self, wt, ap, val):
        self.V([wt], [wt], lambda e: e.memset(ap, val))

    def recip(self, wt, out, rt, in_):
        self.V([wt], [rt], lambda e: e.reciprocal(out=out, in_=in_))

    def cp(self, wt, out, rt, in_):
        self.V([wt], [rt], lambda e: e.tensor_copy(out=out, in_=in_))

    def ps(self, hold=False):
        while True:
            b = self.PSB[self.psi % 8]
            self.psi += 1
            if b not in self.held:
                break
        if hold:
            self.held.append(b)
        return b

    def release(self, b):
        self.held.remove(b)

    def end_block(self, label):
        self.pending = []
        nops, nk = self.k.end_block()
        self.stats.append((label, nops, nk))

    def build(self):
        nc = self.nc
        with ExitStack() as top:
            self.top = top
            self.k.semstack = top
            PS = top.enter_context(nc.psum_tensor("PS", [128, 8, 512], F32))
            self.PSB = [self.k.track(Tile(PS[:, i, :], "ps%d" % i)) for i in range(8)]
            for b_ in self.PSB:
                b_.excl = True
            self.psi = 0
            self.held = []
            self.declare_io()
            self.XP = self.sb(top, "XP", [128, KC, TP], F32)
            self.XS = self.sb(top, "XS", [128, KC, TS], F32)
            self.MOD = self.sb(top, "MOD", [128, DEPTH, 24, 2], F32)
            self.LNG = self.sb(top, "LNG", [128, DEPTH, 8], F32)
            self.LNB = self.sb(top, "LNB", [128, DEPTH, 8], F32)
            self.IDF = self.sb(top, "IDF", [128, 128], F32)
            self.IDB = self.sb(top, "IDB", [128, 128], BF16)
            self.ONESF = self.sb(top, "ONESF", [128, 128], F32)
            self.EPS = self.sb(top, "EPS", [128, 4], F32)
            self.prologue()
            import os
            for i in range(self.nl):
                kind, j = i % 3, i // 3
                if os.environ.get("RW_ONLY") and kind != 2:
                    continue
                if kind == 0:
                    self.hyena_layer(i, j)
                elif kind == 1:
                    self.mla_layer(i, j)
                else:
                    self.rwkv_layer(i, j)
            self.epilogue()
        return nc

    def declare_io(self):
        I = self.inp
        I("xp", [D, TP]); I("xs", [D, TS]); I("condT", [128, 8, 2])
        I("mod_w", [4, D, 3 * D]); I("mod_bT", [128, 4, 24]); I("ln_gT", [128, 4, 8]); I("ln_bT", [128, 4, 8])
        I("hy_win", [2, D, 4096]); I("hy_win_own", [2, D, 1024])
        I("hy_cvT", [128, 2, 8, 3, 4]); I("hy_cv_own", [128, 2, 2, 3, 4])
        I("hy_skipT", [128, 2, 8]); I("hy_skip_own", [128, 2, 2])
        I("hy_w1", [2, 33, 64]); I("hy_b12", [2, 64, 2]); I("hy_w2", [2, 64, 64])
        I("hy_w3a", [2, 65, 2048]); I("hy_w3a_own", [2, 65, 512]); I("hy_freqT", [2, 64, 2])
        I("hy_decay", [2, 1, 2048]); I("hy_decay_own", [2, 1, 512]); I("hy_wout", [2, D, D])
        I("fwd_s", [16, 128, 16, 2, 128], BF16); I("inv_s", [4, 16, 128, 2, 512], BF16)
        I("nyq_s", [128, 16], BF16); I("nyqrow_s", [1, TS], BF16); I("feat_s", [33, TS]); I("negt_s", [128, 16])
        I("fwd_p", [128, 2, 2, 256], BF16); I("inv_p", [128, 2, 2, 256], BF16)
        I("nyq_p", [128, 2], BF16); I("nyqrow_p", [1, SEQ], BF16); I("feat_p", [33, SEQ]); I("negt_p", [128, 2])
        I("ident", [128, 128])
        I("mla_wh", [D, 416]); I("mla_wz", [D, D]); I("mla_wz_own", [D, 256]); I("mla_nrm", [128, 3])
        I("mla_wq", [256, 1536]); I("mla_wq_own", [256, 384]); I("mla_wk", [128, D]); I("mla_wv", [128, D])
        I("mla_wk_own", [128, 256]); I("mla_wv_own", [128, 256]); I("mla_wout", [1, D, D])
        I("rope_cs32", [32, 2, TS]); I("rope_cs96", [96, 2, TS]); I("rope_R32", [32, 32]); I("rope_R96", [96, 96])
        I("cache_ckvT", [128, 512]); I("cache_kpeT", [32, 512])
        self.outp("ckv_out", [128, TP]); self.outp("kpe_out", [32, TP])
        I("rw_muT", [128, 6, 8]); I("rw_win", [D, 4096]); I("rw_win_own", [D, 1024])
        I("rw_w1", [2, D, 64]); I("rw_a1", [2, D, 64]); I("rw_w2", [2, 64, D]); I("rw_a2", [2, 64, D])
        I("rw_w2_own", [2, 64, 256]); I("rw_a2_own", [2, 64, 256]); I("rw_vec", [128, 8, 9]); I("rw_vec_own", [128, 2, 9])
        I("rw_wout", [1, D, D]); I("rw_state", [2, 4, 64, 64])
        I("rw_mask1", [2, 64, 192]); I("rw_mask2", [2, 64, 192]); I("rw_maskf", [2, 128, 128])
        I("rw_reset", [128, 512]); I("rw_ii2", [128, 64]); I("rw_bd", [128, 128])
        self.outp("st_out", [2, 2, 16, 64, 64])
        self.outp("yp", [D, TP]); self.outp("ys", [D, TS])
        self.AGIN = [self.scratch("agin%d" % i, [256, TS], BF16) for i in range(DEPTH)]
        self.AGOUT = [self.scratch("agout%d" % i, [1024, TS], BF16) for i in range(DEPTH)]

    def prologue(self):
        d = self.din
        with self.scope() as st:
            self.dsp([self.XP], [d["xp"]], self.XP[:, :, :], d["xp"].h.ap().rearrange("(k p) t -> p k t", p=128))
            for k in range(KC):
                self.dsp([self.XS], [d["xs"]], self.XS[:, k, :], d["xs"].h[k * 128:(k + 1) * 128, :])
            self.dsp([self.LNG], [d["ln_gT"]], self.LNG[:, :, :], d["ln_gT"].h[:, :, :])
            self.dsp([self.LNB], [d["ln_bT"]], self.LNB[:, :, :], d["ln_bT"].h[:, :, :])
            self.dsp([self.IDF], [d["ident"]], self.IDF[:, :], d["ident"].h[:, :])
            self.dpool([self.IDB], [d["ident"]], self.IDB[:, :], d["ident"].h[:, :])
            self.memset(self.ONESF, self.ONESF[:, :], 1.0)
            self.memset(self.EPS, self.EPS[:, 0:1], LN_EPS)
            self.memset(self.EPS, self.EPS[:, 1:2], RMS_EPS)
            self.memset(self.EPS, self.EPS[:, 2:3], 64e-5)
            self.memset(self.EPS, self.EPS[:, 3:4], 1e-12)
            CD = self.sb(st, "CD", [128, 8, 2], F32)
            SC = self.sb(st, "SC", [128, 8, 2], BF16)
            MB = self.sb(st, "MB", [128, DEPTH, 24], F32)
            WB = [self.sb(st, "WBm%d" % i, [128, 8, 512], BF16) for i in range(3)]
            self.dsp([CD], [d["condT"]], CD[:, :, :], d["condT"].h[:, :, :])
            self.dsp([MB], [d["mod_bT"]], MB[:, :, :], d["mod_bT"].h[:, :, :])
            self.act(SC, SC[:, :, :], CD, CD[:, :, :], AF.Silu)
            wi = 0
            for i in range(DEPTH):
                bank = self.ps()
                for cb in range(6):
                    wb = WB[wi % 3]
                    wi += 1
                    src = d["mod_w"].h[i].rearrange("(k p) e -> p k e", p=128)[:, :, cb * 512:(cb + 1) * 512]
                    self.dpool([wb], [d["mod_w"]], wb[:, :, :], src)
                    for m4 in range(4):
                        m = cb * 4 + m4
                        for k in range(KC):
                            self.P(bank, bank[:, 2 * m:2 * m + 2], wb[:, k, m4 * 128:(m4 + 1) * 128], SC[:, k, :],
                                   start=(k == 0), stop=(k == KC - 1), reads=[wb, SC])
                self.tt(self.MOD, self.MOD[:, i, :, :], bank, bank[:, 0:48].rearrange("p (m n) -> p m n", n=2),
                        MB, MB[:, i, :].unsqueeze(2).to_broadcast([128, 24, 2]), ALU.add)
                self.ts(self.MOD, self.MOD[:, i, 8:16, :], self.MOD, self.MOD[:, i, 8:16, :], 1.0, None, ALU.add)
            self.end_block("prologue")

    def make_h(self, i, n, hb, out_ap_fn, X, col0, ncol):
        for k in range(KC):
            self.act(hb, out_ap_fn(k), X, X[:, k, col0:col0 + ncol], AF.Identity,
                     scale=self.MOD[:, i, 8 + k, n:n + 1], bias=self.MOD[:, i, k, n:n + 1],
                     extra_reads=[self.MOD])

    def outproj_ln(self, i, st, wsrc, blocks):
        WO = self.sb(st, "WO", [128, 8, D], BF16)
        for cb in range(2):
            self.dpool([WO], [wsrc[0]], WO[:, :, cb * 512:(cb + 1) * 512],
                       wsrc[1].rearrange("(k p) e -> p k e", p=128)[:, :, cb * 512:(cb + 1) * 512])
        T1 = self.sb(st, "T1", [128, 512], F32)
        SQ = self.sb(st, "SQ", [128, 8, 512], F32)
        MEAN = self.sb(st, "MEAN", [128, 512], F32)
        VAR = self.sb(st, "VAR", [128, 512], F32)
        M2 = self.sb(st, "M2", [128, 512], F32)
        for (n, X, col0, OB, loader) in blocks:
            if loader is not None:
                loader()
            for ec in range(KC):
                bank = self.ps()
                for k in range(KC):
                    self.P(bank, bank[:, :], WO[:, k, ec * 128:(ec + 1) * 128], OB[:, k, :],
                           start=(k == 0), stop=(k == KC - 1), reads=[WO, OB])
                self.act(T1, T1[:, :], bank, bank[:, :], AF.Identity, scale=self.MOD[:, i, 16 + ec, n:n + 1],
                         extra_reads=[self.MOD])
                xs_ = X[:, ec, col0:col0 + 512]
                self.stt(X, xs_, X, xs_, ALPHA, T1, T1[:, :], ALU.mult, ALU.add)
            self.layernorm_block(i, X, col0, SQ, MEAN, VAR, M2)

    def layernorm_block(self, i, X, col0, SQ, MEAN, VAR, M2):
        xb = X[:, :, col0:col0 + 512]
        self.act(SQ, SQ[:, :, :], X, xb, AF.Square)
        b1 = self.ps()
        b2 = self.ps()
        for k in range(KC):
            self.P(b1, b1[:, :], self.ONESF[:, :], X[:, k, col0:col0 + 512], start=(k == 0), stop=(k == KC - 1),
                   reads=[self.ONESF, X])
        for k in range(KC):
            self.P(b2, b2[:, :], self.ONESF[:, :], SQ[:, k, :], start=(k == 0), stop=(k == KC - 1),
                   reads=[self.ONESF, SQ])
        self.act(MEAN, MEAN[:, :], b1, b1[:, :], AF.Identity, scale=1.0 / D)
        self.tt(M2, M2[:, :], MEAN, MEAN[:, :], MEAN, MEAN[:, :], ALU.mult)
        self.stt(VAR, VAR[:, :], b2, b2[:, :], 1.0 / D, M2, M2[:, :], ALU.mult, ALU.subtract)
        self.act(VAR, VAR[:, :], VAR, VAR[:, :], AF.Sqrt, scale=1.0, bias=self.EPS[:, 0:1], extra_reads=[self.EPS])
        self.recip(VAR, VAR[:, :], VAR, VAR[:, :])
        self.tt(X, xb, X, xb, MEAN, MEAN[:, :].unsqueeze(1).to_broadcast([128, 8, 512]), ALU.subtract)
        self.tt(X, xb, X, xb, VAR, VAR[:, :].unsqueeze(1).to_broadcast([128, 8, 512]), ALU.mult)
        for k in range(KC):
            xk = X[:, k, col0:col0 + 512]
            self.act(X, xk, X, xk, AF.Identity, scale=self.LNG[:, i, k:k + 1], bias=self.LNB[:, i, k:k + 1],
                     extra_reads=[self.LNG, self.LNB])

    def gather_and_outproj(self, i, st, wsrc, OP, OS):
        agin, agout = self.AGIN[i], self.AGOUT[i]
        self.dsp([agin], [OS], agin.h.ap().rearrange("(j p) t -> p j t", p=128), OS[:, :, :])
        self.k.special("pool", lambda e: e.collective_compute(
            "AllGather", ALU.bypass, replica_groups=[[0, 1, 2, 3], [4, 5, 6, 7]],
            ins=[agin.h.ap().opt()], outs=[agout.h.ap().opt()]), reads=[agin], writes=[agout])
        OB = [self.sb(st, "OB%d" % t, [128, 8, 512], BF16) for t in range(2)]
        blocks = [(0, self.XP, 0, OP, None)]
        for tb in range(4):
            ob = OB[tb % 2]

            def loader(ob=ob, tb=tb):
                self.dsp([ob], [agout], ob[:, :, :],
                         agout.h.ap().rearrange("(k p) t -> p k t", p=128)[:, :, tb * 512:(tb + 1) * 512])
            blocks.append((1, self.XS, tb * 512, ob, loader))
        self.outproj_ln(i, st, wsrc, blocks)

    def hyena_filter_mlp(self, st, j, L, featname):
        d = self.din
        W1 = self.sb(st, "hW1", [33, 64], F32)
        W2 = self.sb(st, "hW2", [64, 64], F32)
        B12 = self.sb(st, "hB12", [64, 4], F32)
        FQ = self.sb(st, "hFQ", [64, 2], F32)
        self.dsp([W1], [d["hy_w1"]], W1[:, :], d["hy_w1"].h[j])
        self.dsp([W2], [d["hy_w2"]], W2[:, :], d["hy_w2"].h[j])
        self.dsp([B12], [d["hy_b12"]], B12[:, 0:2], d["hy_b12"].h[j])
        self.dsp([FQ], [d["hy_freqT"]], FQ[:, :], d["hy_freqT"].h[j])
        self.tt(B12, B12[:, 2:4], B12, B12[:, 0:2], FQ, FQ[:, 0:2], ALU.mult)
        FT = self.sb(st, "hFT", [33, L], F32)
        self.dsp([FT], [d[featname]], FT[:, :], d[featname].h[:, :])
        H1 = self.sb(st, "hH1", [64, L], F32)
        H2A = self.sb(st, "hH2A", [65, L], F32)
        AR = self.sb(st, "hAR", [64, 512], F32)
        KI = self.sb(st, "hKI", [64, 512], I32)
        self.memset(H2A, H2A[64:65, :], 1.0)
        nb = max(1, L // 512)
        bw = min(L, 512)
        for layer in range(2):
            for b in range(nb):
                cs = slice(b * bw, (b + 1) * bw)
                bank = self.ps()
                if layer == 0:
                    self.P(bank, bank[0:64, 0:bw], W1[:, :], FT[:, cs], True, True, [W1, FT])
                    dst, dt_ = H1, H1[:, cs]
                else:
                    self.P(bank, bank[0:64, 0:bw], W2[:, :], H1[:, cs], True, True, [W2, H1])
                    dst, dt_ = H2A, H2A[0:64, cs]
                self.act(AR, AR[:, 0:bw], bank, bank[0:64, 0:bw], AF.Identity, scale=FQ[:, layer:layer + 1],
                         bias=B12[:, 2 + layer:3 + layer], extra_reads=[FQ, B12])
                self.ts(KI, KI[:, 0:bw], AR, AR[:, 0:bw], 1.0 / TWO_PI, None, ALU.mult)
                self.stt(AR, AR[:, 0:bw], KI, KI[:, 0:bw], -TWO_PI, AR, AR[:, 0:bw], ALU.mult, ALU.add)
                self.ts(AR, AR[:, 0:bw], AR, AR[:, 0:bw], -3.14159, 3.14159, ALU.max, ALU.min)
                self.act(dst, dt_, AR, AR[:, 0:bw], AF.Sin)
        return H2A

    def hyena_layer(self, i, j):
        d = self.din
        with self.scope() as st:
            OP = self.sb(st, "OP", [128, 8, TP], BF16)
            OS = self.sb(st, "OS", [128, 2, TS], BF16)
            with self.scope() as s2:
                VX = self.sb(s2, "VXp", [128, 8, TP], BF16)
                G = self.sb(s2, "Gp", [128, 8, TP], BF16)
                CV = self.sb(s2, "CVp", [128, 8, 3, 4], F32)
                SK = self.sb(s2, "SKp", [128, 8], F32)
                self.dsp([CV], [d["hy_cvT"]], CV[:, :, :, :], d["hy_cvT"].h[:, j])
                self.dsp([SK], [d["hy_skipT"]], SK[:, :], d["hy_skipT"].h[:, j, :])
                FS = self.sb(s2, "FSp", [128, 2, D], BF16)
                FD = self.sb(s2, "FDp", [128, 2, D], BF16)
                VXT = self.sb(s2, "VXTp", [128, 4, D], BF16)
                with self.scope() as s3:
                    H2A = self.hyena_filter_mlp(s3, j, SEQ, "feat_p")
                    W3 = self.sb(s3, "W3p", [65, 2048], F32)
                    ADEC = self.sb(s3, "ADECp", [128, 2048], F32)
                    NEGT = self.sb(s3, "NEGTp", [128, 2], F32)
                    WIN = self.sb(s3, "WINp", [128, 2, 512], F32)
                    HW = self.sb(s3, "HWp", [128, 2, 512], F32)
                    self.dsp([W3], [d["hy_w3a"]], W3[:, :], d["hy_w3a"].h[j])
                    self.dsp([ADEC], [d["hy_decay"]], ADEC[:, :], d["hy_decay"].h[j].partition_broadcast(128))
                    self.dsp([NEGT], [d["negt_p"]], NEGT[:, :], d["negt_p"].h[:, :])
                    self.act(ADEC, ADEC[:, :], ADEC, ADEC[:, :], AF.Abs)
                    for lch in range(2):
                        for cb in range(2):
                            for dr in range(2):
                                bank = self.ps()
                                cs = slice(dr * 1024 + cb * 512, dr * 1024 + cb * 512 + 512)
                                self.P(bank, bank[:, :], H2A[0:65, lch * 128:(lch + 1) * 128], W3[0:65, cs], True, True,
                                       [H2A, W3])
                                self.act(WIN, WIN[:, dr, :], ADEC, ADEC[:, cs], AF.Exp, scale=NEGT[:, lch:lch + 1],
                                         extra_reads=[NEGT])
                                self.tt(HW, HW[:, dr, :], bank, bank[:, :], WIN, WIN[:, dr, :], ALU.mult)
                            if lch == 0:
                                self.memset(HW, HW[0:1, 1, :], 0.0)
                            self.tt(FS, FS[:, lch, cb * 512:(cb + 1) * 512], HW, HW[:, 0, :], HW, HW[:, 1, :], ALU.add)
                            self.tt(FD, FD[:, lch, cb * 512:(cb + 1) * 512], HW, HW[:, 0, :], HW, HW[:, 1, :], ALU.subtract)
                with self.scope() as s3:
                    WBs = [self.sb(s3, "WBh%d" % t, [128, 8, 512], BF16) for t in range(2)]
                    segs = [(0, SEQ, 0, SEQ), (SEQ, SEQ, 0, SEQ)]
                    HB = [self.sb(s3, "HBp%d" % t, [128, 8, 258], BF16) for t in range(2)]
                    for s in range(2):
                        hb = HB[s]
                        self.memset(hb, hb[:, :, 0:1], 0.0)
                        self.memset(hb, hb[:, :, 257:258], 0.0)
                        self.make_h(i, 0, hb, lambda k, hb=hb: hb[:, k, 1:257], self.XP, s * SEQ, SEQ)
                    UC = [self.sb(s3, "UCp%d" % t, [128, 256], F32) for t in range(3)]
                    SZ = self.sb(s3, "SZp", [128, 256], F32)
                    for jc in range(8):
                        wb = WBs[jc % 2]
                        self.dpool([wb], [d["hy_win"]], wb[:, :, :],
                                   d["hy_win"].h[j].rearrange("(k p) e -> p k e", p=128)[:, :, jc * 512:(jc + 1) * 512])
                        for s in range(2):
                            hb = HB[s]
                            self.conv_chunk(hb, 258, SEQ, wb, 0, CV[:, jc], CV, UC, SZ, VX, G, jc, s * SEQ)
                        bank = self.ps()
                        for tb in range(4):
                            self.P(bank, bank[:, tb * 128:(tb + 1) * 128], VX[:, jc, tb * 128:(tb + 1) * 128], self.IDB[:, :],
                                   True, True, [VX, self.IDB])
                        self.cp(VXT, VXT[:, :, jc * 128:(jc + 1) * 128], bank, bank[:, :].rearrange("p (a b) -> p a b", b=128))
                YRE = self.sb(s2, "YREp", [128, 2, 2, D], BF16)
                YIM = self.sb(s2, "YIMp", [128, 2, 2, D], BF16)
                YNQ = self.sb(s2, "YNQp", [1, 2, D], BF16)
                with self.scope() as s3:
                    FW = self.sb(s3, "FWp", [128, 2, 2, 256], BF16)
                    NQ = self.sb(s3, "NQp", [128, 2], BF16)
                    self.dsp([FW], [d["fwd_p"]], FW[:, :, :, :], d["fwd_p"].h[:, :, :, :])
                    self.dsp([NQ], [d["nyq_p"]], NQ[:, :], d["nyq_p"].h[:, :])
                    KRE = self.sb(s3, "KREp", [128, 2, D], F32)
                    KQ = self.sb(s3, "KQp", [128, 2, D], F32)
                    KNQ = self.sb(s3, "KNQp", [1, D], F32)
                    SA = self.sb(s3, "SAp", [128, 512], F32)
                    SB_ = self.sb(s3, "SBp", [128, 512], F32)
                    T1 = self.sb(s3, "T1p", [128, 512], F32)
                    T2 = self.sb(s3, "T2p", [128, 512], F32)
                    for fch in range(2):
                        for cb in range(2):
                            cs = slice(cb * 512, (cb + 1) * 512)
                            ba = self.ps()
                            bb = self.ps()
                            for lch in range(2):
                                self.P(ba, ba[:, :], FW[:, lch, 0, fch * 128:(fch + 1) * 128], FS[:, lch, cs],
                                       lch == 0, lch == 1, [FW, FS])
                            for lch in range(2):
                                self.P(bb, bb[:, :], FW[:, lch, 1, fch * 128:(fch + 1) * 128], FD[:, lch, cs],
                                       lch == 0, lch == 1, [FW, FD])
                            self.act(KRE, KRE[:, fch, cs], ba, ba[:, :], AF.Copy)
                            self.act(KQ, KQ[:, fch, cs], bb, bb[:, :], AF.Copy)
                    for cb in range(2):
                        cs = slice(cb * 512, (cb + 1) * 512)
                        bn = self.ps()
                        for lch in range(2):
                            self.P(bn, bn[0:1, :], NQ[:, lch:lch + 1], FS[:, lch, cs], lch == 0, lch == 1, [NQ, FS])
                        self.act(KNQ, KNQ[0:1, cs], bn, bn[0:1, :], AF.Copy)
                    for s in range(2):
                        for fch in range(2):
                            for cb in range(2):
                                cs = slice(cb * 512, (cb + 1) * 512)
                                ba = self.ps()
                                bb = self.ps()
                                for lch in range(2):
                                    self.P(ba, ba[:, :], FW[:, lch, 0, fch * 128:(fch + 1) * 128], VXT[:, s * 2 + lch, cs],
                                           lch == 0, lch == 1, [FW, VXT])
                                for lch in range(2):
                                    self.P(bb, bb[:, :], FW[:, lch, 1, fch * 128:(fch + 1) * 128], VXT[:, s * 2 + lch, cs],
                                           lch == 0, lch == 1, [FW, VXT])
                                self.act(SA, SA[:, :], ba, ba[:, :], AF.Copy)
                                self.act(SB_, SB_[:, :], bb, bb[:, :], AF.Copy)
                                self.tt(T1, T1[:, :], SA, SA[:, :], KRE, KRE[:, fch, cs], ALU.mult)
                                self.tt(T2, T2[:, :], SB_, SB_[:, :], KQ, KQ[:, fch, cs], ALU.mult)
                                self.tt(YRE, YRE[:, fch, s, cs], T1, T1[:, :], T2, T2[:, :], ALU.subtract)
                                self.tt(T1, T1[:, :], SA, SA[:, :], KQ, KQ[:, fch, cs], ALU.mult)
                                self.tt(T2, T2[:, :], SB_, SB_[:, :], KRE, KRE[:, fch, cs], ALU.mult)
                                self.tt(YIM, YIM[:, fch, s, cs], T1, T1[:, :], T2, T2[:, :], ALU.add)
                        for cb in range(2):
                            cs = slice(cb * 512, (cb + 1) * 512)
                            bn = self.ps()
                            for lch in range(2):
                                self.P(bn, bn[0:1, :], NQ[:, lch:lch + 1], VXT[:, s * 2 + lch, cs], lch == 0, lch == 1,
                                       [NQ, VXT])
                            self.tt(YNQ, YNQ[0:1, s, cs], bn, bn[0:1, :], KNQ, KNQ[0:1, cs], ALU.mult)
                with self.scope() as s3:
                    IV = self.sb(s3, "IVp", [128, 2, 2, 256], BF16)
                    NR = self.sb(s3, "NRp", [1, SEQ], BF16)
                    T1 = self.sb(s3, "T1q", [128, 512], F32)
                    self.dsp([IV], [d["inv_p"]], IV[:, :, :, :], d["inv_p"].h[:, :, :, :])
                    self.dsp([NR], [d["nyqrow_p"]], NR[:, :], d["nyqrow_p"].h[:, :])
                    for jc in range(8):
                        bank = self.ps()
                        cc = slice(jc * 128, (jc + 1) * 128)
                        for s in range(2):
                            o = bank[:, s * SEQ:(s + 1) * SEQ]
                            for fch in range(2):
                                self.P(bank, o, YRE[:, fch, s, cc], IV[:, fch, 0, :], fch == 0, False, [YRE, IV])
                                self.P(bank, o, YIM[:, fch, s, cc], IV[:, fch, 1, :], False, False, [YIM, IV])
                            self.P(bank, o, YNQ[0:1, s, cc], NR[0:1, :], False, True, [YNQ, NR])
                        self.stt(T1, T1[:, :], VX, VX[:, jc, :], SK[:, jc:jc + 1], bank, bank[:, :], ALU.mult, ALU.add,
                                 extra_reads=[SK])
                        self.tt(OP, OP[:, jc, :], T1, T1[:, :], G, G[:, jc, :], ALU.mult)
            with self.scope() as s2:
                VX = self.sb(s2, "VXs", [128, 2, TS], BF16)
                G = self.sb(s2, "Gs", [128, 2, TS], BF16)
                CV = self.sb(s2, "CVs", [128, 2, 3, 4], F32)
                SK = self.sb(s2, "SKs", [128, 2], F32)
                self.dsp([CV], [d["hy_cv_own"]], CV[:, :, :, :], d["hy_cv_own"].h[:, j])
                self.dsp([SK], [d["hy_skip_own"]], SK[:, :], d["hy_skip_own"].h[:, j, :])
                RS = self.sb(s2, "RSs", [128, 16, 768], BF16)
                with self.scope() as s3:
                    H2A = self.hyena_filter_mlp(s3, j, TS, "feat_s")
                    W3 = self.sb(s3, "W3s", [65, 512], F32)
                    ADEC = self.sb(s3, "ADECs", [128, 512], F32)
                    NEGT = self.sb(s3, "NEGTs", [128, 16], F32)
                    WIN = self.sb(s3, "WINs", [128, 512], F32)
                    HW = self.sb(s3, "HWs", [128, 512], F32)
                    self.dsp([W3], [d["hy_w3a_own"]], W3[:, :], d["hy_w3a_own"].h[j])
                    self.dsp([ADEC], [d["hy_decay_own"]], ADEC[:, :], d["hy_decay_own"].h[j].partition_broadcast(128))
                    self.dsp([NEGT], [d["negt_s"]], NEGT[:, :], d["negt_s"].h[:, :])
                    self.act(ADEC, ADEC[:, :], ADEC, ADEC[:, :], AF.Abs)
                    for lch in range(16):
                        bank = self.ps()
                        self.P(bank, bank[:, :], H2A[0:65, lch * 128:(lch + 1) * 128], W3[0:65, :], True, True, [H2A, W3])
                        self.act(WIN, WIN[:, :], ADEC, ADEC[:, :], AF.Exp, scale=NEGT[:, lch:lch + 1], extra_reads=[NEGT])
                        self.tt(HW, HW[:, :], bank, bank[:, :], WIN, WIN[:, :], ALU.mult)
                        if lch == 0:
                            self.memset(HW, HW[0:1, 256:512], 0.0)
                        self.tt(RS, RS[:, lch, 0:256], HW, HW[:, 0:256], HW, HW[:, 256:512], ALU.add)
                        self.tt(RS, RS[:, lch, 512:768], HW, HW[:, 0:256], HW, HW[:, 256:512], ALU.subtract)
                with self.scope() as s3:
                    WB = self.sb(s3, "WBs", [128, 8, 1024], BF16)
                    for cb in range(2):
                        self.dpool([WB], [d["hy_win_own"]], WB[:, :, cb * 512:(cb + 1) * 512],
                                   d["hy_win_own"].h[j].rearrange("(k p) e -> p k e", p=128)[:, :, cb * 512:(cb + 1) * 512])
                    HB = [self.sb(s3, "HBs%d" % t, [128, 8, 512], BF16) for t in range(2)]
                    UC = [self.sb(s3, "UCs%d" % t, [128, 512], F32) for t in range(3)]
                    SZ = self.sb(s3, "SZs", [128, 512], F32)
                    t0 = 0
                    si = 0
                    while t0 < TS:
                        nt = min(510, TS - t0)
                        hb = HB[si % 2]
                        si += 1
                        lo = max(t0 - 1, 0)
                        hi = min(t0 + nt + 1, TS)
                        off = lo - (t0 - 1)
                        ncols = nt + 2
                        if off > 0:
                            self.memset(hb, hb[:, :, 0:1], 0.0)
                        if hi < t0 + nt + 1:
                            self.memset(hb, hb[:, :, ncols - 1:ncols], 0.0)
                        self.make_h(i, 1, hb, lambda k, hb=hb, off=off, w=hi - lo: hb[:, k, off:off + w], self.XS, lo, hi - lo)
                        for jc in range(2):
                            self.conv_chunk(hb, ncols, nt, WB, jc * 512, CV[:, jc], CV, UC, SZ, VX, G, jc, t0)
                        t0 += nt
                    for l2 in range(8):
                        bank = self.ps()
                        for u in range(2):
                            lch = l2 * 2 + u
                            for jc in range(2):
                                self.P(bank, bank[:, u * 256 + jc * 128:u * 256 + (jc + 1) * 128],
                                       VX[:, jc, lch * 128:(lch + 1) * 128], self.IDB[:, :], True, True, [VX, self.IDB])
                        self.cp(RS, RS[:, l2 * 2:l2 * 2 + 2, 256:512], bank, bank[:, :].rearrange("p (a b) -> p a b", b=256))
                YRE = self.sb(s2, "YREs", [128, 16, 256], BF16)
                YIM = self.sb(s2, "YIMs", [128, 16, 256], BF16)
                YNQ = self.sb(s2, "YNQs", [1, 256], BF16)
                with self.scope() as s3:
                    FW = [self.sb(s3, "FWs%d" % t, [128, 16, 2, 128], BF16) for t in range(2)]
                    NQ = self.sb(s3, "NQs", [128, 16], BF16)
                    SA = self.sb(s3, "SAs", [128, 512], F32)
                    SB_ = self.sb(s3, "SBs", [128, 512], F32)
                    T1 = self.sb(s3, "T1s", [128, 256], F32)
                    T2 = self.sb(s3, "T2s", [128, 256], F32)
                    self.dsp([NQ], [d["nyq_s"]], NQ[:, :], d["nyq_s"].h[:, :])
                    for fch in range(16):
                        fw = FW[fch % 2]
                        self.dsp([fw], [d["fwd_s"]], fw[:, :, :, :], d["fwd_s"].h[fch])
                        ba = self.ps()
                        bb = self.ps()
                        for lch in range(16):
                            self.P(ba, ba[:, :], fw[:, lch, 0, :], RS[:, lch, 0:512], lch == 0, lch == 15, [fw, RS])
                        for lch in range(16):
                            self.P(bb, bb[:, :], fw[:, lch, 1, :], RS[:, lch, 256:768], lch == 0, lch == 15, [fw, RS])
                        self.act(SA, SA[:, :], ba, ba[:, :], AF.Copy)
                        self.act(SB_, SB_[:, :], bb, bb[:, :], AF.Copy)
                        self.tt(T1, T1[:, :], SA, SA[:, 0:256], SA, SA[:, 256:512], ALU.mult)
                        self.tt(T2, T2[:, :], SB_, SB_[:, 0:256], SB_, SB_[:, 256:512], ALU.mult)
                        self.tt(YRE, YRE[:, fch, :], T1, T1[:, :], T2, T2[:, :], ALU.subtract)
                        self.tt(T1, T1[:, :], SA, SA[:, 256:512], SB_, SB_[:, 256:512], ALU.mult)
                        self.tt(T2, T2[:, :], SB_, SB_[:, 0:256], SA, SA[:, 0:256], ALU.mult)
                        self.tt(YIM, YIM[:, fch, :], T1, T1[:, :], T2, T2[:, :], ALU.add)
                    bn = self.ps()
                    for lch in range(16):
                        self.P(bn, bn[0:1, :], NQ[:, lch:lch + 1], RS[:, lch, 0:512], lch == 0, lch == 15, [NQ, RS])
                    self.act(SA, SA[0:1, :], bn, bn[0:1, :], AF.Copy)
                    self.tt(YNQ, YNQ[0:1, :], SA, SA[0:1, 0:256], SA, SA[0:1, 256:512], ALU.mult)
                with self.scope() as s3:
                    IV = [self.sb(s3, "IVs%d" % t, [128, 2, 512], BF16) for t in range(4)]
                    NR = self.sb(s3, "NRs", [1, TS], BF16)
                    T1 = self.sb(s3, "T1t", [128, 512], F32)
                    self.dsp([NR], [d["nyqrow_s"]], NR[:, :], d["nyqrow_s"].h[:, :])
                    ivn = 0
                    for tb in range(4):
                        ts_ = slice(tb * 512, (tb + 1) * 512)
                        banks = [self.ps(), self.ps()]
                        for fch in range(16):
                            iv = IV[ivn % 4]
                            ivn += 1
                            self.dsp([iv], [d["inv_s"]], iv[:, :, :], d["inv_s"].h[tb, fch])
                            for jc in range(2):
                                cc = slice(jc * 128, (jc + 1) * 128)
                                self.P(banks[jc], banks[jc][:, :], YRE[:, fch, cc], iv[:, 0, :], fch == 0, False, [YRE, iv])
                                self.P(banks[jc], banks[jc][:, :], YIM[:, fch, cc], iv[:, 1, :], False, False, [YIM, iv])
                        for jc in range(2):
                            cc = slice(jc * 128, (jc + 1) * 128)
                            self.P(banks[jc], banks[jc][:, :], YNQ[0:1, cc], NR[0:1, ts_], False, True, [YNQ, NR])
                            self.stt(T1, T1[:, :], VX, VX[:, jc, ts_], SK[:, jc:jc + 1], banks[jc], banks[jc][:, :],
                                     ALU.mult, ALU.add, extra_reads=[SK])
                            self.tt(OS, OS[:, jc, ts_], T1, T1[:, :], G, G[:, jc, ts_], ALU.mult)
            self.gather_and_outproj(i, st, (d["hy_wout"], d["hy_wout"].h[j]), OP, OS)
            self.end_block("hyena%d" % i)

    def conv_chunk(self, hb, ncols, nt, WB, wc0, cv, cv_t, UC, SZ, VX, G, jc, outc0):
        banks = [self.ps() for _ in range(4)]
        for w in range(4):
            for k in range(KC):
                c0 = wc0 + w * 128
                self.P(banks[w], banks[w][:, 0:ncols], WB[:, k, c0:c0 + 128], hb[:, k, 0:ncols],
                       start=(k == 0), stop=(k == KC - 1), reads=[WB, hb])
        for w in range(3):
            uc = UC[w]
            bk = banks[w]
            self.act(uc, uc[:, 0:nt], bk, bk[:, 1:nt + 1], AF.Identity, scale=cv[:, w, 1:2], bias=cv[:, w, 3:4],
                     extra_reads=[cv_t])
            self.stt(uc, uc[:, 0:nt], bk, bk[:, 0:nt], cv[:, w, 0:1], uc, uc[:, 0:nt], ALU.mult, ALU.add,
                     extra_reads=[cv_t])
            self.stt(uc, uc[:, 0:nt], bk, bk[:, 2:nt + 2], cv[:, w, 2:3], uc, uc[:, 0:nt], ALU.mult, ALU.add,
                     extra_reads=[cv_t])
        self.act(SZ, SZ[:, 0:nt], banks[3], banks[3][:, 1:nt + 1], AF.Silu)
        self.tt(VX, VX[:, jc, outc0:outc0 + nt], UC[1], UC[1][:, 0:nt], UC[2], UC[2][:, 0:nt], ALU.mult)
        self.tt(G, G[:, jc, outc0:outc0 + nt], UC[0], UC[0][:, 0:nt], SZ, SZ[:, 0:nt], ALU.mult)

    def mla_layer(self, i, j):
        d = self.din
        with self.scope() as st:
            OP = self.sb(st, "OPm", [128, 8, TP], BF16)
            OS = self.sb(st, "OSm", [128, 2, TS], BF16)
            NRM = self.sb(st, "NRMm", [128, 3], F32)
            WH = None
            self.dsp([NRM], [d["mla_nrm"]], NRM[:, :], d["mla_nrm"].h[:, :])
            self.mla_group(i, st, 0, self.XP, TP, 16, 8, 0, TP, [(0, SEQ, 0, SEQ), (SEQ, SEQ, SEQ, SEQ)], False,
                           d["mla_wz"], d["mla_wq"], d["mla_wk"], d["mla_wv"], OP, WH, NRM)
            self.mla_group(i, st, 1, self.XS, TS, 4, 2, 512, 2560, [(0, TS, 0, 2560)], True,
                           d["mla_wz_own"], d["mla_wq_own"], d["mla_wk_own"], d["mla_wv_own"], OS, WH, NRM)
            self.gather_and_outproj(i, st, (d["mla_wout"], d["mla_wout"].h[j]), OP, OS)
            self.end_block("mla%d" % i)

    def mla_group(self, i, st0, n, X, T, NH, ZC, koff, KA, units, rope, wz, wq, wk, wv, OUT, WH, NRM):
        d = self.din
        SCALE = 96.0 ** -0.5
        nblk = T // 512
        with self.scope() as st:
            QN = self.sb(st, "QN", [128, 2, T], BF16)
            CKVb = self.sb(st, "CKVb", [128, KA], BF16)
            KPEb = self.sb(st, "KPEb", [32, KA], BF16)
            ZG = self.sb(st, "ZG", [128, ZC, T], BF16)
            if rope:
                self.dpool([CKVb], [d["cache_ckvT"]], CKVb[:, 0:512], d["cache_ckvT"].h[:, :])
                self.dpool([KPEb], [d["cache_kpeT"]], KPEb[:, 0:512], d["cache_kpeT"].h[:, :])
            with self.scope() as s2:
                WH = self.sb(s2, "WHm", [128, 8, 416], BF16)
                self.dpool([WH], [d["mla_wh"]], WH[:, :, :], d["mla_wh"].h.ap().rearrange("(k p) e -> p k e", p=128))
                if rope:
                    CS32 = [self.sb(s2, "CS32_%d" % t, [32, 2, 512], F32) for t in range(2)]
                    R32 = self.sb(s2, "R32", [32, 32], F32)
                    self.dsp([R32], [d["rope_R32"]], R32[:, :], d["rope_R32"].h[:, :])
                WZ = self.sb(s2, "WZ", [128, 8, ZC * 128], BF16)
                for cb in range(max(1, ZC * 128 // 512)):
                    w_ = min(512, ZC * 128)
                    self.dpool([WZ], [wz], WZ[:, :, cb * w_:(cb + 1) * w_],
                               wz.h.ap().rearrange("(k p) e -> p k e", p=128)[:, :, cb * w_:(cb + 1) * w_])
                HB = [self.sb(s2, "HBm%d" % t, [128, 8, 512], BF16) for t in range(min(2, nblk))]
                QC = self.sb(s2, "QC", [128, 2, 512], F32)
                KVC = self.sb(s2, "KVC", [128, 512], F32)
                KPF = self.sb(s2, "KPF", [32, 512], F32)
                SQ = self.sb(s2, "SQm", [128, 3, 512], F32)
                RS = self.sb(s2, "RSm", [128, 2, 512], F32)
                CKVf = self.sb(s2, "CKVf", [128, 512], F32)
                T1 = self.sb(s2, "T1m", [32, 512], F32)
                T2 = self.sb(s2, "T2m", [32, 512], F32)
                for tb in range(nblk):
                    ts_ = slice(tb * 512, (tb + 1) * 512)
                    hb = HB[tb % len(HB)]
                    self.make_h(i, n, hb, lambda k, hb=hb: hb[:, k, :], X, tb * 512, 512)
                    if rope:
                        cs32 = CS32[tb % 2]
                        self.dsp([cs32], [d["rope_cs32"]], cs32[:, :, :], d["rope_cs32"].h[:, :, ts_])
                    for m in range(3):
                        bank = self.ps()
                        for k in range(KC):
                            self.P(bank, bank[:, :], WH[:, k, m * 128:(m + 1) * 128], hb[:, k, :], k == 0, k == KC - 1, [WH, hb])
                        if m < 2:
                            self.act(QC, QC[:, m, :], bank, bank[:, :], AF.Copy)
                        else:
                            self.act(KVC, KVC[:, :], bank, bank[:, :], AF.Copy)
                    bank = self.ps()
                    for k in range(KC):
                        self.P(bank, bank[0:32, :], WH[:, k, 384:416], hb[:, k, :], k == 0, k == KC - 1, [WH, hb])
                    self.act(KPF, KPF[:, :], bank, bank[0:32, :], AF.Copy)
                    for zc in range(ZC):
                        bank = self.ps()
                        for k in range(KC):
                            self.P(bank, bank[:, :], WZ[:, k, zc * 128:(zc + 1) * 128], hb[:, k, :], k == 0, k == KC - 1, [WZ, hb])
                        self.act(ZG, ZG[:, zc, ts_], bank, bank[:, :], AF.Silu)
                    self.act(SQ, SQ[:, 0:2, :], QC, QC[:, :, :], AF.Square)
                    self.act(SQ, SQ[:, 2, :], KVC, KVC[:, :], AF.Square)
                    bq = self.ps()
                    for rk in range(2):
                        self.P(bq, bq[:, :], self.ONESF[:, :], SQ[:, rk, :], rk == 0, rk == 1, [self.ONESF, SQ])
                    bk = self.ps()
                    self.P(bk, bk[:, :], self.ONESF[:, :], SQ[:, 2, :], True, True, [self.ONESF, SQ])
                    self.act(RS, RS[:, 0, :], bq, bq[:, :], AF.Sqrt, scale=1.0 / 256, bias=self.EPS[:, 1:2], extra_reads=[self.EPS])
                    self.act(RS, RS[:, 1, :], bk, bk[:, :], AF.Sqrt, scale=1.0 / 128, bias=self.EPS[:, 1:2], extra_reads=[self.EPS])
                    self.recip(RS, RS[:, :, :], RS, RS[:, :, :])
                    for rk in range(2):
                        self.stt(QN, QN[:, rk, ts_], QC, QC[:, rk, :], NRM[:, rk:rk + 1], RS, RS[:, 0, :], ALU.mult, ALU.mult,
                                 extra_reads=[NRM])
                    self.stt(CKVf, CKVf[:, :], KVC, KVC[:, :], NRM[:, 2:3], RS, RS[:, 1, :], ALU.mult, ALU.mult, extra_reads=[NRM])
                    self.cp(CKVb, CKVb[:, koff + tb * 512:koff + (tb + 1) * 512], CKVf, CKVf[:, :])
                    if not rope:
                        self.cp(KPEb, KPEb[:, ts_], KPF, KPF[:, :])
                        o = self.dout
                        self.dsp([o["ckv_out"]], [CKVf], o["ckv_out"].h[:, ts_], CKVf[:, :])
                        self.dsp([o["kpe_out"]], [KPF], o["kpe_out"].h[:, ts_], KPF[:, :])
                    else:
                        bank = self.ps()
                        self.P(bank, bank[0:32, :], R32[:, :], KPF[:, :], True, True, [R32, KPF])
                        self.tt(T1, T1[:, :], KPF, KPF[:, :], cs32, cs32[:, 0, :], ALU.mult)
                        self.tt(T2, T2[:, :], bank, bank[0:32, :], cs32, cs32[:, 1, :], ALU.mult)
                        self.tt(KPEb, KPEb[:, koff + tb * 512:koff + (tb + 1) * 512], T1, T1[:, :], T2, T2[:, :], ALU.add)
            QT = self.sb(st, "QT", [96, NH, T], BF16)
            KT = self.sb(st, "KT", [96, NH, KA], BF16)
            VT = self.sb(st, "VT", [128, KA // 128, NH, 128], BF16)
            NEGC = self.sb(st, "NEGC", [128, NH], F32)
            with self.scope() as s2:
                WQ = self.sb(s2, "WQ", [128, 2, NH * 96], BF16)
                WK = self.sb(s2, "WK", [128, NH * 64], BF16)
                WV = self.sb(s2, "WV", [128, NH * 64], BF16)
                self.dpool([WQ], [wq], WQ[:, :, :], wq.h.ap().rearrange("(k p) e -> p k e", p=128))
                self.dpool([WK], [wk], WK[:, :], wk.h[:, :])
                self.dpool([WV], [wv], WV[:, :], wv.h[:, :])
                if rope:
                    CS96 = [self.sb(s2, "CS96_%d" % t, [96, 2, 512], F32) for t in range(2)]
                    R96 = self.sb(s2, "R96", [96, 96], F32)
                    self.dsp([R96], [d["rope_R96"]], R96[:, :], d["rope_R96"].h[:, :])
                    csi = 0
                QF = self.sb(s2, "QF", [96, 512], F32)
                T1 = self.sb(s2, "T1n", [96, 512], F32)
                T2 = self.sb(s2, "T2n", [96, 512], F32)
                SQ2 = self.sb(s2, "SQ2", [96, 512], F32)
                MX = self.sb(s2, "MX", [128, 16], F32)
                MQK = self.sb(s2, "MQK", [128, 2], F32)
                self.memset(VT, VT[:, :, :, :], 1.0)
                for kc4 in range(KA // 512):
                    for h in range(NH):
                        bank = self.ps()
                        self.P(bank, bank[0:64, :], WK[:, h * 64:(h + 1) * 64], CKVb[:, kc4 * 512:(kc4 + 1) * 512], True, True, [WK, CKVb])
                        self.act(KT, KT[0:64, h, kc4 * 512:(kc4 + 1) * 512], bank, bank[0:64, :], AF.Copy)
                for h in range(NH):
                    self.cp(KT, KT[64:96, h, :], KPEb, KPEb[0:32, :])
                for kc in range(KA // 128):
                    for vb in range(max(1, NH * 64 // 512)):
                        w_ = min(512, NH * 64)
                        nh_ = w_ // 64
                        bank = self.ps()
                        self.P(bank, bank[:, 0:w_], CKVb[:, kc * 128:(kc + 1) * 128], WV[:, vb * w_:(vb + 1) * w_], True, True, [CKVb, WV])
                        src = bank[:, 0:w_].rearrange("p (h v) -> p h v", v=64)
                        h0 = vb * nh_
                        self.cp(VT, VT[:, kc, h0:h0 + nh_:2, 0:64], bank, src[:, 0::2, :])
                        self.cp(VT, VT[:, kc, h0 + 1:h0 + nh_:2, 64:128], bank, src[:, 1::2, :])
                for h in range(NH):
                    for tb in range(nblk):
                        ts_ = slice(tb * 512, (tb + 1) * 512)
                        bank = self.ps()
                        for rk in range(2):
                            self.P(bank, bank[0:96, :], WQ[:, rk, h * 96:(h + 1) * 96], QN[:, rk, ts_], rk == 0, rk == 1, [WQ, QN])
                        if rope:
                            cs96 = CS96[csi % 2]
                            csi += 1
                            self.dsp([cs96], [d["rope_cs96"]], cs96[:, :, :], d["rope_cs96"].h[:, :, ts_])
                            self.act(QF, QF[:, :], bank, bank[0:96, :], AF.Copy)
                            b2 = self.ps()
                            self.P(b2, b2[0:96, :], R96[:, :], QF[:, :], True, True, [R96, QF])
                            self.tt(T1, T1[:, :], QF, QF[:, :], cs96, cs96[:, 0, :], ALU.mult)
                            self.tt(T2, T2[:, :], b2, b2[0:96, :], cs96, cs96[:, 1, :], ALU.mult)
                            self.tt(QT, QT[:, h, ts_], T1, T1[:, :], T2, T2[:, :], ALU.add)
                        else:
                            self.act(QT, QT[:, h, ts_], bank, bank[0:96, :], AF.Copy)
                    nq_ = T // 512
                    nk_ = KA // 512
                    for b_ in range(nq_ + nk_):
                        if b_ < nq_:
                            src_t, src = QT, QT[:, h, b_ * 512:(b_ + 1) * 512]
                        else:
                            src_t, src = KT, KT[:, h, (b_ - nq_) * 512:(b_ - nq_ + 1) * 512]
                        self.act(SQ2, SQ2[:, :], src_t, src, AF.Square)
                        bank = self.ps()
                        self.P(bank, bank[:, :], self.ONESF[0:96, :], SQ2[:, :], True, True, [self.ONESF, SQ2])
                        self.V([MX], [bank, MX], lambda e, o=MX[:, b_:b_ + 1], i_=bank[:, :]: e.tensor_reduce(
                            out=o, in_=i_, axis=mybir.AxisListType.X, op=ALU.max))
                    self.V([MQK], [MX, MQK], lambda e, o=MQK[:, 0:1], i_=MX[:, 0:nq_]: e.tensor_reduce(
                        out=o, in_=i_, axis=mybir.AxisListType.X, op=ALU.max))
                    self.V([MQK], [MX, MQK], lambda e, o=MQK[:, 1:2], i_=MX[:, nq_:nq_ + nk_]: e.tensor_reduce(
                        out=o, in_=i_, axis=mybir.AxisListType.X, op=ALU.max))
                    self.tt(MQK, MQK[:, 0:1], MQK, MQK[:, 0:1], MQK, MQK[:, 1:2], ALU.mult)
                    self.act(MQK, MQK[:, 0:1], MQK, MQK[:, 0:1], AF.Sqrt)
                    self.ts(NEGC, NEGC[:, h:h + 1], MQK, MQK[:, 0:1], -SCALE, None, ALU.mult)
            with self.scope() as s2:
                ET = [self.sb(s2, "ET%d" % t, [128, 512], BF16) for t in range(4)]
                RD = self.sb(s2, "RD", [128, 512], F32)
                T1 = self.sb(s2, "T1a", [128, 512], F32)
                eti = 0
                for (q0, nq, k0, nk) in units:
                    QB = min(512, nq)
                    for h in range(NH):
                        ch = h // 2
                        for qb in range(nq // QB):
                            qs = slice(q0 + qb * QB, q0 + (qb + 1) * QB)
                            obank = self.ps(hold=True)
                            nkc = nk // 128
                            for kc in range(nkc):
                                sbank = self.ps()
                                self.P(sbank, sbank[:, 0:QB], KT[0:96, h, k0 + kc * 128:k0 + (kc + 1) * 128], QT[0:96, h, qs],
                                       True, True, [KT, QT])
                                et = ET[eti % 4]
                                eti += 1
                                self.act(et, et[:, 0:QB], sbank, sbank[:, 0:QB], AF.Exp, scale=SCALE, bias=NEGC[:, h:h + 1],
                                         extra_reads=[NEGC])
                                self.P(obank, obank[:, 0:QB], VT[:, k0 // 128 + kc, h, :], et[:, 0:QB], kc == 0, kc == nkc - 1,
                                       [VT, et])
                            if h % 2 == 0:
                                vs, ds_ = slice(0, 64), slice(64, 128)
                            else:
                                vs, ds_ = slice(64, 128), slice(0, 64)
                            self.recip(RD, RD[vs, 0:QB], obank, obank[ds_, 0:QB])
                            self.tt(T1, T1[vs, 0:QB], obank, obank[vs, 0:QB], RD, RD[vs, 0:QB], ALU.mult)
                            self.tt(OUT, OUT[vs, ch, qs], T1, T1[vs, 0:QB], ZG, ZG[vs, ch, qs], ALU.mult)
                            self.release(obank)

    def rwkv_layer(self, i, j):
        import os
        d = self.din
        CDEC = float(np.exp(-0.5))
        with self.scope() as st:
            OP = self.sb(st, "OPr", [128, 8, TP], BF16)
            with self.scope() as sc:
                MU = self.sb(sc, "MU", [128, 6, 8], F32)
                M1 = self.sb(sc, "M1", [64, 2, 192], F32)
                M2 = self.sb(sc, "M2m", [64, 2, 192], F32)
                MF = self.sb(sc, "MF", [128, 2, 128], F32)
                RST = self.sb(sc, "RST", [128, 512], F32)
                II2 = self.sb(sc, "II2", [128, 64], F32)
                BD = self.sb(sc, "BD", [128, 128], F32)
                self.dsp([MU], [d["rw_muT"]], MU[:, :, :], d["rw_muT"].h[:, :, :])
                for dr in range(2):
                    self.dsp([M1], [d["rw_mask1"]], M1[:, dr, :], d["rw_mask1"].h[dr])
                    self.dsp([M2], [d["rw_mask2"]], M2[:, dr, :], d["rw_mask2"].h[dr])
                    self.dsp([MF], [d["rw_maskf"]], MF[:, dr, :], d["rw_maskf"].h[dr])
                self.dsp([RST], [d["rw_reset"]], RST[:, :], d["rw_reset"].h[:, :])
                self.dsp([II2], [d["rw_ii2"]], II2[:, :], d["rw_ii2"].h[:, :])
                self.dsp([BD], [d["rw_bd"]], BD[:, :], d["rw_bd"].h[:, :])
                W1 = self.sb(sc, "W1r", [128, 8, 2, 2, 64], BF16)
                W1s = self.sb(sc, "W1s", [128, 8, 2, 2, 64], BF16)
                for dr in range(2):
                    self.dpool([W1], [d["rw_w1"]], W1[:, :, 0, dr, :], d["rw_w1"].h[dr].rearrange("(k p) r -> p k r", p=128))
                    self.dpool([W1], [d["rw_a1"]], W1[:, :, 1, dr, :], d["rw_a1"].h[dr].rearrange("(k p) r -> p k r", p=128))
                for k in range(KC):
                    self.act(W1s, W1s[:, k, 0, :, :], W1, W1[:, k, 0, :, :], AF.Identity, scale=MU[:, 1, k:k + 1], extra_reads=[MU])
                    self.act(W1s, W1s[:, k, 1, :, :], W1, W1[:, k, 1, :, :], AF.Identity, scale=MU[:, 4, k:k + 1], extra_reads=[MU])
                ctx = dict(i=i, MU=MU, M1=M1, M2=M2, MF=MF, RST=RST, II2=II2, BD=BD, W1=W1, W1s=W1s, CDEC=CDEC)
                with self.scope() as sg:
                    W2c = [self.sb(sg, "W2p%d" % t, [64, 2, 2, 128], BF16) for t in range(2)]
                    VEC = self.sb(sg, "VECp", [128, 8, 9], F32)
                    self.dsp([VEC], [d["rw_vec"]], VEC[:, :, :], d["rw_vec"].h[:, :, :])
                    HT = self.sb(sg, "HTp", [128, 8, 2, SEQ], BF16)
                    DT = self.sb(sg, "DTp", [128, 8, 2, SEQ], BF16)
                    LD = self.sb(sg, "LDp", [64, 2, 2, TP], BF16)
                    for s_ in range(2):
                        self.rw_make_hd(ctx, sg, 0, self.XP, s_ * SEQ, SEQ, 0, SEQ, HT[:, :, s_, :], DT[:, :, s_, :], HT, DT, "p%d" % s_)
                        self.rw_lora_down(ctx, HT, DT, lambda k, s_=s_: HT[:, k, s_, :], lambda k, s_=s_: DT[:, k, s_, :], SEQ,
                                          LD, lambda wa, dr, s_=s_: LD[:, wa, dr, s_ * SEQ:(s_ + 1) * SEQ], (0, 1))
                    WBs = [self.sb(sg, "WBr%d" % t, [128, 8, 512], BF16) for t in range(2)]
                    WSs = [self.sb(sg, "WSr0", [128, 8, 512], BF16)]
                    SST = self.sb(sg, "SSTp", [64, 4, 64], F32)
                    import os
                    for c in range(int(os.environ.get("RW_MAXC", "8"))):
                        wb, ws = WBs[c % 2], WSs[0]
                        self.dpool([wb], [d["rw_win"]], wb[:, :, :],
                                   d["rw_win"].h.ap().rearrange("(k p) e -> p k e", p=128)[:, :, c * 512:(c + 1) * 512])
                        self.rw_scale_w(ctx, wb, ws)
                        W2 = W2c[c % 2]
                        for dr in range(2):
                            self.dpool([W2], [d["rw_w2"]], W2[:, 0, dr, :], d["rw_w2"].h[dr][:, c * 128:(c + 1) * 128])
                            self.dpool([W2], [d["rw_a2"]], W2[:, 1, dr, :], d["rw_a2"].h[dr][:, c * 128:(c + 1) * 128])
                        for s_ in range(2):
                            self.memset(SST, SST[:, :, :], 0.0)
                            self.rw_pass(ctx, 0, NT=SEQ, hT=lambda k, s_=s_: HT[:, k, s_, :], dT=lambda k, s_=s_: DT[:, k, s_, :],
                                         HTt=HT, DTt=DT, wb=wb, ws=ws, wc0=0,
                                         LD=LD, ld=lambda wa, dr, s_=s_: LD[:, wa, dr, s_ * SEQ:(s_ + 1) * SEQ],
                                         W2=W2, w2=lambda wa, dr, W2=W2: W2[:, wa, dr, :],
                                         VEC=VEC, vec=VEC[:, c, :], dirs=(0, 1), final=True, first_block=True,
                                         SST=SST, YF=None, yf=None, OUT=OP, out=OP[:, c, s_ * SEQ:(s_ + 1) * SEQ],
                                         st_out=(s_, c), tag="p")
                self.end_block("rwkv_prompt")
                OS = self.sb(sc, "OSr", [128, 2, TS], BF16)
                with self.scope() as sg:
                    W2 = self.sb(sg, "W2s", [64, 2, 2, 256], BF16)
                    for dr in range(2):
                        self.dpool([W2], [d["rw_w2_own"]], W2[:, 0, dr, :], d["rw_w2_own"].h[dr])
                        self.dpool([W2], [d["rw_a2_own"]], W2[:, 1, dr, :], d["rw_a2_own"].h[dr])
                    VEC = self.sb(sg, "VECs", [128, 2, 9], F32)
                    self.dsp([VEC], [d["rw_vec_own"]], VEC[:, :, :], d["rw_vec_own"].h[:, :, :])
                    WB = self.sb(sg, "WBo", [128, 8, 1024], BF16)
                    WS = self.sb(sg, "WSo", [128, 8, 1024], BF16)
                    for cb in range(2):
                        self.dpool([WB], [d["rw_win_own"]], WB[:, :, cb * 512:(cb + 1) * 512],
                                   d["rw_win_own"].h.ap().rearrange("(k p) e -> p k e", p=128)[:, :, cb * 512:(cb + 1) * 512])
                    for cb in range(2):
                        self.rw_scale_w(ctx, WB, WS, cb * 512)
                    SSTs = self.sb(sg, "SSTs", [64, 2, 2, 2, 64], F32)
                    for jc in range(2):
                        for dr in range(2):
                            for hl in range(2):
                                self.dsp([SSTs], [d["rw_state"]], SSTs[:, jc, dr, hl, :], d["rw_state"].h[dr, jc * 2 + hl])
                    YF = self.scratch("yf_scr%d" % i, [2, 128, TS], F32)
                    HT = self.sb(sg, "HTs", [128, 8, SEQ], BF16)
                    DT = self.sb(sg, "DTs", [128, 8, SEQ], BF16)
                    LD = self.sb(sg, "LDs", [64, 2, 2, SEQ], BF16)
                    for dr in range(2 if os.environ.get("RW_NOSAMPLE", "0") == "0" else 0):
                        order = range(8) if dr == 0 else range(7, -1, -1)
                        for tb in order:
                            self.rw_make_hd(ctx, sg, 1, self.XS, 0, TS, tb * SEQ, SEQ, HT[:, :, :], DT[:, :, :], HT, DT, "s")
                            self.rw_lora_down(ctx, HT, DT, lambda k: HT[:, k, :], lambda k: DT[:, k, :], SEQ,
                                              LD, lambda wa, dr_: LD[:, wa, dr_, :], (0, 1) if dr == 1 else (0,))
                            for jc in range(2):
                                self.rw_pass(ctx, 1, NT=SEQ, hT=lambda k: HT[:, k, :], dT=lambda k: DT[:, k, :], HTt=HT, DTt=DT,
                                             wb=WB, ws=WS, wc0=jc * 512, LD=LD, ld=lambda wa, dr_: LD[:, wa, dr_, :],
                                             W2=W2, w2=lambda wa, dr_, jc=jc: W2[:, wa, dr_, jc * 128:(jc + 1) * 128],
                                             VEC=VEC, vec=VEC[:, jc, :], dirs=(dr,), final=(dr == 1), first_block=False,
                                             SST=SSTs, sst=lambda dr_, hl, jc=jc: SSTs[:, jc, dr_, hl, :],
                                             YF=YF, yf=YF.h[jc][:, tb * SEQ:(tb + 1) * SEQ], OUT=OS,
                                             out=OS[:, jc, tb * SEQ:(tb + 1) * SEQ], st_out=None, tag="s")
                if not os.environ.get("RW_ONLY") or os.environ.get("RW_GATHER"):
                    self.gather_and_outproj(i, sc, (d["rw_wout"], d["rw_wout"].h[j]), OP, OS)
            self.end_block("rwkv%d" % i)

    def rw_scale_w(self, ctx, wb, ws, c0=0):
        MU = ctx["MU"]
        muidx = (0, 2, 3, 5)
        n = 0
        for k in range(KC):
            for p4 in range(4):
                cs = slice(c0 + p4 * 128, c0 + (p4 + 1) * 128)
                if n % 2 == 0:
                    self.act(ws, ws[:, k, cs], wb, wb[:, k, cs], AF.Identity, scale=MU[:, muidx[p4], k:k + 1], extra_reads=[MU])
                else:
                    self.ts(ws, ws[:, k, cs], wb, wb[:, k, cs], MU[:, muidx[p4], k:k + 1], None, ALU.mult, extra_reads=[MU])
                n += 1

    def rw_make_hd(self, ctx, sc, n, X, seq0, seqlen, t0, NT, ht, dt, HTt, DTt, tag):
        i = ctx["i"]
        with self.scope() as s2:
            HF = [self.sb(s2, "HF%d" % t, [128, NT + 2], F32) for t in range(2)]
            TT = [self.sb(s2, "TTh%d" % t, [128, NT], F32) for t in range(2)]
            lo = max(t0 - 1, 0)
            hi = min(t0 + NT + 1, seqlen)
            off = lo - (t0 - 1)
            for k in range(KC):
                hf, tt_ = HF[k % 2], TT[k % 2]
                if off > 0:
                    self.memset(hf, hf[:, 0:1], 0.0)
                if hi < t0 + NT + 1:
                    self.memset(hf, hf[:, NT + 1:NT + 2], 0.0)
                self.act(hf, hf[:, off:off + hi - lo], X, X[:, k, seq0 + lo:seq0 + hi], AF.Identity,
                         scale=self.MOD[:, i, 8 + k, n:n + 1], bias=self.MOD[:, i, k, n:n + 1], extra_reads=[self.MOD])
                self.tt(tt_, tt_[:, :], hf, hf[:, 0:NT], hf, hf[:, 2:NT + 2], ALU.add)
                self.stt(DTt, dt[:, k, :], tt_, tt_[:, :], 0.5, hf, hf[:, 1:NT + 1], ALU.mult, ALU.subtract)
                self.act(HTt, ht[:, k, :], hf, hf[:, 1:NT + 1], AF.Copy)

    def rw_lora_down(self, ctx, HTt, DTt, hT, dT, NT, LD, ld, dirs):
        W1, W1s = ctx["W1"], ctx["W1s"]
        for wa in range(2):
            for dr in dirs:
                bank = self.ps()
                for k in range(KC):
                    self.P(bank, bank[0:64, 0:NT], W1[:, k, wa, dr, :], hT(k), k == 0, False, [W1, HTt])
                    self.P(bank, bank[0:64, 0:NT], W1s[:, k, wa, dr, :], dT(k), False, k == KC - 1, [W1s, DTt])
                self.act(LD, ld(wa, dr), bank, bank[0:64, 0:NT], AF.Tanh if wa == 0 else AF.Copy)

    def rw_pass(self, ctx, n, NT, hT, dT, HTt, DTt, wb, ws, wc0, LD, ld, W2, w2, VEC, vec, dirs, final, first_block,
                SST, YF, yf, OUT, out, st_out, tag, sst=None):
        d = self.din
        i = ctx["i"]
        CDEC = ctx["CDEC"]
        M1, M2, MF, RST, II2, BD = ctx["M1"], ctx["M2"], ctx["MF"], ctx["RST"], ctx["II2"], ctx["BD"]
        NCH = NT // 64
        NU = 2 * NCH
        ND = len(dirs)
        import os
        DBG = int(os.environ.get("RW_DBG", "9"))
        c3 = lambda t_: t_[:, :].rearrange("p (c v) -> p c v", v=64)
        with self.scope() as sp_:
            f32t = lambda nm, shp=None, dt=F32: self.sb(sp_, nm + tag, shp or [128, NT], dt)
            R, K, V, KK = f32t("R"), f32t("K"), f32t("V"), f32t("KK")
            TMP, TM2, TM3 = f32t("TMP"), f32t("TM2"), f32t("TM3")
            SG, AS, CS = f32t("SG"), f32t("AS"), f32t("CS")
            OMK = f32t("OMK", [128, 1])
            TOT = f32t("TOT", [128, NCH, 1])
            ET_ = f32t("ETOT", [128, NCH, 1])
            XL = f32t("XL", [128, NCH, 192], F32R)
            XR = f32t("XR", [128, NCH, 192], F32R)
            GS = []
            for g_ in range(2):
                GS.append((f32t("XMN%d" % g_, [64, 4, 256], F32R), f32t("LT%d" % g_, [64, 4, 192], F32R)))
            STK = f32t("STK", [128, ND, NU, 64], F32R)
            YD = f32t("YD", [128, ND, NT])
            FULL = f32t("FULL", [128, ND, NU, 128], F32R)
            ATOT = f32t("ATOT") if final else None
            ZS = self.sb(sp_, "ZS" + tag, [128, NT], BF16) if final else None
            dests = [R, K, V] + ([ZS] if final else [])
            for p4, dst in enumerate(dests):
                bank = self.ps()
                cs = slice(wc0 + p4 * 128, wc0 + (p4 + 1) * 128)
                for k in range(KC):
                    self.P(bank, bank[:, 0:NT], wb[:, k, cs], hT(k), k == 0, False, [wb, HTt])
                    self.P(bank, bank[:, 0:NT], ws[:, k, cs], dT(k), False, k == KC - 1, [ws, DTt])
                self.act(dst, dst[:, :], bank, bank[:, 0:NT], AF.Silu if p4 == 3 else AF.Copy)
            self.act(TMP, TMP[:, :], K, K[:, :], AF.Identity, scale=vec[:, 4:5], extra_reads=[VEC])
            self.act(TM2, TM2[:, :], TMP, TMP[:, :], AF.Square)
            bank = self.ps()
            self.P(bank, bank[:, 0:NT], BD[:, :], TM2[:, :], True, True, [BD, TM2])
            self.act(TM2, TM2[:, :], bank, bank[:, 0:NT], AF.Sqrt, scale=1.0, bias=self.EPS[:, 3:4], extra_reads=[self.EPS])
            self.recip(TM2, TM2[:, :], TM2, TM2[:, :])
            self.tt(KK, KK[:, :], TMP, TMP[:, :], TM2, TM2[:, :], ALU.mult)
            self.ts(OMK, OMK[:, 0:1], VEC, vec[:, 5:6], -1.0, 1.0, ALU.mult, ALU.add)
            if DBG < 1:
                return
            for hl in range(2):
                hs = slice(hl * 64, (hl + 1) * 64)
                bank = self.ps()
                for ch in range(NCH):
                    self.P(bank, bank[0:64, ch * 64:(ch + 1) * 64], V[hs, ch * 64:(ch + 1) * 64], II2[hs, :], True, True, [V, II2])
                for di in range(ND):
                    self.act(STK, STK[64:128, di, hl * NCH:(hl + 1) * NCH, :], bank,
                             bank[0:64, 0:NCH * 64].rearrange("p (c v) -> p c v", v=64), AF.Copy)
            for dr in ((0, 1) if final else dirs):
                chain = dr in dirs
                bank = self.ps()
                self.P(bank, bank[:, 0:NT], w2(1, dr), ld(1, dr), True, True, [W2, LD])
                self.act(AS, AS[:, :], bank, bank[:, 0:NT], AF.Sigmoid, bias=vec[:, 2 + dr:3 + dr], extra_reads=[VEC])
                self.act(TMP, TMP[:, :], AS, AS[:, :], AF.Identity, scale=vec[:, 5:6], bias=OMK[:, 0:1], extra_reads=[VEC, OMK])
                if final:
                    if dr == 0:
                        self.cp(ATOT, ATOT[:, :], TMP, TMP[:, :])
                    else:
                        self.tt(ATOT, ATOT[:, :], ATOT, ATOT[:, :], TMP, TMP[:, :], ALU.add)
                if not chain:
                    continue
                di = dirs.index(dr)
                self.tt(XL, XL[:, :, 64:128], K, c3(K), TMP, c3(TMP), ALU.mult)
                self.tt(XL, XL[:, :, 128:192], KK, c3(KK), AS, c3(AS), ALU.mult)
                bank = self.ps()
                self.P(bank, bank[:, 0:NT], w2(0, dr), ld(0, dr), True, True, [W2, LD])
                self.act(SG, SG[:, :], bank, bank[:, 0:NT], AF.Sigmoid, bias=vec[:, dr:dr + 1], extra_reads=[VEC])
                self.V([CS], [RST, SG], lambda e, o=CS[:, :], a=RST[:, 0:NT], b=SG[:, :]: e.tensor_tensor_scan(
                    out=o, data0=a, data1=b, initial=0.0, op0=ALU.mult, op1=ALU.add))
                self.cp(TOT, TOT[:, :, :], CS, c3(CS)[:, :, 63:64])
                if dr == 1:
                    self.tt(TM2, TM2[:, :], SG, SG[:, :], CS, CS[:, :], ALU.subtract)
                    self.tt(CS, c3(CS), TM2, c3(TM2), TOT, TOT[:, :, :].to_broadcast([128, NCH, 64]), ALU.add)
                self.tt(TM2, TM2[:, :], CS, CS[:, :], SG, SG[:, :], ALU.subtract)
                self.act(TM3, TM3[:, :], TM2, TM2[:, :], AF.Exp, scale=-CDEC)
                self.stt(XR, XR[:, :, 128:192], KK, c3(KK), -1.0, TM3, c3(TM3), ALU.mult, ALU.mult)
                self.act(TM3, TM3[:, :], CS, CS[:, :], AF.Exp, scale=-CDEC)
                self.tt(XR, XR[:, :, 64:128], R, c3(R), TM3, c3(TM3), ALU.mult)
                self.act(TM3, TM3[:, :], CS, CS[:, :], AF.Exp, scale=CDEC)
                self.tt(XL, XL[:, :, 64:128], XL, XL[:, :, 64:128], TM3, c3(TM3), ALU.mult)
                self.tt(XL, XL[:, :, 128:192], XL, XL[:, :, 128:192], TM3, c3(TM3), ALU.mult)
                self.act(ET_, ET_[:, :, :], TOT, TOT[:, :, :], AF.Exp, scale=-CDEC)
                self.cp(XL, XL[:, :, 0:64], II2, II2[:, :].unsqueeze(1).to_broadcast([128, NCH, 64]))
                self.tt(XR, XR[:, :, 0:64], II2, II2[:, :].unsqueeze(1).to_broadcast([128, NCH, 64]),
                        ET_, ET_[:, :, :].to_broadcast([128, NCH, 64]), ALU.mult)
                if DBG < 2:
                    continue
                units = [(hl, ch) for hl in range(2) for ch in range(NCH)]
                groups = [units[g0:g0 + 4] for g0 in range(0, NU, 4)]
                ngl = len(GS)
                for gi, grp in enumerate(groups):
                    XMN, LTt = GS[gi % ngl]
                    b1 = [self.ps(hold=True), self.ps(hold=True)]
                    b2 = [self.ps(hold=True), self.ps(hold=True)]
                    for ui, (hl, ch) in enumerate(grp):
                        hs = slice(hl * 64, (hl + 1) * 64)
                        o1 = b1[ui // 2][0:64, (ui % 2) * 192:(ui % 2) * 192 + 192]
                        o2 = b2[ui // 2][0:64, (ui % 2) * 192:(ui % 2) * 192 + 192]
                        self.P(b1[ui // 2], o1, XL[hs, ch, 128:192], XR[hs, ch, 0:192], True, True, [XL, XR])
                        self.P(b2[ui // 2], o2, XR[hs, ch, 128:192], XL[hs, ch, 0:192], True, True, [XL, XR])
                    for hb_ in range(2):
                        self.tt(XMN, XMN[:, hb_ * 2:hb_ * 2 + 2, 0:192], b1[hb_], b1[hb_][0:64, 0:384].rearrange("p (u v) -> p u v", v=192),
                                M1, M1[:, dr:dr + 1, :].to_broadcast([64, 2, 192]), ALU.mult)
                        self.tt(LTt, LTt[:, hb_ * 2:hb_ * 2 + 2, :], b2[hb_], b2[hb_][0:64, 0:384].rearrange("p (u v) -> p u v", v=192),
                                M2, M2[:, dr:dr + 1, :].to_broadcast([64, 2, 192]), ALU.mult)
                    for b_ in b1 + b2:
                        self.release(b_)
                    self.act(XMN, XMN[:, :, 192:256], LTt, LTt[:, :, 128:192], AF.Copy)
                for gbase in range(0, len(groups), ngl):
                    gsel = list(range(gbase, min(gbase + ngl, len(groups))))
                    for lev in range(6):
                        banks = {}
                        for gi in gsel:
                            XMN, LTt = GS[gi % ngl]
                            bk = [self.ps(), self.ps()]
                            for ui in range(4):
                                bb_ = bk[ui // 2]
                                c0 = (ui % 2) * 256
                                if lev < 5:
                                    self.P(bb_, bb_[0:64, c0:c0 + 192], XMN[:, ui, 192:256], XMN[:, ui, 0:192], True, True, [XMN], fast=True)
                                    self.P(bb_, bb_[0:64, c0 + 192:c0 + 256], XMN[:, ui, 128:192], XMN[:, ui, 192:256], True, True, [XMN], fast=True)
                                else:
                                    self.P(bb_, bb_[0:64, c0:c0 + 128], XMN[:, ui, 192:256], XMN[:, ui, 0:128], True, True, [XMN], fast=True)
                            banks[gi] = bk
                        for gi in gsel:
                            XMN, LTt = GS[gi % ngl]
                            for hb_ in range(2):
                                src = banks[gi][hb_][0:64, :].rearrange("p (u v) -> p u v", v=256)
                                if lev < 5:
                                    self.act(XMN, XMN[:, hb_ * 2:hb_ * 2 + 2, 128:256], banks[gi][hb_], src[:, :, 128:256], AF.Copy)
                                self.tt(XMN, XMN[:, hb_ * 2:hb_ * 2 + 2, 0:128], XMN, XMN[:, hb_ * 2:hb_ * 2 + 2, 0:128],
                                        banks[gi][hb_], src[:, :, 0:128], ALU.add)
                    for gi in gsel:
                        XMN, LTt = GS[gi % ngl]
                        grp = groups[gi]
                        fbank = self.ps()
                        for ui, (hl, ch) in enumerate(grp):
                            hs = slice(hl * 64, (hl + 1) * 64)
                            self.P(fbank, fbank[:, ui * 128:(ui + 1) * 128], XL[hs, ch, 0:128], XR[hs, ch, 0:128], True, False, [XL, XR])
                            self.P(fbank, fbank[:, ui * 128:(ui + 1) * 128], LTt[:, ui, 0:128], XMN[:, ui, 0:128], False, True, [LTt, XMN],
                                   serial=True)
                        u0 = grp[0][0] * NCH + grp[0][1]
                        self.tt(FULL, FULL[:, di, u0:u0 + 4, :], fbank, fbank[:, :].rearrange("p (u v) -> p u v", v=128),
                                MF, MF[:, dr:dr + 1, :].to_broadcast([128, 4, 128]), ALU.mult)
            for di, dr in enumerate(dirs):
                for hl in range(2):
                    u_first = hl * NCH + (0 if dr == 0 else NCH - 1)
                    src = SST[:, hl * 2 + dr, :] if sst is None else sst(dr, hl)
                    self.cp(STK, STK[0:64, di, u_first, :], SST, src)
            for step in range(NCH):
                for di, dr in enumerate(dirs):
                    ch = step if dr == 0 else NCH - 1 - step
                    nxt = ch + 1 if dr == 0 else ch - 1
                    for hl in range(2):
                        u = hl * NCH + ch
                        hs = slice(hl * 64, (hl + 1) * 64)
                        cbank = self.ps()
                        self.P(cbank, cbank[0:64, 0:64], FULL[:, di, u, 0:64], STK[:, di, u, :], True, True, [FULL, STK], fast=True)
                        self.P(cbank, cbank[0:64, 64:128], STK[:, di, u, :], FULL[:, di, u, 64:128], True, True, [FULL, STK], fast=True)
                        if step < NCH - 1:
                            self.cp(STK, STK[0:64, di, hl * NCH + nxt, :], cbank, cbank[0:64, 0:64])
                        else:
                            dst = SST[:, hl * 2 + dr, :] if sst is None else sst(dr, hl)
                            self.cp(SST, dst, cbank, cbank[0:64, 0:64])
                        self.act(YD, YD[hs, di, ch * 64:(ch + 1) * 64], cbank, cbank[0:64, 64:128], AF.Copy)
            if DBG < 4:
                return
            if st_out is not None:
                s_, c = st_out
                o = self.dout["st_out"]
                for dr in range(2):
                    for hl in range(2):
                        self.dsp([o], [SST], o.h[s_, dr, c * 2 + hl], SST[:, hl * 2 + dr, :])
            if not final:
                self.dsp([YF], [YD], yf, YD[:, 0, :])
                return
            Y, MN_, VR = SG, AS, CS
            if ND == 2:
                self.tt(Y, Y[:, :], YD, YD[:, 0, :], YD, YD[:, 1, :], ALU.add)
            else:
                self.dsp([TM3], [YF], TM3[:, :], yf)
                self.tt(Y, Y[:, :], TM3, TM3[:, :], YD, YD[:, 0, :], ALU.add)
            b1 = self.ps()
            self.P(b1, b1[:, 0:NT], BD[:, :], Y[:, :], True, True, [BD, Y])
            self.act(TM2, TM2[:, :], Y, Y[:, :], AF.Square)
            b2 = self.ps()
            self.P(b2, b2[:, 0:NT], BD[:, :], TM2[:, :], True, True, [BD, TM2])
            self.act(MN_, MN_[:, :], b1, b1[:, 0:NT], AF.Identity, scale=1.0 / 64)
            self.tt(VR, VR[:, :], MN_, MN_[:, :], MN_, MN_[:, :], ALU.mult)
            self.stt(VR, VR[:, :], b2, b2[:, 0:NT], 1.0 / 64, VR, VR[:, :], ALU.mult, ALU.subtract)
            self.act(VR, VR[:, :], VR, VR[:, :], AF.Sqrt, scale=1.0, bias=self.EPS[:, 2:3], extra_reads=[self.EPS])
            self.recip(VR, VR[:, :], VR, VR[:, :])
            self.tt(Y, Y[:, :], Y, Y[:, :], MN_, MN_[:, :], ALU.subtract)
            self.tt(Y, Y[:, :], Y, Y[:, :], VR, VR[:, :], ALU.mult)
            self.act(Y, Y[:, :], Y, Y[:, :], AF.Identity, scale=vec[:, 7:8], bias=vec[:, 8:9], extra_reads=[VEC])
            self.tt(TMP, TMP[:, :], R, R[:, :], K, K[:, :], ALU.mult)
            self.stt(TMP, TMP[:, :], TMP, TMP[:, :], vec[:, 6:7], ATOT, ATOT[:, :], ALU.mult, ALU.mult, extra_reads=[VEC])
            b3 = self.ps()
            self.P(b3, b3[:, 0:NT], BD[:, :], TMP[:, :], True, True, [BD, TMP])
            self.tt(TMP, TMP[:, :], b3, b3[:, 0:NT], V, V[:, :], ALU.mult)
            self.tt(Y, Y[:, :], Y, Y[:, :], TMP, TMP[:, :], ALU.add)
            self.tt(OUT, out, Y, Y[:, :], ZS, ZS[:, :], ALU.mult)

    def epilogue(self):
        o = self.dout
        self.dsp([o["yp"]], [self.XP], o["yp"].h.ap().rearrange("(k p) t -> p k t", p=128), self.XP[:, :, :])
        for k in range(KC):
            self.dsp([o["ys"]], [self.XS], o["ys"].h[k * 128:(k + 1) * 128, :], self.XS[:, k, :])
        self.end_block("epilogue")


_PROG_CACHE = {}


def _get_prog(nl=DEPTH):
    if nl not in _PROG_CACHE:
        p = Prog(nl)
        p.build()
        _PROG_CACHE[nl] = p
    return _PROG_CACHE[nl]


def run_device(inputs, nl=DEPTH):
    prog = _get_prog(nl)
    maps = prep_inputs(inputs)
    names = set(prog.din.keys())
    maps = [{k_: v for k_, v in m.items() if k_ in names} for m in maps]
    res = run_bass_kernel_spmd(prog.nc, maps, core_ids=list(range(8)))
    return res.results


def assemble(results):
    yp = np.zeros((16, SEQ, D), np.float32)
    ys = np.zeros((2, TS, D), np.float32)
    for c in range(8):
        r = results[c]
        b, q = c // 4, c % 4
        yp[2 * c:2 * c + 2] = np.asarray(r["yp"]).T.reshape(2, SEQ, D)
        ys[b, q * 512:(q + 1) * 512] = np.asarray(r["ys"])[:, q * 512:(q + 1) * 512].T
    return yp, ys


def assemble_mla(results):
    ckv = np.zeros((16, 1, SEQ, 128), np.float32)
    kpe = np.zeros((16, 1, SEQ, 32), np.float32)
    for c in range(8):
        r = results[c]
        ckv[2 * c:2 * c + 2, 0] = np.asarray(r["ckv_out"]).T.reshape(2, SEQ, 128)
        kpe[2 * c:2 * c + 2, 0] = np.asarray(r["kpe_out"]).T.reshape(2, SEQ, 32)
    return ckv, kpe


def assemble_state(results):
    st = np.zeros((16, 1, 2, 16, 64, 64), np.float32)
    for c in range(8):
        r = np.asarray(results[c]["st_out"])
        st[2 * c:2 * c + 2, 0] = r.transpose(0, 1, 2, 4, 3)
    return st


def kernel(**inputs):
    results = run_device(inputs)
    yp, ys = assemble(results)
    ckv, kpe = assemble_mla(results)
    st = assemble_state(results)
    return yp, ys, ckv, kpe, st
```

```python
import numpy as np
import ml_dtypes
from contextlib import ExitStack
import concourse.bass as bass
import concourse.mybir as mybir
from concourse.bass_utils import run_bass_kernel_spmd

F32 = mybir.dt.float32
BF16 = mybir.dt.bfloat16
I32 = mybir.dt.int32
F32R = mybir.dt.float32r
AF = mybir.ActivationFunctionType
ALU = mybir.AluOpType
NPBF = ml_dtypes.bfloat16

D = 1024
KC = 8
SEQ = 256
TP = 512
TS = 2048
DEPTH = 4
ALPHA = (2.0 * DEPTH) ** 0.25
LN_EPS = 1e-5
RMS_EPS = 1e-6
TWO_PI = float(2.0 * np.pi)
SEM_ROT = 3000


class Tile:
    __slots__ = ("h", "name", "w", "r", "dsem", "dval", "excl", "lastrg")

    def __init__(self, h, name=""):
        self.h = h
        self.name = name
        self.excl = False
        self.lastrg = None
        self.w = []
        self.r = []
        self.dsem = None
        self.dval = 0

    def __getitem__(self, idx):
        return self.h[idx]


class Eng:
    def __init__(self, name):
        self.name = name
        self.ops = []
        self.n = 0
        self.seen = {}
        self.maxblk = -1


class KB:
    NPOOL = 110

    def __init__(self, nc):
        self.nc = nc
        self.tiles = []
        self.blockno = 0
        self.E = {n: Eng(n) for n in ("pe", "act", "dve", "pool", "sp")}
        self.dma_latest = {}
        self.pool_vals = [0] * self.NPOOL
        self.pool_next = 0
        self.ncoll = 0
        self.semh = {}
        self.semstack = None

    def track(self, t):
        self.tiles.append(t)
        return t

    def _engsem(self, e, n):
        return ("E", e.name, n // SEM_ROT), (n % SEM_ROT) + 1

    def _deps(self, e, reads, writes, acc, skip_self=False):
        deps = {}

        def add(tok):
            s, v = tok
            if skip_self and s[0] == "E" and s[1] == e.name:
                return
            if s[0] == "D":
                v = max(v, self.dma_latest.get(s, v))
            if deps.get(s, 0) < v:
                deps[s] = v
        for t in reads:
            for tok in t.w:
                add(tok)
            if t.excl:
                for tok in t.r:
                    add(tok)
        if not acc:
            for t in writes:
                for tok in t.w:
                    add(tok)
                for tok in t.r:
                    add(tok)
        waits = []
        for s, v in deps.items():
            if e.seen.get(s, 0) >= v:
                continue
            if s[0] == "E":
                mb = e.seen.get(("MB", s[1]), -1)
                if mb > s[2]:
                    continue
                if s[2] > mb:
                    e.seen[("MB", s[1])] = s[2]
            e.seen[s] = v
            waits.append((s, v))
        return waits

    def _commit(self, tok, reads, writes, acc):
        for t in reads:
            t.r.append(tok)
        for t in writes:
            if acc:
                t.w.append(tok)
            else:
                t.w = [tok]
                t.r = []

    def op(self, eng, emit, reads=(), writes=(), acc=False, skip_self=False):
        e = self.E[eng]
        waits = self._deps(e, reads, writes, acc, skip_self)
        tok = self._engsem(e, e.n)
        e.n += 1
        e.ops.append((waits, emit, tok, 1))
        self._commit(tok, reads, writes, acc)

    def dma(self, eng, emit, reads=(), writes=()):
        e = self.E[eng]
        waits = self._deps(e, reads, writes, False)
        st = writes[0] if writes else reads[0]
        if st.dsem is None:
            assert self.pool_next < self.NPOOL, "out of DMA semaphores in this block"
            idx = self.pool_next
            self.pool_next += 1
            st.dsem = ("D", idx)
            st.dval = self.pool_vals[idx]
        st.dval += 16
        self.pool_vals[st.dsem[1]] = st.dval
        tok = (st.dsem, st.dval)
        self.dma_latest[st.dsem] = st.dval
        e.ops.append((waits, emit, tok, 16))
        self._commit(tok, reads, writes, False)

    def special(self, eng, emit, reads=(), writes=()):
        e = self.E[eng]
        waits = self._deps(e, reads, writes, False)
        tok = (("C", self.ncoll), 1)
        self.ncoll += 1
        e.ops.append((waits, emit, tok, 1))
        self._commit(tok, reads, writes, False)

    def end_block(self):
        nc = self.nc
        sp = self.E["sp"]
        waits = []
        for s, v in self.dma_latest.items():
            if sp.seen.get(s, 0) < v:
                sp.seen[s] = v
                waits.append((s, v))
        for t in self.tiles:
            for tok in t.w + t.r:
                if tok[0][0] == "C" and sp.seen.get(tok[0], 0) < 1:
                    sp.seen[tok[0]] = 1
                    waits.append(tok)
        sp.ops.append((waits, None, None, 0))
        keys = []
        for e in self.E.values():
            for waits_, emit, tok, inc in e.ops:
                for s, v in waits_:
                    if s not in keys:
                        keys.append(s)
                if tok is not None and tok[0] not in keys:
                    keys.append(tok[0])
        semh = self.semh
        for kk in keys:
            if kk not in semh:
                semh[kk] = self.semstack.enter_context(nc.semaphore("sem%d" % len(semh)))
        with nc.Block() as block:
            def run(e):
                def body(h):
                    for waits_, emit, tok, inc in e.ops:
                        for s, v in waits_:
                            h.wait_ge(semh[s], v)
                        if emit is None:
                            continue
                        ins = emit(h)
                        if tok is not None:
                            ins.then_inc(semh[tok[0]], inc)
                return body
            block.tensor(run(self.E["pe"]))
            block.scalar(run(self.E["act"]))
            block.vector(run(self.E["dve"]))
            block.gpsimd(run(self.E["pool"]))
            block.sync(run(self.E["sp"]))
        nops = {n: len(e.ops) for n, e in self.E.items()}
        for t in self.tiles:
            t.w = []
            t.r = []
            t.dsem = None
            t.dval = 0
        for e in self.E.values():
            e.ops = []
        self.pool_next = 0
        self.blockno += 1
        return nops, len(semh)


def _dft_tables(L):
    n2 = 2 * L
    l = np.arange(L, dtype=np.float64)
    ang = 2.0 * np.pi * np.outer(l, l) / n2
    C = np.cos(ang)
    S = np.sin(ang)
    wf = np.full(L, 2.0 / n2)
    wf[0] = 1.0 / n2
    Gc = wf[:, None] * np.cos(ang)
    Gs = (2.0 / n2) * np.sin(ang)
    nyq = (-1.0) ** l
    return C, S, Gc, Gs, nyq, nyq / n2


def _feat_tables(L):
    pos = np.arange(L, dtype=np.float32)
    t = np.linspace(0.0, 1.0, L, dtype=np.float32)
    bands = np.linspace(1e-4, 16 - 1, 16, dtype=np.float32)
    ang = (np.float32(2.0 * np.pi / L) * pos[:, None] * bands[None, :]).astype(np.float32)
    feat = np.concatenate([t[:, None], np.cos(ang), -np.sin(ang)], -1).astype(np.float32)
    return np.ascontiguousarray(feat.T), t


_CONST = {}


def _constants():
    if _CONST:
        return _CONST
    c = {}
    C, S, Gc, Gs, nyq, nyqrow = _dft_tables(TS)
    fw = np.stack([C, S], 0)
    fw = fw.reshape(2, 16, 128, 16, 128)
    c["fwd_s"] = np.ascontiguousarray(fw.transpose(3, 2, 1, 0, 4)).astype(NPBF)
    iv = np.stack([Gc, Gs], 0).reshape(2, 16, 128, 4, 512)
    c["inv_s"] = np.ascontiguousarray(iv.transpose(3, 1, 2, 0, 4)).astype(NPBF)
    c["nyq_s"] = np.ascontiguousarray(nyq.reshape(16, 128).T).astype(NPBF)
    c["nyqrow_s"] = nyqrow.reshape(1, TS).astype(NPBF)
    ft, t = _feat_tables(TS)
    c["feat_s"] = ft
    c["negt_s"] = np.ascontiguousarray((-t).reshape(16, 128).T).astype(np.float32)
    C, S, Gc, Gs, nyq, nyqrow = _dft_tables(SEQ)
    fw = np.stack([C, S], 0).reshape(2, 2, 128, 256)
    c["fwd_p"] = np.ascontiguousarray(fw.transpose(2, 1, 0, 3)).astype(NPBF)
    iv = np.stack([Gc, Gs], 0).reshape(2, 2, 128, 256)
    c["inv_p"] = np.ascontiguousarray(iv.transpose(2, 1, 0, 3)).astype(NPBF)
    c["nyq_p"] = np.ascontiguousarray(nyq.reshape(2, 128).T).astype(NPBF)
    c["nyqrow_p"] = nyqrow.reshape(1, SEQ).astype(NPBF)
    ft, t = _feat_tables(SEQ)
    c["feat_p"] = ft
    c["negt_p"] = np.ascontiguousarray((-t).reshape(2, 128).T).astype(np.float32)
    c["ident"] = np.eye(128, dtype=np.float32)
    _CONST.update(c)
    return _CONST


def _fm(v):
    v = np.asarray(v, np.float32)
    lead = v.shape[:-1]
    n = v.shape[-1] // 128
    return np.ascontiguousarray(np.moveaxis(v.reshape(*lead, n, 128), -1, 0))


def _own_cols(q, nblk, width=1024):
    idx = []
    for jc in range(2):
        for w in range(nblk):
            s = w * width + q * 256 + jc * 128
            idx.extend(range(s, s + 128))
    return np.array(idx)


def _perm_cols(nblk, width=1024):
    idx = []
    for jc in range(8):
        for w in range(nblk):
            s = w * width + jc * 128
            idx.extend(range(s, s + 128))
    return np.array(idx)


def prep_inputs(inp):
    cst = _constants()
    f = lambda a: np.ascontiguousarray(np.asarray(a, np.float32))
    shared = {}
    shared["mod_w"] = f(inp["mod_w"])
    shared["mod_bT"] = _fm(inp["mod_b"])
    shared["ln_gT"] = _fm(inp["ln_g"])
    shared["ln_bT"] = _fm(inp["ln_b"])
    pc = _perm_cols(4)
    shared["hy_win"] = f(np.asarray(inp["hy_w_in"])[:, :, pc])
    cw = np.asarray(inp["hy_conv_w"], np.float32)
    cb = np.asarray(inp["hy_conv_b"], np.float32)
    cv = np.concatenate([cw, cb[:, None, :]], 1)
    cvT = cv.reshape(2, 4, 3, 8, 128).transpose(4, 0, 3, 2, 1)
    shared["hy_cvT"] = f(cvT)
    shared["hy_skipT"] = _fm(inp["hy_skip"])
    shared["hy_w1"] = f(inp["hy_ffn_w1"])
    shared["hy_b12"] = f(np.stack([np.asarray(inp["hy_ffn_b1"]), np.asarray(inp["hy_ffn_b2"])], -1))
    shared["hy_w2"] = f(inp["hy_ffn_w2"])
    w3a = np.concatenate([np.asarray(inp["hy_ffn_w3"]), np.asarray(inp["hy_ffn_b3"])[:, None, :]], 1)
    shared["hy_w3a"] = f(w3a)
    shared["hy_freqT"] = f(np.asarray(inp["hy_freq"]).transpose(0, 2, 1))
    shared["hy_decay"] = f(np.asarray(inp["hy_decay"]).reshape(2, 1, 2048))
    shared["hy_wout"] = f(inp["hy_w_out"])
    for k_ in ("fwd_s", "inv_s", "nyq_s", "nyqrow_s", "feat_s", "negt_s", "fwd_p", "inv_p", "nyq_p",
               "nyqrow_p", "feat_p", "negt_p", "ident"):
        shared[k_] = cst[k_]
    wi = np.asarray(inp["mla_w_in"], np.float32)[0]
    shared["mla_wh"] = f(wi[:, 0:416])
    shared["mla_wz"] = f(wi[:, 416:1440])
    shared["mla_nrm"] = f(np.concatenate([_fm(np.asarray(inp["mla_q_norm"])[0]), _fm(np.asarray(inp["mla_kv_norm"])[0])], 1))
    wq = np.asarray(inp["mla_w_q_up"], np.float32)[0]
    shared["mla_wq"] = f(wq)
    wkv = np.asarray(inp["mla_w_kv_up"], np.float32)[0].reshape(128, 16, 2, 64)
    shared["mla_wk"] = f(wkv[:, :, 0, :].reshape(128, 1024))
    shared["mla_wv"] = f(wkv[:, :, 1, :].reshape(128, 1024))
    shared["mla_wout"] = f(inp["mla_w_out"])
    half = 16
    inv = (10000.0 ** (-np.arange(0, half, 2, dtype=np.float32) / half)).astype(np.float32)
    rr = np.repeat(np.arange(32, dtype=np.float32), 64)
    col = np.tile(np.arange(64, dtype=np.float32), 32)
    ar, ac = rr[:, None] * inv, col[:, None] * inv
    ang = np.concatenate([ar, ar, ac, ac], -1).astype(np.float32)
    cosT, sinT = np.cos(ang).T.astype(np.float32), np.sin(ang).T.astype(np.float32)
    shared["rope_cs32"] = f(np.stack([cosT, sinT], 1))
    cs96 = np.zeros((96, 2, TS), np.float32)
    cs96[0:64, 0] = 1.0
    cs96[64:96, 0] = cosT
    cs96[64:96, 1] = sinT
    shared["rope_cs96"] = cs96
    R = np.zeros((32, 32), np.float32)
    for i_ in range(8):
        R[8 + i_, i_] = -1.0
        R[i_, 8 + i_] = 1.0
        R[24 + i_, 16 + i_] = -1.0
        R[16 + i_, 24 + i_] = 1.0
    shared["rope_R32"] = R
    R96 = np.zeros((96, 96), np.float32)
    R96[64:96, 64:96] = R
    shared["rope_R96"] = R96
    shared["rw_muT"] = _fm(np.asarray(inp["rw_mu"])[0])
    rwin = np.asarray(inp["rw_w_in"], np.float32)[0]
    rwin_cat = np.concatenate([rwin[0], rwin[1], rwin[2], rwin[3]], 1)
    shared["rw_win"] = f(rwin_cat[:, pc])
    shared["rw_w1"] = f(np.asarray(inp["rw_w1"])[0])
    shared["rw_a1"] = f(np.asarray(inp["rw_a1"])[0])
    shared["rw_w2"] = f(np.asarray(inp["rw_w2"])[0])
    shared["rw_a2"] = f(np.asarray(inp["rw_a2"])[0])
    vecs = np.stack([np.asarray(inp["rw_w0"])[0, 0], np.asarray(inp["rw_w0"])[0, 1],
                     np.asarray(inp["rw_a0"])[0, 0], np.asarray(inp["rw_a0"])[0, 1],
                     np.asarray(inp["rw_k_k"])[0], np.asarray(inp["rw_k_a"])[0],
                     np.asarray(inp["rw_r_k"])[0].reshape(-1), np.asarray(inp["rw_gn_g"])[0],
                     np.asarray(inp["rw_gn_b"])[0]], 0).astype(np.float32)
    vecT = np.ascontiguousarray(vecs.reshape(9, 8, 128).transpose(2, 1, 0))
    shared["rw_vec"] = vecT
    shared["rw_wout"] = f(inp["rw_w_out"])
    tau = np.arange(64)
    m1 = np.zeros((2, 64, 192), np.float32)
    m2 = np.zeros((2, 64, 192), np.float32)
    mf = np.ones((2, 128, 128), np.float32)
    for dr in range(2):
        if dr == 0:
            ms = (tau[:, None] < tau[None, :]).astype(np.float32)
            mi = (tau[:, None] <= tau[None, :]).astype(np.float32)
        else:
            ms = (tau[:, None] > tau[None, :]).astype(np.float32)
            mi = (tau[:, None] >= tau[None, :]).astype(np.float32)
        m1[dr, :, 0:64] = 1.0
        m1[dr, :, 64:128] = mi
        m1[dr, :, 128:192] = ms
        m2[dr, :, 0:64] = 1.0
        m2[dr, :, 64:128] = ms.T
        m2[dr, :, 128:192] = ms.T
        mf[dr, 64:128, 64:128] = mi
    shared["rw_mask1"] = m1
    shared["rw_mask2"] = m2
    shared["rw_maskf"] = mf
    rst = np.ones((128, 512), np.float32)
    rst[:, 0::64] = 0.0
    shared["rw_reset"] = rst
    shared["rw_ii2"] = np.concatenate([np.eye(64, dtype=np.float32)] * 2, 0)
    bd = np.zeros((128, 128), np.float32)
    bd[0:64, 0:64] = 1.0
    bd[64:128, 64:128] = 1.0
    shared["rw_bd"] = bd
    maps = []
    xp = np.asarray(inp["x_prompt"], np.float32)
    xs = np.asarray(inp["x_sample"], np.float32)
    cc = np.asarray(inp["c"], np.float32)
    cctx = np.asarray(inp["c_ctx"], np.float32)
    for c in range(8):
        b, q = c // 4, c % 4
        m = dict(shared)
        m["xp"] = f(xp[2 * c:2 * c + 2].reshape(TP, D).T)
        m["xs"] = f(xs[b].T)
        m["condT"] = f(np.stack([cctx, cc[b]], 0).reshape(2, 8, 128).transpose(2, 1, 0))
        oc = _own_cols(q, 4)
        m["hy_win_own"] = f(np.asarray(inp["hy_w_in"])[:, :, oc])
        cvo = cv.reshape(2, 4, 3, 8, 128)[:, :, :, 2 * q:2 * q + 2, :].transpose(4, 0, 3, 2, 1)
        m["hy_cv_own"] = f(cvo)
        m["hy_skip_own"] = f(_fm(inp["hy_skip"])[:, :, 2 * q:2 * q + 2])
        oc2 = np.concatenate([np.arange(256 * q, 256 * q + 256), 1024 + np.arange(256 * q, 256 * q + 256)])
        m["hy_w3a_own"] = f(w3a[:, :, oc2])
        m["hy_decay_own"] = f(np.asarray(inp["hy_decay"]).reshape(2, 2048)[:, oc2].reshape(2, 1, 512))
        m["mla_wz_own"] = f(wi[:, 416 + 256 * q:416 + 256 * q + 256])
        m["mla_wq_own"] = f(wq[:, 384 * q:384 * q + 384])
        m["mla_wk_own"] = f(wkv[:, 4 * q:4 * q + 4, 0, :].reshape(128, 256))
        m["mla_wv_own"] = f(wkv[:, 4 * q:4 * q + 4, 1, :].reshape(128, 256))
        oc4 = _own_cols(q, 4)
        m["rw_win_own"] = f(rwin_cat[:, oc4])
        m["rw_w2_own"] = f(np.asarray(inp["rw_w2"])[0][:, :, 256 * q:256 * q + 256])
        m["rw_a2_own"] = f(np.asarray(inp["rw_a2"])[0][:, :, 256 * q:256 * q + 256])
        m["rw_vec_own"] = f(vecT[:, 2 * q:2 * q + 2, :])
        stt_ = np.asarray(inp["state_rwkv"], np.float32)[b, 0][:, 4 * q:4 * q + 4]
        m["rw_state"] = f(stt_.transpose(0, 1, 3, 2))
        m["cache_ckvT"] = f(np.asarray(inp["cache_mla_ckv"], np.float32)[b, 0].T)
        m["cache_kpeT"] = f(np.asarray(inp["cache_mla_kpe"], np.float32)[b, 0].T)
        maps.append(m)
    return maps


class Scope(ExitStack):
    def __init__(self, prog):
        super().__init__()
        self.prog = prog
        self.tiles = []

    def __exit__(self, *a):
        pend = self.prog.pending
        for t in self.tiles:
            for tok in t.w + t.r:
                if tok not in pend:
                    pend.append(tok)
            self.prog.k.tiles.remove(t)
        return super().__exit__(*a)


class Prog:
    def __init__(self, nl=DEPTH, debug=False):
        self.pending = []
        self.nl = nl
        self.debug = debug
        self.nc = bass.Bass("TRN2", target_bir_lowering=False)
        self.k = KB(self.nc)
        self.din = {}
        self.dout = {}
        self.stats = []

    def inp(self, name, shape, dt=F32):
        t = self.k.track(Tile(self.nc.dram_tensor(name, list(shape), dt, kind="ExternalInput"), name))
        self.din[name] = t
        return t

    def outp(self, name, shape, dt=F32):
        t = self.k.track(Tile(self.nc.dram_tensor(name, list(shape), dt, kind="ExternalOutput"), name))
        self.dout[name] = t
        return t

    def scratch(self, name, shape, dt):
        return self.k.track(Tile(self.nc.dram_tensor(name, list(shape), dt), name))

    def sb(self, st, name, shape, dt):
        self.uid = getattr(self, "uid", 0) + 1
        name = "%s_%d" % (name, self.uid)
        t = self.k.track(Tile(st.enter_context(self.nc.sbuf_tensor(name, list(shape), dt)), name))
        t.w = list(self.pending)
        if hasattr(st, "tiles"):
            st.tiles.append(t)
        return t

    def scope(self):
        return Scope(self)

    def P(self, bank, out, lhsT, rhs, start, stop, reads, serial=False, fast=False):
        b0 = lhsT.base_partition()
        rg = (b0, b0 + lhsT.partition_size())
        last = bank.lastrg
        disjoint = last is not None and (rg[1] <= last[0] or last[1] <= rg[0])
        bank.lastrg = rg
        serial = serial or disjoint
        self.k.op("pe", lambda e: e.matmul(out, lhsT=lhsT, rhs=rhs, start=start, stop=stop),
                  reads=reads, writes=[bank], acc=(not start) and (not serial), skip_self=not serial)

    def A(self, writes, reads, fn):
        self.k.op("act", fn, reads=reads, writes=writes)

    def V(self, writes, reads, fn):
        self.k.op("dve", fn, reads=reads, writes=writes)

    def G(self, writes, reads, fn):
        self.k.op("pool", fn, reads=reads, writes=writes)

    def dsp(self, writes, reads, out, in_):
        self.k.dma("sp", lambda e: e.dma_start(out=out, in_=in_), reads=reads, writes=writes)

    def dpool(self, writes, reads, out, in_):
        self.k.dma("pool", lambda e: e.dma_start(out=out, in_=in_), reads=reads, writes=writes)

    def act(self, wt, out, rt, in_, func, scale=None, bias=None, extra_reads=()):
        kw = {}
        if scale is not None:
            kw["scale"] = scale
        if bias is not None:
            kw["bias"] = bias
        self.A([wt], [rt] + list(extra_reads), lambda e: e.activation(out=out, in_=in_, func=func, **kw))

    def tt(self, wt, out, r0, in0, r1, in1, op):
        self.V([wt], [r0, r1], lambda e: e.tensor_tensor(out=out, in0=in0, in1=in1, op=op))

    def stt(self, wt, out, r0, in0, scalar, r1, in1, op0, op1, extra_reads=()):
        self.V([wt], [r0, r1] + list(extra_reads),
               lambda e: e.scalar_tensor_tensor(out=out, in0=in0, scalar=scalar, in1=in1, op0=op0, op1=op1))

    def ts(self, wt, out, r0, in0, s1, s2, op0, op1=None, extra_reads=()):
        if op1 is None:
            self.V([wt], [r0] + list(extra_reads),
                   lambda e: e.tensor_scalar(out=out, in0=in0, scalar1=s1, scalar2=None, op0=op0))
        else:
            self.V([wt], [r0] + list(extra_reads),
                   lambda e: e.tensor_scalar(out=out, in0=in0, scalar1=s1, scalar2=s2, op0=op0, op1=op1))

    def memset(self, wt, ap, val):
        self.V([wt], [wt], lambda e: e.memset(ap, val))

    def recip(self, wt, out, rt, in_):
        self.V([wt], [rt], lambda e: e.reciprocal(out=out, in_=in_))

    def cp(self, wt, out, rt, in_):
        self.V([wt], [rt], lambda e: e.tensor_copy(out=out, in_=in_))

    def ps(self, hold=False):
        while True:
            b = self.PSB[self.psi % 8]
            self.psi += 1
            if b not in self.held:
                break
        if hold:
            self.held.append(b)
        return b

    def release(self, b):
        self.held.remove(b)

    def end_block(self, label):
        self.pending = []
        nops, nk = self.k.end_block()
        self.stats.append((label, nops, nk))

    def build(self):
        nc = self.nc
        with ExitStack() as top:
            self.top = top
            self.k.semstack = top
            PS = top.enter_context(nc.psum_tensor("PS", [128, 8, 512], F32))
            self.PSB = [self.k.track(Tile(PS[:, i, :], "ps%d" % i)) for i in range(8)]
            for b_ in self.PSB:
                b_.excl = True
            self.psi = 0
            self.held = []
            self.declare_io()
            self.XP = self.sb(top, "XP", [128, KC, TP], F32)
            self.XS = self.sb(top, "XS", [128, KC, TS], F32)
            self.MOD = self.sb(top, "MOD", [128, DEPTH, 24, 2], F32)
            self.LNG = self.sb(top, "LNG", [128, DEPTH, 8], F32)
            self.LNB = self.sb(top, "LNB", [128, DEPTH, 8], F32)
            self.IDF = self.sb(top, "IDF", [128, 128], F32)
            self.IDB = self.sb(top, "IDB", [128, 128], BF16)
            self.ONESF = self.sb(top, "ONESF", [128, 128], F32)
            self.ONESR = self.sb(top, "ONESR", [128, 128], F32R)
            self.EPS = self.sb(top, "EPS", [128, 4], F32)
            self.prologue()
            import os
            for i in range(self.nl):
                kind, j = i % 3, i // 3
                if os.environ.get("RW_ONLY") and kind != 2:
                    continue
                if kind == 0:
                    self.hyena_layer(i, j)
                elif kind == 1:
                    self.mla_layer(i, j)
                else:
                    self.rwkv_layer(i, j)
            self.epilogue()
        return nc

    def declare_io(self):
        I = self.inp
        I("xp", [D, TP]); I("xs", [D, TS]); I("condT", [128, 8, 2])
        I("mod_w", [4, D, 3 * D]); I("mod_bT", [128, 4, 24]); I("ln_gT", [128, 4, 8]); I("ln_bT", [128, 4, 8])
        I("hy_win", [2, D, 4096]); I("hy_win_own", [2, D, 1024])
        I("hy_cvT", [128, 2, 8, 3, 4]); I("hy_cv_own", [128, 2, 2, 3, 4])
        I("hy_skipT", [128, 2, 8]); I("hy_skip_own", [128, 2, 2])
        I("hy_w1", [2, 33, 64]); I("hy_b12", [2, 64, 2]); I("hy_w2", [2, 64, 64])
        I("hy_w3a", [2, 65, 2048]); I("hy_w3a_own", [2, 65, 512]); I("hy_freqT", [2, 64, 2])
        I("hy_decay", [2, 1, 2048]); I("hy_decay_own", [2, 1, 512]); I("hy_wout", [2, D, D])
        I("fwd_s", [16, 128, 16, 2, 128], BF16); I("inv_s", [4, 16, 128, 2, 512], BF16)
        I("nyq_s", [128, 16], BF16); I("nyqrow_s", [1, TS], BF16); I("feat_s", [33, TS]); I("negt_s", [128, 16])
        I("fwd_p", [128, 2, 2, 256], BF16); I("inv_p", [128, 2, 2, 256], BF16)
        I("nyq_p", [128, 2], BF16); I("nyqrow_p", [1, SEQ], BF16); I("feat_p", [33, SEQ]); I("negt_p", [128, 2])
        I("ident", [128, 128])
        I("mla_wh", [D, 416]); I("mla_wz", [D, D]); I("mla_wz_own", [D, 256]); I("mla_nrm", [128, 3])
        I("mla_wq", [256, 1536]); I("mla_wq_own", [256, 384]); I("mla_wk", [128, D]); I("mla_wv", [128, D])
        I("mla_wk_own", [128, 256]); I("mla_wv_own", [128, 256]); I("mla_wout", [1, D, D])
        I("rope_cs32", [32, 2, TS]); I("rope_cs96", [96, 2, TS]); I("rope_R32", [32, 32]); I("rope_R96", [96, 96])
        I("cache_ckvT", [128, 512]); I("cache_kpeT", [32, 512])
        self.outp("ckv_out", [128, TP]); self.outp("kpe_out", [32, TP])
        I("rw_muT", [128, 6, 8]); I("rw_win", [D, 4096]); I("rw_win_own", [D, 1024])
        I("rw_w1", [2, D, 64]); I("rw_a1", [2, D, 64]); I("rw_w2", [2, 64, D]); I("rw_a2", [2, 64, D])
        I("rw_w2_own", [2, 64, 256]); I("rw_a2_own", [2, 64, 256]); I("rw_vec", [128, 8, 9]); I("rw_vec_own", [128, 2, 9])
        I("rw_wout", [1, D, D]); I("rw_state", [2, 4, 64, 64])
        I("rw_mask1", [2, 64, 192]); I("rw_mask2", [2, 64, 192]); I("rw_maskf", [2, 128, 128])
        I("rw_reset", [128, 512]); I("rw_ii2", [128, 64]); I("rw_bd", [128, 128])
        self.outp("st_out", [2, 2, 16, 64, 64])
        self.outp("yp", [D, TP]); self.outp("ys", [D, TS])
        self.AGIN = [self.scratch("agin%d" % i, [256, TS], BF16) for i in range(DEPTH)]
        self.AGOUT = [self.scratch("agout%d" % i, [1024, TS], BF16) for i in range(DEPTH)]

    def prologue(self):
        d = self.din
        with self.scope() as st:
            self.dsp([self.XP], [d["xp"]], self.XP[:, :, :], d["xp"].h.ap().rearrange("(k p) t -> p k t", p=128))
            for k in range(KC):
                self.dsp([self.XS], [d["xs"]], self.XS[:, k, :], d["xs"].h[k * 128:(k + 1) * 128, :])
            self.dsp([self.LNG], [d["ln_gT"]], self.LNG[:, :, :], d["ln_gT"].h[:, :, :])
            self.dsp([self.LNB], [d["ln_bT"]], self.LNB[:, :, :], d["ln_bT"].h[:, :, :])
            self.dsp([self.IDF], [d["ident"]], self.IDF[:, :], d["ident"].h[:, :])
            self.dpool([self.IDB], [d["ident"]], self.IDB[:, :], d["ident"].h[:, :])
            self.memset(self.ONESF, self.ONESF[:, :], 1.0)
            self.act(self.ONESR, self.ONESR[:, :], self.ONESF, self.ONESF[:, :], AF.Copy)
            self.memset(self.EPS, self.EPS[:, 0:1], LN_EPS)
            self.memset(self.EPS, self.EPS[:, 1:2], RMS_EPS)
            self.memset(self.EPS, self.EPS[:, 2:3], 64e-5)
            self.memset(self.EPS, self.EPS[:, 3:4], 1e-12)
            CD = self.sb(st, "CD", [128, 8, 2], F32)
            SC = self.sb(st, "SC", [128, 8, 2], BF16)
            MB = self.sb(st, "MB", [128, DEPTH, 24], F32)
            WB = [self.sb(st, "WBm%d" % i, [128, 8, 512], BF16) for i in range(3)]
            self.dsp([CD], [d["condT"]], CD[:, :, :], d["condT"].h[:, :, :])
            self.dsp([MB], [d["mod_bT"]], MB[:, :, :], d["mod_bT"].h[:, :, :])
            self.act(SC, SC[:, :, :], CD, CD[:, :, :], AF.Silu)
            wi = 0
            for i in range(DEPTH):
                bank = self.ps()
                for cb in range(6):
                    wb = WB[wi % 3]
                    wi += 1
                    src = d["mod_w"].h[i].rearrange("(k p) e -> p k e", p=128)[:, :, cb * 512:(cb + 1) * 512]
                    self.dpool([wb], [d["mod_w"]], wb[:, :, :], src)
                    for m4 in range(4):
                        m = cb * 4 + m4
                        for k in range(KC):
                            self.P(bank, bank[:, 2 * m:2 * m + 2], wb[:, k, m4 * 128:(m4 + 1) * 128], SC[:, k, :],
                                   start=(k == 0), stop=(k == KC - 1), reads=[wb, SC])
                self.tt(self.MOD, self.MOD[:, i, :, :], bank, bank[:, 0:48].rearrange("p (m n) -> p m n", n=2),
                        MB, MB[:, i, :].unsqueeze(2).to_broadcast([128, 24, 2]), ALU.add)
                self.ts(self.MOD, self.MOD[:, i, 8:16, :], self.MOD, self.MOD[:, i, 8:16, :], 1.0, None, ALU.add)
            self.end_block("prologue")

    def make_h(self, i, n, hb, out_ap_fn, X, col0, ncol):
        for k in range(KC):
            self.act(hb, out_ap_fn(k), X, X[:, k, col0:col0 + ncol], AF.Identity,
                     scale=self.MOD[:, i, 8 + k, n:n + 1], bias=self.MOD[:, i, k, n:n + 1],
                     extra_reads=[self.MOD])

    def outproj_ln(self, i, st, wsrc, blocks):
        WO = self.sb(st, "WO", [128, 8, D], BF16)
        for cb in range(2):
            self.dpool([WO], [wsrc[0]], WO[:, :, cb * 512:(cb + 1) * 512],
                       wsrc[1].rearrange("(k p) e -> p k e", p=128)[:, :, cb * 512:(cb + 1) * 512])
        T1 = self.sb(st, "T1", [128, 512], F32)
        SQ = self.sb(st, "SQ", [128, 8, 512], F32R)
        MEAN = self.sb(st, "MEAN", [128, 512], F32)
        VAR = self.sb(st, "VAR", [128, 512], F32)
        M2 = self.sb(st, "M2", [128, 512], F32)
        for (n, X, col0, OB, loader) in blocks:
            if loader is not None:
                loader()
            for ec in range(KC):
                bank = self.ps()
                for k in range(KC):
                    self.P(bank, bank[:, :], WO[:, k, ec * 128:(ec + 1) * 128], OB[:, k, :],
                           start=(k == 0), stop=(k == KC - 1), reads=[WO, OB])
                self.act(T1, T1[:, :], bank, bank[:, :], AF.Identity, scale=self.MOD[:, i, 16 + ec, n:n + 1],
                         extra_reads=[self.MOD])
                xs_ = X[:, ec, col0:col0 + 512]
                self.stt(X, xs_, X, xs_, ALPHA, T1, T1[:, :], ALU.mult, ALU.add)
            self.layernorm_block(i, X, col0, SQ, MEAN, VAR, M2)

    def layernorm_block(self, i, X, col0, SQ, MEAN, VAR, M2):
        xb = X[:, :, col0:col0 + 512]
        self.act(SQ, SQ[:, :, :], X, xb, AF.Square)
        b1 = self.ps()
        b2 = self.ps()
        for k in range(KC):
            self.P(b1, b1[:, :], self.ONESF[:, :], X[:, k, col0:col0 + 512], start=(k == 0), stop=(k == KC - 1),
                   reads=[self.ONESF, X])
        for k in range(KC):
            self.P(b2, b2[:, :], self.ONESR[:, :], SQ[:, k, :], start=(k == 0), stop=(k == KC - 1),
                   reads=[self.ONESR, SQ])
        self.act(MEAN, MEAN[:, :], b1, b1[:, :], AF.Identity, scale=1.0 / D)
        self.tt(M2, M2[:, :], MEAN, MEAN[:, :], MEAN, MEAN[:, :], ALU.mult)
        self.stt(VAR, VAR[:, :], b2, b2[:, :], 1.0 / D, M2, M2[:, :], ALU.mult, ALU.subtract)
        self.act(VAR, VAR[:, :], VAR, VAR[:, :], AF.Sqrt, scale=1.0, bias=self.EPS[:, 0:1], extra_reads=[self.EPS])
        self.recip(VAR, VAR[:, :], VAR, VAR[:, :])
        self.tt(X, xb, X, xb, MEAN, MEAN[:, :].unsqueeze(1).to_broadcast([128, 8, 512]), ALU.subtract)
        self.tt(X, xb, X, xb, VAR, VAR[:, :].unsqueeze(1).to_broadcast([128, 8, 512]), ALU.mult)
        for k in range(KC):
            xk = X[:, k, col0:col0 + 512]
            self.act(X, xk, X, xk, AF.Identity, scale=self.LNG[:, i, k:k + 1], bias=self.LNB[:, i, k:k + 1],
                     extra_reads=[self.LNG, self.LNB])

    def gather_and_outproj(self, i, st, wsrc, OP, OS):
        agin, agout = self.AGIN[i], self.AGOUT[i]
        self.dsp([agin], [OS], agin.h.ap().rearrange("(j p) t -> p j t", p=128), OS[:, :, :])
        self.k.special("pool", lambda e: e.collective_compute(
            "AllGather", ALU.bypass, replica_groups=[[0, 1, 2, 3], [4, 5, 6, 7]],
            ins=[agin.h.ap().opt()], outs=[agout.h.ap().opt()]), reads=[agin], writes=[agout])
        OB = [self.sb(st, "OB%d" % t, [128, 8, 512], BF16) for t in range(2)]
        blocks = [(0, self.XP, 0, OP, None)]
        for tb in range(4):
            ob = OB[tb % 2]

            def loader(ob=ob, tb=tb):
                self.dsp([ob], [agout], ob[:, :, :],
                         agout.h.ap().rearrange("(k p) t -> p k t", p=128)[:, :, tb * 512:(tb + 1) * 512])
            blocks.append((1, self.XS, tb * 512, ob, loader))
        self.outproj_ln(i, st, wsrc, blocks)

    def hyena_filter_mlp(self, st, j, L, featname):
        d = self.din
        W1 = self.sb(st, "hW1", [33, 64], F32)
        W2 = self.sb(st, "hW2", [64, 64], F32)
        B12 = self.sb(st, "hB12", [64, 4], F32)
        FQ = self.sb(st, "hFQ", [64, 2], F32)
        self.dsp([W1], [d["hy_w1"]], W1[:, :], d["hy_w1"].h[j])
        self.dsp([W2], [d["hy_w2"]], W2[:, :], d["hy_w2"].h[j])
        self.dsp([B12], [d["hy_b12"]], B12[:, 0:2], d["hy_b12"].h[j])
        self.dsp([FQ], [d["hy_freqT"]], FQ[:, :], d["hy_freqT"].h[j])
        self.tt(B12, B12[:, 2:4], B12, B12[:, 0:2], FQ, FQ[:, 0:2], ALU.mult)
        FT = self.sb(st, "hFT", [33, L], F32)
        self.dsp([FT], [d[featname]], FT[:, :], d[featname].h[:, :])
        H1 = self.sb(st, "hH1", [64, L], F32)
        H2A = self.sb(st, "hH2A", [65, L], F32)
        AR = self.sb(st, "hAR", [64, 512], F32)
        KI = self.sb(st, "hKI", [64, 512], I32)
        self.memset(H2A, H2A[64:65, :], 1.0)
        nb = max(1, L // 512)
        bw = min(L, 512)
        for layer in range(2):
            for b in range(nb):
                cs = slice(b * bw, (b + 1) * bw)
                bank = self.ps()
                if layer == 0:
                    self.P(bank, bank[0:64, 0:bw], W1[:, :], FT[:, cs], True, True, [W1, FT])
                    dst, dt_ = H1, H1[:, cs]
                else:
                    self.P(bank, bank[0:64, 0:bw], W2[:, :], H1[:, cs], True, True, [W2, H1])
                    dst, dt_ = H2A, H2A[0:64, cs]
                self.act(AR, AR[:, 0:bw], bank, bank[0:64, 0:bw], AF.Identity, scale=FQ[:, layer:layer + 1],
                         bias=B12[:, 2 + layer:3 + layer], extra_reads=[FQ, B12])
                self.ts(KI, KI[:, 0:bw], AR, AR[:, 0:bw], 1.0 / TWO_PI, None, ALU.mult)
                self.stt(AR, AR[:, 0:bw], KI, KI[:, 0:bw], -TWO_PI, AR, AR[:, 0:bw], ALU.mult, ALU.add)
                self.ts(AR, AR[:, 0:bw], AR, AR[:, 0:bw], -3.14159, 3.14159, ALU.max, ALU.min)
                self.act(dst, dt_, AR, AR[:, 0:bw], AF.Sin)
        return H2A

    def hyena_layer(self, i, j):
        d = self.din
        with self.scope() as st:
            OP = self.sb(st, "OP", [128, 8, TP], BF16)
            OS = self.sb(st, "OS", [128, 2, TS], BF16)
            with self.scope() as s2:
                VX = self.sb(s2, "VXp", [128, 8, TP], BF16)
                G = self.sb(s2, "Gp", [128, 8, TP], BF16)
                CV = self.sb(s2, "CVp", [128, 8, 3, 4], F32)
                SK = self.sb(s2, "SKp", [128, 8], F32)
                self.dsp([CV], [d["hy_cvT"]], CV[:, :, :, :], d["hy_cvT"].h[:, j])
                self.dsp([SK], [d["hy_skipT"]], SK[:, :], d["hy_skipT"].h[:, j, :])
                FS = self.sb(s2, "FSp", [128, 2, D], BF16)
                FD = self.sb(s2, "FDp", [128, 2, D], BF16)
                VXT = self.sb(s2, "VXTp", [128, 4, D], BF16)
                with self.scope() as s3:
                    H2A = self.hyena_filter_mlp(s3, j, SEQ, "feat_p")
                    W3 = self.sb(s3, "W3p", [65, 2048], F32)
                    ADEC = self.sb(s3, "ADECp", [128, 2048], F32)
                    NEGT = self.sb(s3, "NEGTp", [128, 2], F32)
                    WIN = self.sb(s3, "WINp", [128, 2, 512], F32)
                    HW = self.sb(s3, "HWp", [128, 2, 512], F32)
                    self.dsp([W3], [d["hy_w3a"]], W3[:, :], d["hy_w3a"].h[j])
                    self.dsp([ADEC], [d["hy_decay"]], ADEC[:, :], d["hy_decay"].h[j].partition_broadcast(128))
                    self.dsp([NEGT], [d["negt_p"]], NEGT[:, :], d["negt_p"].h[:, :])
                    self.act(ADEC, ADEC[:, :], ADEC, ADEC[:, :], AF.Abs)
                    for lch in range(2):
                        for cb in range(2):
                            for dr in range(2):
                                bank = self.ps()
                                cs = slice(dr * 1024 + cb * 512, dr * 1024 + cb * 512 + 512)
                                self.P(bank, bank[:, :], H2A[0:65, lch * 128:(lch + 1) * 128], W3[0:65, cs], True, True,
                                       [H2A, W3])
                                self.act(WIN, WIN[:, dr, :], ADEC, ADEC[:, cs], AF.Exp, scale=NEGT[:, lch:lch + 1],
                                         extra_reads=[NEGT])
                                self.tt(HW, HW[:, dr, :], bank, bank[:, :], WIN, WIN[:, dr, :], ALU.mult)
                            if lch == 0:
                                self.memset(HW, HW[0:1, 1, :], 0.0)
                            self.tt(FS, FS[:, lch, cb * 512:(cb + 1) * 512], HW, HW[:, 0, :], HW, HW[:, 1, :], ALU.add)
                            self.tt(FD, FD[:, lch, cb * 512:(cb + 1) * 512], HW, HW[:, 0, :], HW, HW[:, 1, :], ALU.subtract)
                with self.scope() as s3:
                    WBs = [self.sb(s3, "WBh%d" % t, [128, 8, 512], BF16) for t in range(2)]
                    segs = [(0, SEQ, 0, SEQ), (SEQ, SEQ, 0, SEQ)]
                    HB = [self.sb(s3, "HBp%d" % t, [128, 8, 258], BF16) for t in range(2)]
                    for s in range(2):
                        hb = HB[s]
                        self.memset(hb, hb[:, :, 0:1], 0.0)
                        self.memset(hb, hb[:, :, 257:258], 0.0)
                        self.make_h(i, 0, hb, lambda k, hb=hb: hb[:, k, 1:257], self.XP, s * SEQ, SEQ)
                    UC = [self.sb(s3, "UCp%d" % t, [128, 256], F32) for t in range(3)]
                    SZ = self.sb(s3, "SZp", [128, 256], F32)
                    for jc in range(8):
                        wb = WBs[jc % 2]
                        self.dpool([wb], [d["hy_win"]], wb[:, :, :],
                                   d["hy_win"].h[j].rearrange("(k p) e -> p k e", p=128)[:, :, jc * 512:(jc + 1) * 512])
                        for s in range(2):
                            hb = HB[s]
                            self.conv_chunk(hb, 258, SEQ, wb, 0, CV[:, jc], CV, UC, SZ, VX, G, jc, s * SEQ)
                        bank = self.ps()
                        for tb in range(4):
                            self.P(bank, bank[:, tb * 128:(tb + 1) * 128], VX[:, jc, tb * 128:(tb + 1) * 128], self.IDB[:, :],
                                   True, True, [VX, self.IDB])
                        self.cp(VXT, VXT[:, :, jc * 128:(jc + 1) * 128], bank, bank[:, :].rearrange("p (a b) -> p a b", b=128))
                YRE = self.sb(s2, "YREp", [128, 2, 2, D], BF16)
                YIM = self.sb(s2, "YIMp", [128, 2, 2, D], BF16)
                YNQ = self.sb(s2, "YNQp", [1, 2, D], BF16)
                with self.scope() as s3:
                    FW = self.sb(s3, "FWp", [128, 2, 2, 256], BF16)
                    NQ = self.sb(s3, "NQp", [128, 2], BF16)
                    self.dsp([FW], [d["fwd_p"]], FW[:, :, :, :], d["fwd_p"].h[:, :, :, :])
                    self.dsp([NQ], [d["nyq_p"]], NQ[:, :], d["nyq_p"].h[:, :])
                    KRE = self.sb(s3, "KREp", [128, 2, D], F32)
                    KQ = self.sb(s3, "KQp", [128, 2, D], F32)
                    KNQ = self.sb(s3, "KNQp", [1, D], F32)
                    SA = self.sb(s3, "SAp", [128, 512], F32)
                    SB_ = self.sb(s3, "SBp", [128, 512], F32)
                    T1 = self.sb(s3, "T1p", [128, 512], F32)
                    T2 = self.sb(s3, "T2p", [128, 512], F32)
                    for fch in range(2):
                        for cb in range(2):
                            cs = slice(cb * 512, (cb + 1) * 512)
                            ba = self.ps()
                            bb = self.ps()
                            for lch in range(2):
                                self.P(ba, ba[:, :], FW[:, lch, 0, fch * 128:(fch + 1) * 128], FS[:, lch, cs],
                                       lch == 0, lch == 1, [FW, FS])
                            for lch in range(2):
                                self.P(bb, bb[:, :], FW[:, lch, 1, fch * 128:(fch + 1) * 128], FD[:, lch, cs],
                                       lch == 0, lch == 1, [FW, FD])
                            self.act(KRE, KRE[:, fch, cs], ba, ba[:, :], AF.Copy)
                            self.act(KQ, KQ[:, fch, cs], bb, bb[:, :], AF.Copy)
                    for cb in range(2):
                        cs = slice(cb * 512, (cb + 1) * 512)
                        bn = self.ps()
                        for lch in range(2):
                            self.P(bn, bn[0:1, :], NQ[:, lch:lch + 1], FS[:, lch, cs], lch == 0, lch == 1, [NQ, FS])
                        self.act(KNQ, KNQ[0:1, cs], bn, bn[0:1, :], AF.Copy)
                    for s in range(2):
                        for fch in range(2):
                            for cb in range(2):
                                cs = slice(cb * 512, (cb + 1) * 512)
                                ba = self.ps()
                                bb = self.ps()
                                for lch in range(2):
                                    self.P(ba, ba[:, :], FW[:, lch, 0, fch * 128:(fch + 1) * 128], VXT[:, s * 2 + lch, cs],
                                           lch == 0, lch == 1, [FW, VXT])
                                for lch in range(2):
                                    self.P(bb, bb[:, :], FW[:, lch, 1, fch * 128:(fch + 1) * 128], VXT[:, s * 2 + lch, cs],
                                           lch == 0, lch == 1, [FW, VXT])
                                self.act(SA, SA[:, :], ba, ba[:, :], AF.Copy)
                                self.act(SB_, SB_[:, :], bb, bb[:, :], AF.Copy)
                                self.tt(T1, T1[:, :], SA, SA[:, :], KRE, KRE[:, fch, cs], ALU.mult)
                                self.tt(T2, T2[:, :], SB_, SB_[:, :], KQ, KQ[:, fch, cs], ALU.mult)
                                self.tt(YRE, YRE[:, fch, s, cs], T1, T1[:, :], T2, T2[:, :], ALU.subtract)
                                self.tt(T1, T1[:, :], SA, SA[:, :], KQ, KQ[:, fch, cs], ALU.mult)
                                self.tt(T2, T2[:, :], SB_, SB_[:, :], KRE, KRE[:, fch, cs], ALU.mult)
                                self.tt(YIM, YIM[:, fch, s, cs], T1, T1[:, :], T2, T2[:, :], ALU.add)
                        for cb in range(2):
                            cs = slice(cb * 512, (cb + 1) * 512)
                            bn = self.ps()
                            for lch in range(2):
                                self.P(bn, bn[0:1, :], NQ[:, lch:lch + 1], VXT[:, s * 2 + lch, cs], lch == 0, lch == 1,
                                       [NQ, VXT])
                            self.tt(YNQ, YNQ[0:1, s, cs], bn, bn[0:1, :], KNQ, KNQ[0:1, cs], ALU.mult)
                with self.scope() as s3:
                    IV = self.sb(s3, "IVp", [128, 2, 2, 256], BF16)
                    NR = self.sb(s3, "NRp", [1, SEQ], BF16)
                    T1 = self.sb(s3, "T1q", [128, 512], F32)
                    self.dsp([IV], [d["inv_p"]], IV[:, :, :, :], d["inv_p"].h[:, :, :, :])
                    self.dsp([NR], [d["nyqrow_p"]], NR[:, :], d["nyqrow_p"].h[:, :])
                    for jc in range(8):
                        bank = self.ps()
                        cc = slice(jc * 128, (jc + 1) * 128)
                        for s in range(2):
                            o = bank[:, s * SEQ:(s + 1) * SEQ]
                            for fch in range(2):
                                self.P(bank, o, YRE[:, fch, s, cc], IV[:, fch, 0, :], fch == 0, False, [YRE, IV])
                                self.P(bank, o, YIM[:, fch, s, cc], IV[:, fch, 1, :], False, False, [YIM, IV])
                            self.P(bank, o, YNQ[0:1, s, cc], NR[0:1, :], False, True, [YNQ, NR])
                        self.stt(T1, T1[:, :], VX, VX[:, jc, :], SK[:, jc:jc + 1], bank, bank[:, :], ALU.mult, ALU.add,
                                 extra_reads=[SK])
                        self.tt(OP, OP[:, jc, :], T1, T1[:, :], G, G[:, jc, :], ALU.mult)
            with self.scope() as s2:
                VX = self.sb(s2, "VXs", [128, 2, TS], BF16)
                G = self.sb(s2, "Gs", [128, 2, TS], BF16)
                CV = self.sb(s2, "CVs", [128, 2, 3, 4], F32)
                SK = self.sb(s2, "SKs", [128, 2], F32)
                self.dsp([CV], [d["hy_cv_own"]], CV[:, :, :, :], d["hy_cv_own"].h[:, j])
                self.dsp([SK], [d["hy_skip_own"]], SK[:, :], d["hy_skip_own"].h[:, j, :])
                RS = self.sb(s2, "RSs", [128, 16, 768], BF16)
                with self.scope() as s3:
                    H2A = self.hyena_filter_mlp(s3, j, TS, "feat_s")
                    W3 = self.sb(s3, "W3s", [65, 512], F32)
                    ADEC = self.sb(s3, "ADECs", [128, 512], F32)
                    NEGT = self.sb(s3, "NEGTs", [128, 16], F32)
                    WIN = self.sb(s3, "WINs", [128, 512], F32)
                    HW = self.sb(s3, "HWs", [128, 512], F32)
                    self.dsp([W3], [d["hy_w3a_own"]], W3[:, :], d["hy_w3a_own"].h[j])
                    self.dsp([ADEC], [d["hy_decay_own"]], ADEC[:, :], d["hy_decay_own"].h[j].partition_broadcast(128))
                    self.dsp([NEGT], [d["negt_s"]], NEGT[:, :], d["negt_s"].h[:, :])
                    self.act(ADEC, ADEC[:, :], ADEC, ADEC[:, :], AF.Abs)
                    for lch in range(16):
                        bank = self.ps()
                        self.P(bank, bank[:, :], H2A[0:65, lch * 128:(lch + 1) * 128], W3[0:65, :], True, True, [H2A, W3])
                        self.act(WIN, WIN[:, :], ADEC, ADEC[:, :], AF.Exp, scale=NEGT[:, lch:lch + 1], extra_reads=[NEGT])
                        self.tt(HW, HW[:, :], bank, bank[:, :], WIN, WIN[:, :], ALU.mult)
                        if lch == 0:
                            self.memset(HW, HW[0:1, 256:512], 0.0)
                        self.tt(RS, RS[:, lch, 0:256], HW, HW[:, 0:256], HW, HW[:, 256:512], ALU.add)
                        self.tt(RS, RS[:, lch, 512:768], HW, HW[:, 0:256], HW, HW[:, 256:512], ALU.subtract)
                with self.scope() as s3:
                    WB = self.sb(s3, "WBs", [128, 8, 1024], BF16)
                    for cb in range(2):
                        self.dpool([WB], [d["hy_win_own"]], WB[:, :, cb * 512:(cb + 1) * 512],
                                   d["hy_win_own"].h[j].rearrange("(k p) e -> p k e", p=128)[:, :, cb * 512:(cb + 1) * 512])
                    HB = [self.sb(s3, "HBs%d" % t, [128, 8, 512], BF16) for t in range(2)]
                    UC = [self.sb(s3, "UCs%d" % t, [128, 512], F32) for t in range(3)]
                    SZ = self.sb(s3, "SZs", [128, 512], F32)
                    t0 = 0
                    si = 0
                    while t0 < TS:
                        nt = min(510, TS - t0)
                        hb = HB[si % 2]
                        si += 1
                        lo = max(t0 - 1, 0)
                        hi = min(t0 + nt + 1, TS)
                        off = lo - (t0 - 1)
                        ncols = nt + 2
                        if off > 0:
                            self.memset(hb, hb[:, :, 0:1], 0.0)
                        if hi < t0 + nt + 1:
                            self.memset(hb, hb[:, :, ncols - 1:ncols], 0.0)
                        self.make_h(i, 1, hb, lambda k, hb=hb, off=off, w=hi - lo: hb[:, k, off:off + w], self.XS, lo, hi - lo)
                        for jc in range(2):
                            self.conv_chunk(hb, ncols, nt, WB, jc * 512, CV[:, jc], CV, UC, SZ, VX, G, jc, t0)
                        t0 += nt
                    for l2 in range(8):
                        bank = self.ps()
                        for u in range(2):
                            lch = l2 * 2 + u
                            for jc in range(2):
                                self.P(bank, bank[:, u * 256 + jc * 128:u * 256 + (jc + 1) * 128],
                                       VX[:, jc, lch * 128:(lch + 1) * 128], self.IDB[:, :], True, True, [VX, self.IDB])
                        self.cp(RS, RS[:, l2 * 2:l2 * 2 + 2, 256:512], bank, bank[:, :].rearrange("p (a b) -> p a b", b=256))
                YRE = self.sb(s2, "YREs", [128, 16, 256], BF16)
                YIM = self.sb(s2, "YIMs", [128, 16, 256], BF16)
                YNQ = self.sb(s2, "YNQs", [1, 256], BF16)
                with self.scope() as s3:
                    FW = [self.sb(s3, "FWs%d" % t, [128, 16, 2, 128], BF16) for t in range(2)]
                    NQ = self.sb(s3, "NQs", [128, 16], BF16)
                    SA = self.sb(s3, "SAs", [128, 512], F32)
                    SB_ = self.sb(s3, "SBs", [128, 512], F32)
                    T1 = self.sb(s3, "T1s", [128, 256], F32)
                    T2 = self.sb(s3, "T2s", [128, 256], F32)
                    self.dsp([NQ], [d["nyq_s"]], NQ[:, :], d["nyq_s"].h[:, :])
                    for fch in range(16):
                        fw = FW[fch % 2]
                        self.dsp([fw], [d["fwd_s"]], fw[:, :, :, :], d["fwd_s"].h[fch])
                        ba = self.ps()
                        bb = self.ps()
                        for lch in range(16):
                            self.P(ba, ba[:, :], fw[:, lch, 0, :], RS[:, lch, 0:512], lch == 0, lch == 15, [fw, RS])
                        for lch in range(16):
                            self.P(bb, bb[:, :], fw[:, lch, 1, :], RS[:, lch, 256:768], lch == 0, lch == 15, [fw, RS])
                        self.act(SA, SA[:, :], ba, ba[:, :], AF.Copy)
                        self.act(SB_, SB_[:, :], bb, bb[:, :], AF.Copy)
                        self.tt(T1, T1[:, :], SA, SA[:, 0:256], SA, SA[:, 256:512], ALU.mult)
                        self.tt(T2, T2[:, :], SB_, SB_[:, 0:256], SB_, SB_[:, 256:512], ALU.mult)
                        self.tt(YRE, YRE[:, fch, :], T1, T1[:, :], T2, T2[:, :], ALU.subtract)
                        self.tt(T1, T1[:, :], SA, SA[:, 256:512], SB_, SB_[:, 256:512], ALU.mult)
                        self.tt(T2, T2[:, :], SB_, SB_[:, 0:256], SA, SA[:, 0:256], ALU.mult)
                        self.tt(YIM, YIM[:, fch, :], T1, T1[:, :], T2, T2[:, :], ALU.add)
                    bn = self.ps()
                    for lch in range(16):
                        self.P(bn, bn[0:1, :], NQ[:, lch:lch + 1], RS[:, lch, 0:512], lch == 0, lch == 15, [NQ, RS])
                    self.act(SA, SA[0:1, :], bn, bn[0:1, :], AF.Copy)
                    self.tt(YNQ, YNQ[0:1, :], SA, SA[0:1, 0:256], SA, SA[0:1, 256:512], ALU.mult)
                with self.scope() as s3:
                    IV = [self.sb(s3, "IVs%d" % t, [128, 2, 512], BF16) for t in range(4)]
                    NR = self.sb(s3, "NRs", [1, TS], BF16)
                    T1 = self.sb(s3, "T1t", [128, 512], F32)
                    self.dsp([NR], [d["nyqrow_s"]], NR[:, :], d["nyqrow_s"].h[:, :])
                    ivn = 0
                    for tb in range(4):
                        ts_ = slice(tb * 512, (tb + 1) * 512)
                        banks = [self.ps(), self.ps()]
                        for fch in range(16):
                            iv = IV[ivn % 4]
                            ivn += 1
                            self.dsp([iv], [d["inv_s"]], iv[:, :, :], d["inv_s"].h[tb, fch])
                            for jc in range(2):
                                cc = slice(jc * 128, (jc + 1) * 128)
                                self.P(banks[jc], banks[jc][:, :], YRE[:, fch, cc], iv[:, 0, :], fch == 0, False, [YRE, iv])
                                self.P(banks[jc], banks[jc][:, :], YIM[:, fch, cc], iv[:, 1, :], False, False, [YIM, iv])
                        for jc in range(2):
                            cc = slice(jc * 128, (jc + 1) * 128)
                            self.P(banks[jc], banks[jc][:, :], YNQ[0:1, cc], NR[0:1, ts_], False, True, [YNQ, NR])
                            self.stt(T1, T1[:, :], VX, VX[:, jc, ts_], SK[:, jc:jc + 1], banks[jc], banks[jc][:, :],
                                     ALU.mult, ALU.add, extra_reads=[SK])
                            self.tt(OS, OS[:, jc, ts_], T1, T1[:, :], G, G[:, jc, ts_], ALU.mult)
            self.gather_and_outproj(i, st, (d["hy_wout"], d["hy_wout"].h[j]), OP, OS)
            self.end_block("hyena%d" % i)

    def conv_chunk(self, hb, ncols, nt, WB, wc0, cv, cv_t, UC, SZ, VX, G, jc, outc0):
        banks = [self.ps() for _ in range(4)]
        for w in range(4):
            for k in range(KC):
                c0 = wc0 + w * 128
                self.P(banks[w], banks[w][:, 0:ncols], WB[:, k, c0:c0 + 128], hb[:, k, 0:ncols],
                       start=(k == 0), stop=(k == KC - 1), reads=[WB, hb])
        for w in range(3):
            uc = UC[w]
            bk = banks[w]
            self.act(uc, uc[:, 0:nt], bk, bk[:, 1:nt + 1], AF.Identity, scale=cv[:, w, 1:2], bias=cv[:, w, 3:4],
                     extra_reads=[cv_t])
            self.stt(uc, uc[:, 0:nt], bk, bk[:, 0:nt], cv[:, w, 0:1], uc, uc[:, 0:nt], ALU.mult, ALU.add,
                     extra_reads=[cv_t])
            self.stt(uc, uc[:, 0:nt], bk, bk[:, 2:nt + 2], cv[:, w, 2:3], uc, uc[:, 0:nt], ALU.mult, ALU.add,
                     extra_reads=[cv_t])
        self.act(SZ, SZ[:, 0:nt], banks[3], banks[3][:, 1:nt + 1], AF.Silu)
        self.tt(VX, VX[:, jc, outc0:outc0 + nt], UC[1], UC[1][:, 0:nt], UC[2], UC[2][:, 0:nt], ALU.mult)
        self.tt(G, G[:, jc, outc0:outc0 + nt], UC[0], UC[0][:, 0:nt], SZ, SZ[:, 0:nt], ALU.mult)

    def mla_layer(self, i, j):
        d = self.din
        with self.scope() as st:
            OP = self.sb(st, "OPm", [128, 8, TP], BF16)
            OS = self.sb(st, "OSm", [128, 2, TS], BF16)
            NRM = self.sb(st, "NRMm", [128, 3], F32)
            WH = None
            self.dsp([NRM], [d["mla_nrm"]], NRM[:, :], d["mla_nrm"].h[:, :])
            self.mla_group(i, st, 0, self.XP, TP, 16, 8, 0, TP, [(0, SEQ, 0, SEQ), (SEQ, SEQ, SEQ, SEQ)], False,
                           d["mla_wz"], d["mla_wq"], d["mla_wk"], d["mla_wv"], OP, WH, NRM)
            self.mla_group(i, st, 1, self.XS, TS, 4, 2, 512, 2560, [(0, TS, 0, 2560)], True,
                           d["mla_wz_own"], d["mla_wq_own"], d["mla_wk_own"], d["mla_wv_own"], OS, WH, NRM)
            self.gather_and_outproj(i, st, (d["mla_wout"], d["mla_wout"].h[j]), OP, OS)
            self.end_block("mla%d" % i)

    def mla_group(self, i, st0, n, X, T, NH, ZC, koff, KA, units, rope, wz, wq, wk, wv, OUT, WH, NRM):
        d = self.din
        SCALE = 96.0 ** -0.5
        nblk = T // 512
        with self.scope() as st:
            QN = self.sb(st, "QN", [128, 2, T], BF16)
            CKVb = self.sb(st, "CKVb", [128, KA], BF16)
            KPEb = self.sb(st, "KPEb", [32, KA], BF16)
            ZG = self.sb(st, "ZG", [128, ZC, T], BF16)
            if rope:
                self.dpool([CKVb], [d["cache_ckvT"]], CKVb[:, 0:512], d["cache_ckvT"].h[:, :])
                self.dpool([KPEb], [d["cache_kpeT"]], KPEb[:, 0:512], d["cache_kpeT"].h[:, :])
            with self.scope() as s2:
                WH = self.sb(s2, "WHm", [128, 8, 416], BF16)
                self.dpool([WH], [d["mla_wh"]], WH[:, :, :], d["mla_wh"].h.ap().rearrange("(k p) e -> p k e", p=128))
                if rope:
                    CS32 = [self.sb(s2, "CS32_%d" % t, [32, 2, 512], F32) for t in range(2)]
                    R32 = self.sb(s2, "R32", [32, 32], F32)
                    self.dsp([R32], [d["rope_R32"]], R32[:, :], d["rope_R32"].h[:, :])
                WZ = self.sb(s2, "WZ", [128, 8, ZC * 128], BF16)
                for cb in range(max(1, ZC * 128 // 512)):
                    w_ = min(512, ZC * 128)
                    self.dpool([WZ], [wz], WZ[:, :, cb * w_:(cb + 1) * w_],
                               wz.h.ap().rearrange("(k p) e -> p k e", p=128)[:, :, cb * w_:(cb + 1) * w_])
                HB = [self.sb(s2, "HBm%d" % t, [128, 8, 512], BF16) for t in range(min(2, nblk))]
                QC = self.sb(s2, "QC", [128, 2, 512], F32)
                KVC = self.sb(s2, "KVC", [128, 512], F32)
                KPF = self.sb(s2, "KPF", [32, 512], F32)
                SQ = self.sb(s2, "SQm", [128, 3, 512], F32)
                RS = self.sb(s2, "RSm", [128, 2, 512], F32)
                CKVf = self.sb(s2, "CKVf", [128, 512], F32)
                T1 = self.sb(s2, "T1m", [32, 512], F32)
                T2 = self.sb(s2, "T2m", [32, 512], F32)
                for tb in range(nblk):
                    ts_ = slice(tb * 512, (tb + 1) * 512)
                    hb = HB[tb % len(HB)]
                    self.make_h(i, n, hb, lambda k, hb=hb: hb[:, k, :], X, tb * 512, 512)
                    if rope:
                        cs32 = CS32[tb % 2]
                        self.dsp([cs32], [d["rope_cs32"]], cs32[:, :, :], d["rope_cs32"].h[:, :, ts_])
                    for m in range(3):
                        bank = self.ps()
                        for k in range(KC):
                            self.P(bank, bank[:, :], WH[:, k, m * 128:(m + 1) * 128], hb[:, k, :], k == 0, k == KC - 1, [WH, hb])
                        if m < 2:
                            self.act(QC, QC[:, m, :], bank, bank[:, :], AF.Copy)
                        else:
                            self.act(KVC, KVC[:, :], bank, bank[:, :], AF.Copy)
                    bank = self.ps()
                    for k in range(KC):
                        self.P(bank, bank[0:32, :], WH[:, k, 384:416], hb[:, k, :], k == 0, k == KC - 1, [WH, hb])
                    self.act(KPF, KPF[:, :], bank, bank[0:32, :], AF.Copy)
                    for zc in range(ZC):
                        bank = self.ps()
                        for k in range(KC):
                            self.P(bank, bank[:, :], WZ[:, k, zc * 128:(zc + 1) * 128], hb[:, k, :], k == 0, k == KC - 1, [WZ, hb])
                        self.act(ZG, ZG[:, zc, ts_], bank, bank[:, :], AF.Silu)
                    self.act(SQ, SQ[:, 0:2, :], QC, QC[:, :, :], AF.Square)
                    self.act(SQ, SQ[:, 2, :], KVC, KVC[:, :], AF.Square)
                    bq = self.ps()
                    for rk in range(2):
                        self.P(bq, bq[:, :], self.ONESF[:, :], SQ[:, rk, :], rk == 0, rk == 1, [self.ONESF, SQ])
                    bk = self.ps()
                    self.P(bk, bk[:, :], self.ONESF[:, :], SQ[:, 2, :], True, True, [self.ONESF, SQ])
                    self.act(RS, RS[:, 0, :], bq, bq[:, :], AF.Sqrt, scale=1.0 / 256, bias=self.EPS[:, 1:2], extra_reads=[self.EPS])
                    self.act(RS, RS[:, 1, :], bk, bk[:, :], AF.Sqrt, scale=1.0 / 128, bias=self.EPS[:, 1:2], extra_reads=[self.EPS])
                    self.recip(RS, RS[:, :, :], RS, RS[:, :, :])
                    for rk in range(2):
                        self.stt(QN, QN[:, rk, ts_], QC, QC[:, rk, :], NRM[:, rk:rk + 1], RS, RS[:, 0, :], ALU.mult, ALU.mult,
                                 extra_reads=[NRM])
                    self.stt(CKVf, CKVf[:, :], KVC, KVC[:, :], NRM[:, 2:3], RS, RS[:, 1, :], ALU.mult, ALU.mult, extra_reads=[NRM])
                    self.cp(CKVb, CKVb[:, koff + tb * 512:koff + (tb + 1) * 512], CKVf, CKVf[:, :])
                    if not rope:
                        self.cp(KPEb, KPEb[:, ts_], KPF, KPF[:, :])
                        o = self.dout
                        self.dsp([o["ckv_out"]], [CKVf], o["ckv_out"].h[:, ts_], CKVf[:, :])
                        self.dsp([o["kpe_out"]], [KPF], o["kpe_out"].h[:, ts_], KPF[:, :])
                    else:
                        bank = self.ps()
                        self.P(bank, bank[0:32, :], R32[:, :], KPF[:, :], True, True, [R32, KPF])
                        self.tt(T1, T1[:, :], KPF, KPF[:, :], cs32, cs32[:, 0, :], ALU.mult)
                        self.tt(T2, T2[:, :], bank, bank[0:32, :], cs32, cs32[:, 1, :], ALU.mult)
                        self.tt(KPEb, KPEb[:, koff + tb * 512:koff + (tb + 1) * 512], T1, T1[:, :], T2, T2[:, :], ALU.add)
            QT = self.sb(st, "QT", [96, NH, T], BF16)
            KT = self.sb(st, "KT", [96, NH, KA], BF16)
            VT = self.sb(st, "VT", [128, KA // 128, NH, 128], BF16)
            NEGC = self.sb(st, "NEGC", [128, NH], F32)
            with self.scope() as s2:
                WQ = self.sb(s2, "WQ", [128, 2, NH * 96], BF16)
                WK = self.sb(s2, "WK", [128, NH * 64], BF16)
                WV = self.sb(s2, "WV", [128, NH * 64], BF16)
                self.dpool([WQ], [wq], WQ[:, :, :], wq.h.ap().rearrange("(k p) e -> p k e", p=128))
                self.dpool([WK], [wk], WK[:, :], wk.h[:, :])
                self.dpool([WV], [wv], WV[:, :], wv.h[:, :])
                if rope:
                    CS96 = [self.sb(s2, "CS96_%d" % t, [96, 2, 512], F32) for t in range(2)]
                    R96 = self.sb(s2, "R96", [96, 96], F32)
                    self.dsp([R96], [d["rope_R96"]], R96[:, :], d["rope_R96"].h[:, :])
                    csi = 0
                QF = self.sb(s2, "QF", [96, 512], F32)
                T1 = self.sb(s2, "T1n", [96, 512], F32)
                T2 = self.sb(s2, "T2n", [96, 512], F32)
                SQ2 = self.sb(s2, "SQ2", [96, 512], F32)
                MX = self.sb(s2, "MX", [128, 16], F32)
                MQK = self.sb(s2, "MQK", [128, 2], F32)
                self.memset(VT, VT[:, :, :, :], 1.0)
                for kc4 in range(KA // 512):
                    for h in range(NH):
                        bank = self.ps()
                        self.P(bank, bank[0:64, :], WK[:, h * 64:(h + 1) * 64], CKVb[:, kc4 * 512:(kc4 + 1) * 512], True, True, [WK, CKVb])
                        self.act(KT, KT[0:64, h, kc4 * 512:(kc4 + 1) * 512], bank, bank[0:64, :], AF.Copy)
                for h in range(NH):
                    self.cp(KT, KT[64:96, h, :], KPEb, KPEb[0:32, :])
                for kc in range(KA // 128):
                    for vb in range(max(1, NH * 64 // 512)):
                        w_ = min(512, NH * 64)
                        nh_ = w_ // 64
                        bank = self.ps()
                        self.P(bank, bank[:, 0:w_], CKVb[:, kc * 128:(kc + 1) * 128], WV[:, vb * w_:(vb + 1) * w_], True, True, [CKVb, WV])
                        src = bank[:, 0:w_].rearrange("p (h v) -> p h v", v=64)
                        h0 = vb * nh_
                        self.cp(VT, VT[:, kc, h0:h0 + nh_:2, 0:64], bank, src[:, 0::2, :])
                        self.cp(VT, VT[:, kc, h0 + 1:h0 + nh_:2, 64:128], bank, src[:, 1::2, :])
                for h in range(NH):
                    for tb in range(nblk):
                        ts_ = slice(tb * 512, (tb + 1) * 512)
                        bank = self.ps()
                        for rk in range(2):
                            self.P(bank, bank[0:96, :], WQ[:, rk, h * 96:(h + 1) * 96], QN[:, rk, ts_], rk == 0, rk == 1, [WQ, QN])
                        if rope:
                            cs96 = CS96[csi % 2]
                            csi += 1
                            self.dsp([cs96], [d["rope_cs96"]], cs96[:, :, :], d["rope_cs96"].h[:, :, ts_])
                            self.act(QF, QF[:, :], bank, bank[0:96, :], AF.Copy)
                            b2 = self.ps()
                            self.P(b2, b2[0:96, :], R96[:, :], QF[:, :], True, True, [R96, QF])
                            self.tt(T1, T1[:, :], QF, QF[:, :], cs96, cs96[:, 0, :], ALU.mult)
                            self.tt(T2, T2[:, :], b2, b2[0:96, :], cs96, cs96[:, 1, :], ALU.mult)
                            self.tt(QT, QT[:, h, ts_], T1, T1[:, :], T2, T2[:, :], ALU.add)
                        else:
                            self.act(QT, QT[:, h, ts_], bank, bank[0:96, :], AF.Copy)
                    nq_ = T // 512
                    nk_ = KA // 512
                    for b_ in range(nq_ + nk_):
                        if b_ < nq_:
                            src_t, src = QT, QT[:, h, b_ * 512:(b_ + 1) * 512]
                        else:
                            src_t, src = KT, KT[:, h, (b_ - nq_) * 512:(b_ - nq_ + 1) * 512]
                        self.act(SQ2, SQ2[:, :], src_t, src, AF.Square)
                        bank = self.ps()
                        self.P(bank, bank[:, :], self.ONESF[0:96, :], SQ2[:, :], True, True, [self.ONESF, SQ2])
                        self.V([MX], [bank, MX], lambda e, o=MX[:, b_:b_ + 1], i_=bank[:, :]: e.tensor_reduce(
                            out=o, in_=i_, axis=mybir.AxisListType.X, op=ALU.max))
                    self.V([MQK], [MX, MQK], lambda e, o=MQK[:, 0:1], i_=MX[:, 0:nq_]: e.tensor_reduce(
                        out=o, in_=i_, axis=mybir.AxisListType.X, op=ALU.max))
                    self.V([MQK], [MX, MQK], lambda e, o=MQK[:, 1:2], i_=MX[:, nq_:nq_ + nk_]: e.tensor_reduce(
                        out=o, in_=i_, axis=mybir.AxisListType.X, op=ALU.max))
                    self.tt(MQK, MQK[:, 0:1], MQK, MQK[:, 0:1], MQK, MQK[:, 1:2], ALU.mult)
                    self.act(MQK, MQK[:, 0:1], MQK, MQK[:, 0:1], AF.Sqrt)
                    self.ts(NEGC, NEGC[:, h:h + 1], MQK, MQK[:, 0:1], -SCALE, None, ALU.mult)
            with self.scope() as s2:
                ET = [self.sb(s2, "ET%d" % t, [128, 512], BF16) for t in range(4)]
                RD = self.sb(s2, "RD", [128, 512], F32)
                T1 = self.sb(s2, "T1a", [128, 512], F32)
                eti = 0
                for (q0, nq, k0, nk) in units:
                    QB = min(512, nq)
                    for h in range(NH):
                        ch = h // 2
                        for qb in range(nq // QB):
                            qs = slice(q0 + qb * QB, q0 + (qb + 1) * QB)
                            obank = self.ps(hold=True)
                            nkc = nk // 128
                            for kc in range(nkc):
                                sbank = self.ps()
                                self.P(sbank, sbank[:, 0:QB], KT[0:96, h, k0 + kc * 128:k0 + (kc + 1) * 128], QT[0:96, h, qs],
                                       True, True, [KT, QT])
                                et = ET[eti % 4]
                                eti += 1
                                self.act(et, et[:, 0:QB], sbank, sbank[:, 0:QB], AF.Exp, scale=SCALE, bias=NEGC[:, h:h + 1],
                                         extra_reads=[NEGC])
                                self.P(obank, obank[:, 0:QB], VT[:, k0 // 128 + kc, h, :], et[:, 0:QB], kc == 0, kc == nkc - 1,
                                       [VT, et])
                            if h % 2 == 0:
                                vs, ds_ = slice(0, 64), slice(64, 128)
                            else:
                                vs, ds_ = slice(64, 128), slice(0, 64)
                            self.recip(RD, RD[vs, 0:QB], obank, obank[ds_, 0:QB])
                            self.tt(T1, T1[vs, 0:QB], obank, obank[vs, 0:QB], RD, RD[vs, 0:QB], ALU.mult)
                            self.tt(OUT, OUT[vs, ch, qs], T1, T1[vs, 0:QB], ZG, ZG[vs, ch, qs], ALU.mult)
                            self.release(obank)

    def rwkv_layer(self, i, j):
        import os
        d = self.din
        CDEC = float(np.exp(-0.5))
        with self.scope() as st:
            OP = self.sb(st, "OPr", [128, 8, TP], BF16)
            with self.scope() as sc:
                MU = self.sb(sc, "MU", [128, 6, 8], F32)
                M1 = self.sb(sc, "M1", [64, 2, 192], F32)
                M2 = self.sb(sc, "M2m", [64, 2, 192], F32)
                MF = self.sb(sc, "MF", [128, 2, 128], F32)
                RST = self.sb(sc, "RST", [128, 512], F32)
                II2 = self.sb(sc, "II2", [128, 64], F32)
                BD = self.sb(sc, "BD", [128, 128], F32)
                self.dsp([MU], [d["rw_muT"]], MU[:, :, :], d["rw_muT"].h[:, :, :])
                for dr in range(2):
                    self.dsp([M1], [d["rw_mask1"]], M1[:, dr, :], d["rw_mask1"].h[dr])
                    self.dsp([M2], [d["rw_mask2"]], M2[:, dr, :], d["rw_mask2"].h[dr])
                    self.dsp([MF], [d["rw_maskf"]], MF[:, dr, :], d["rw_maskf"].h[dr])
                self.dsp([RST], [d["rw_reset"]], RST[:, :], d["rw_reset"].h[:, :])
                self.dsp([II2], [d["rw_ii2"]], II2[:, :], d["rw_ii2"].h[:, :])
                self.dsp([BD], [d["rw_bd"]], BD[:, :], d["rw_bd"].h[:, :])
                W1 = self.sb(sc, "W1r", [128, 8, 2, 2, 64], BF16)
                W1s = self.sb(sc, "W1s", [128, 8, 2, 2, 64], BF16)
                for dr in range(2):
                    self.dpool([W1], [d["rw_w1"]], W1[:, :, 0, dr, :], d["rw_w1"].h[dr].rearrange("(k p) r -> p k r", p=128))
                    self.dpool([W1], [d["rw_a1"]], W1[:, :, 1, dr, :], d["rw_a1"].h[dr].rearrange("(k p) r -> p k r", p=128))
                for k in range(KC):
                    self.act(W1s, W1s[:, k, 0, :, :], W1, W1[:, k, 0, :, :], AF.Identity, scale=MU[:, 1, k:k + 1], extra_reads=[MU])
                    self.act(W1s, W1s[:, k, 1, :, :], W1, W1[:, k, 1, :, :], AF.Identity, scale=MU[:, 4, k:k + 1], extra_reads=[MU])
                ctx = dict(i=i, MU=MU, M1=M1, M2=M2, MF=MF, RST=RST, II2=II2, BD=BD, W1=W1, W1s=W1s, CDEC=CDEC)
                with self.scope() as sg:
                    W2c = [self.sb(sg, "W2p%d" % t, [64, 2, 2, 128], BF16) for t in range(2)]
                    VEC = self.sb(sg, "VECp", [128, 8, 9], F32)
                    self.dsp([VEC], [d["rw_vec"]], VEC[:, :, :], d["rw_vec"].h[:, :, :])
                    HT = self.sb(sg, "HTp", [128, 8, 2, SEQ], BF16)
                    DT = self.sb(sg, "DTp", [128, 8, 2, SEQ], BF16)
                    LD = self.sb(sg, "LDp", [64, 2, 2, TP], BF16)
                    for s_ in range(2):
                        self.rw_make_hd(ctx, sg, 0, self.XP, s_ * SEQ, SEQ, 0, SEQ, HT[:, :, s_, :], DT[:, :, s_, :], HT, DT, "p%d" % s_)
                        self.rw_lora_down(ctx, HT, DT, lambda k, s_=s_: HT[:, k, s_, :], lambda k, s_=s_: DT[:, k, s_, :], SEQ,
                                          LD, lambda wa, dr, s_=s_: LD[:, wa, dr, s_ * SEQ:(s_ + 1) * SEQ], (0, 1))
                    WBs = [self.sb(sg, "WBr%d" % t, [128, 8, 512], BF16) for t in range(2)]
                    WSs = [self.sb(sg, "WSr0", [128, 8, 512], BF16)]
                    SST = self.sb(sg, "SSTp", [64, 4, 64], F32)
                    import os
                    for c in range(int(os.environ.get("RW_MAXC", "8"))):
                        wb, ws = WBs[c % 2], WSs[0]
                        self.dpool([wb], [d["rw_win"]], wb[:, :, :],
                                   d["rw_win"].h.ap().rearrange("(k p) e -> p k e", p=128)[:, :, c * 512:(c + 1) * 512])
                        self.rw_scale_w(ctx, wb, ws)
                        W2 = W2c[c % 2]
                        for dr in range(2):
                            self.dpool([W2], [d["rw_w2"]], W2[:, 0, dr, :], d["rw_w2"].h[dr][:, c * 128:(c + 1) * 128])
                            self.dpool([W2], [d["rw_a2"]], W2[:, 1, dr, :], d["rw_a2"].h[dr][:, c * 128:(c + 1) * 128])
                        for s_ in range(2):
                            self.memset(SST, SST[:, :, :], 0.0)
                            self.rw_pass(ctx, 0, NT=SEQ, hT=lambda k, s_=s_: HT[:, k, s_, :], dT=lambda k, s_=s_: DT[:, k, s_, :],
                                         HTt=HT, DTt=DT, wb=wb, ws=ws, wc0=0,
                                         LD=LD, ld=lambda wa, dr, s_=s_: LD[:, wa, dr, s_ * SEQ:(s_ + 1) * SEQ],
                                         W2=W2, w2=lambda wa, dr, W2=W2: W2[:, wa, dr, :],
                                         VEC=VEC, vec=VEC[:, c, :], dirs=(0, 1), final=True, first_block=True,
                                         SST=SST, YF=None, yf=None, OUT=OP, out=OP[:, c, s_ * SEQ:(s_ + 1) * SEQ],
                                         st_out=(s_, c), tag="p")
                self.end_block("rwkv_prompt")
                OS = self.sb(sc, "OSr", [128, 2, TS], BF16)
                with self.scope() as sg:
                    W2 = self.sb(sg, "W2s", [64, 2, 2, 256], BF16)
                    for dr in range(2):
                        self.dpool([W2], [d["rw_w2_own"]], W2[:, 0, dr, :], d["rw_w2_own"].h[dr])
                        self.dpool([W2], [d["rw_a2_own"]], W2[:, 1, dr, :], d["rw_a2_own"].h[dr])
                    VEC = self.sb(sg, "VECs", [128, 2, 9], F32)
                    self.dsp([VEC], [d["rw_vec_own"]], VEC[:, :, :], d["rw_vec_own"].h[:, :, :])
                    WB = self.sb(sg, "WBo", [128, 8, 1024], BF16)
                    WS = self.sb(sg, "WSo", [128, 8, 1024], BF16)
                    for cb in range(2):
                        self.dpool([WB], [d["rw_win_own"]], WB[:, :, cb * 512:(cb + 1) * 512],
                                   d["rw_win_own"].h.ap().rearrange("(k p) e -> p k e", p=128)[:, :, cb * 512:(cb + 1) * 512])
                    for cb in range(2):
                        self.rw_scale_w(ctx, WB, WS, cb * 512)
                    SSTs = self.sb(sg, "SSTs", [64, 2, 2, 2, 64], F32)
                    for jc in range(2):
                        for dr in range(2):
                            for hl in range(2):
                                self.dsp([SSTs], [d["rw_state"]], SSTs[:, jc, dr, hl, :], d["rw_state"].h[dr, jc * 2 + hl])
                    YF = self.scratch("yf_scr%d" % i, [2, 128, TS], F32)
                    HT = self.sb(sg, "HTs", [128, 8, SEQ], BF16)
                    DT = self.sb(sg, "DTs", [128, 8, SEQ], BF16)
                    LD = self.sb(sg, "LDs", [64, 2, 2, SEQ], BF16)
                    for dr in range(2 if os.environ.get("RW_NOSAMPLE", "0") == "0" else 0):
                        order = range(8) if dr == 0 else range(7, -1, -1)
                        for tb in order:
                            self.rw_make_hd(ctx, sg, 1, self.XS, 0, TS, tb * SEQ, SEQ, HT[:, :, :], DT[:, :, :], HT, DT, "s")
                            self.rw_lora_down(ctx, HT, DT, lambda k: HT[:, k, :], lambda k: DT[:, k, :], SEQ,
                                              LD, lambda wa, dr_: LD[:, wa, dr_, :], (0, 1) if dr == 1 else (0,))
                            for jc in range(2):
                                self.rw_pass(ctx, 1, NT=SEQ, hT=lambda k: HT[:, k, :], dT=lambda k: DT[:, k, :], HTt=HT, DTt=DT,
                                             wb=WB, ws=WS, wc0=jc * 512, LD=LD, ld=lambda wa, dr_: LD[:, wa, dr_, :],
                                             W2=W2, w2=lambda wa, dr_, jc=jc: W2[:, wa, dr_, jc * 128:(jc + 1) * 128],
                                             VEC=VEC, vec=VEC[:, jc, :], dirs=(dr,), final=(dr == 1), first_block=False,
                                             SST=SSTs, sst=lambda dr_, hl, jc=jc: SSTs[:, jc, dr_, hl, :],
                                             YF=YF, yf=YF.h[jc][:, tb * SEQ:(tb + 1) * SEQ], OUT=OS,
                                             out=OS[:, jc, tb * SEQ:(tb + 1) * SEQ], st_out=None, tag="s")
                if not os.environ.get("RW_ONLY") or os.environ.get("RW_GATHER"):
                    self.gather_and_outproj(i, sc, (d["rw_wout"], d["rw_wout"].h[j]), OP, OS)
            self.end_block("rwkv%d" % i)

    def rw_scale_w(self, ctx, wb, ws, c0=0):
        MU = ctx["MU"]
        muidx = (0, 2, 3, 5)
        n = 0
        for k in range(KC):
            for p4 in range(4):
                cs = slice(c0 + p4 * 128, c0 + (p4 + 1) * 128)
                if n % 2 == 0:
                    self.act(ws, ws[:, k, cs], wb, wb[:, k, cs], AF.Identity, scale=MU[:, muidx[p4], k:k + 1], extra_reads=[MU])
                else:
                    self.ts(ws, ws[:, k, cs], wb, wb[:, k, cs], MU[:, muidx[p4], k:k + 1], None, ALU.mult, extra_reads=[MU])
                n += 1

    def rw_make_hd(self, ctx, sc, n, X, seq0, seqlen, t0, NT, ht, dt, HTt, DTt, tag):
        i = ctx["i"]
        with self.scope() as s2:
            HF = [self.sb(s2, "HF%d" % t, [128, NT + 2], F32) for t in range(2)]
            TT = [self.sb(s2, "TTh%d" % t, [128, NT], F32) for t in range(2)]
            lo = max(t0 - 1, 0)
            hi = min(t0 + NT + 1, seqlen)
            off = lo - (t0 - 1)
            for k in range(KC):
                hf, tt_ = HF[k % 2], TT[k % 2]
                if off > 0:
                    self.memset(hf, hf[:, 0:1], 0.0)
                if hi < t0 + NT + 1:
                    self.memset(hf, hf[:, NT + 1:NT + 2], 0.0)
                self.act(hf, hf[:, off:off + hi - lo], X, X[:, k, seq0 + lo:seq0 + hi], AF.Identity,
                         scale=self.MOD[:, i, 8 + k, n:n + 1], bias=self.MOD[:, i, k, n:n + 1], extra_reads=[self.MOD])
                self.tt(tt_, tt_[:, :], hf, hf[:, 0:NT], hf, hf[:, 2:NT + 2], ALU.add)
                self.stt(DTt, dt[:, k, :], tt_, tt_[:, :], 0.5, hf, hf[:, 1:NT + 1], ALU.mult, ALU.subtract)
                self.act(HTt, ht[:, k, :], hf, hf[:, 1:NT + 1], AF.Copy)

    def rw_lora_down(self, ctx, HTt, DTt, hT, dT, NT, LD, ld, dirs):
        W1, W1s = ctx["W1"], ctx["W1s"]
        for wa in range(2):
            for dr in dirs:
                bank = self.ps()
                for k in range(KC):
                    self.P(bank, bank[0:64, 0:NT], W1[:, k, wa, dr, :], hT(k), k == 0, False, [W1, HTt])
                    self.P(bank, bank[0:64, 0:NT], W1s[:, k, wa, dr, :], dT(k), False, k == KC - 1, [W1s, DTt])
                self.act(LD, ld(wa, dr), bank, bank[0:64, 0:NT], AF.Tanh if wa == 0 else AF.Copy)

    def rw_pass(self, ctx, n, NT, hT, dT, HTt, DTt, wb, ws, wc0, LD, ld, W2, w2, VEC, vec, dirs, final, first_block,
                SST, YF, yf, OUT, out, st_out, tag, sst=None):
        d = self.din
        i = ctx["i"]
        CDEC = ctx["CDEC"]
        M1, M2, MF, RST, II2, BD = ctx["M1"], ctx["M2"], ctx["MF"], ctx["RST"], ctx["II2"], ctx["BD"]
        NCH = NT // 64
        NU = 2 * NCH
        ND = len(dirs)
        import os
        DBG = int(os.environ.get("RW_DBG", "9"))
        c3 = lambda t_: t_[:, :].rearrange("p (c v) -> p c v", v=64)
        with self.scope() as sp_:
            f32t = lambda nm, shp=None, dt=F32: self.sb(sp_, nm + tag, shp or [128, NT], dt)
            R, K, V, KK = f32t("R"), f32t("K"), f32t("V"), f32t("KK")
            TMP, TM2, TM3 = f32t("TMP"), f32t("TM2"), f32t("TM3")
            SG, AS, CS = f32t("SG"), f32t("AS"), f32t("CS")
            OMK = f32t("OMK", [128, 1])
            TOT = f32t("TOT", [128, NCH, 1])
            ET_ = f32t("ETOT", [128, NCH, 1])
            XL = f32t("XL", [128, NCH, 192], F32R)
            XR = f32t("XR", [128, NCH, 192], F32R)
            GS = []
            for g_ in range(2):
                GS.append((f32t("XMN%d" % g_, [64, 4, 256], F32R), f32t("LT%d" % g_, [64, 4, 192], F32R)))
            STK = f32t("STK", [128, ND, NU, 64], F32R)
            YD = f32t("YD", [128, ND, NT])
            FULL = f32t("FULL", [128, ND, NU, 128], F32R)
            ATOT = f32t("ATOT") if final else None
            ZS = self.sb(sp_, "ZS" + tag, [128, NT], BF16) if final else None
            dests = [R, K, V] + ([ZS] if final else [])
            for p4, dst in enumerate(dests):
                bank = self.ps()
                cs = slice(wc0 + p4 * 128, wc0 + (p4 + 1) * 128)
                for k in range(KC):
                    self.P(bank, bank[:, 0:NT], wb[:, k, cs], hT(k), k == 0, False, [wb, HTt])
                    self.P(bank, bank[:, 0:NT], ws[:, k, cs], dT(k), False, k == KC - 1, [ws, DTt])
                self.act(dst, dst[:, :], bank, bank[:, 0:NT], AF.Silu if p4 == 3 else AF.Copy)
            self.act(TMP, TMP[:, :], K, K[:, :], AF.Identity, scale=vec[:, 4:5], extra_reads=[VEC])
            self.act(TM2, TM2[:, :], TMP, TMP[:, :], AF.Square)
            bank = self.ps()
            self.P(bank, bank[:, 0:NT], BD[:, :], TM2[:, :], True, True, [BD, TM2])
            self.act(TM2, TM2[:, :], bank, bank[:, 0:NT], AF.Sqrt, scale=1.0, bias=self.EPS[:, 3:4], extra_reads=[self.EPS])
            self.recip(TM2, TM2[:, :], TM2, TM2[:, :])
            self.tt(KK, KK[:, :], TMP, TMP[:, :], TM2, TM2[:, :], ALU.mult)
            self.ts(OMK, OMK[:, 0:1], VEC, vec[:, 5:6], -1.0, 1.0, ALU.mult, ALU.add)
            if DBG < 1:
                return
            for hl in range(2):
                hs = slice(hl * 64, (hl + 1) * 64)
                bank = self.ps()
                for ch in range(NCH):
                    self.P(bank, bank[0:64, ch * 64:(ch + 1) * 64], V[hs, ch * 64:(ch + 1) * 64], II2[hs, :], True, True, [V, II2])
                for di in range(ND):
                    self.act(STK, STK[64:128, di, hl * NCH:(hl + 1) * NCH, :], bank,
                             bank[0:64, 0:NCH * 64].rearrange("p (c v) -> p c v", v=64), AF.Copy)
            for dr in ((0, 1) if final else dirs):
                chain = dr in dirs
                bank = self.ps()
                self.P(bank, bank[:, 0:NT], w2(1, dr), ld(1, dr), True, True, [W2, LD])
                self.act(AS, AS[:, :], bank, bank[:, 0:NT], AF.Sigmoid, bias=vec[:, 2 + dr:3 + dr], extra_reads=[VEC])
                self.act(TMP, TMP[:, :], AS, AS[:, :], AF.Identity, scale=vec[:, 5:6], bias=OMK[:, 0:1], extra_reads=[VEC, OMK])
                if final:
                    if dr == 0:
                        self.cp(ATOT, ATOT[:, :], TMP, TMP[:, :])
                    else:
                        self.tt(ATOT, ATOT[:, :], ATOT, ATOT[:, :], TMP, TMP[:, :], ALU.add)
                if not chain:
                    continue
                di = dirs.index(dr)
                self.tt(XL, XL[:, :, 64:128], K, c3(K), TMP, c3(TMP), ALU.mult)
                self.tt(XL, XL[:, :, 128:192], KK, c3(KK), AS, c3(AS), ALU.mult)
                bank = self.ps()
                self.P(bank, bank[:, 0:NT], w2(0, dr), ld(0, dr), True, True, [W2, LD])
                self.act(SG, SG[:, :], bank, bank[:, 0:NT], AF.Sigmoid, bias=vec[:, dr:dr + 1], extra_reads=[VEC])
                self.V([CS], [RST, SG], lambda e, o=CS[:, :], a=RST[:, 0:NT], b=SG[:, :]: e.tensor_tensor_scan(
                    out=o, data0=a, data1=b, initial=0.0, op0=ALU.mult, op1=ALU.add))
                self.cp(TOT, TOT[:, :, :], CS, c3(CS)[:, :, 63:64])
                if dr == 1:
                    self.tt(TM2, TM2[:, :], SG, SG[:, :], CS, CS[:, :], ALU.subtract)
                    self.tt(CS, c3(CS), TM2, c3(TM2), TOT, TOT[:, :, :].to_broadcast([128, NCH, 64]), ALU.add)
                self.tt(TM2, TM2[:, :], CS, CS[:, :], SG, SG[:, :], ALU.subtract)
                self.act(TM3, TM3[:, :], TM2, TM2[:, :], AF.Exp, scale=-CDEC)
                self.stt(XR, XR[:, :, 128:192], KK, c3(KK), -1.0, TM3, c3(TM3), ALU.mult, ALU.mult)
                self.act(TM3, TM3[:, :], CS, CS[:, :], AF.Exp, scale=-CDEC)
                self.tt(XR, XR[:, :, 64:128], R, c3(R), TM3, c3(TM3), ALU.mult)
                self.act(TM3, TM3[:, :], CS, CS[:, :], AF.Exp, scale=CDEC)
                self.tt(XL, XL[:, :, 64:128], XL, XL[:, :, 64:128], TM3, c3(TM3), ALU.mult)
                self.tt(XL, XL[:, :, 128:192], XL, XL[:, :, 128:192], TM3, c3(TM3), ALU.mult)
                self.act(ET_, ET_[:, :, :], TOT, TOT[:, :, :], AF.Exp, scale=-CDEC)
                self.cp(XL, XL[:, :, 0:64], II2, II2[:, :].unsqueeze(1).to_broadcast([128, NCH, 64]))
                self.tt(XR, XR[:, :, 0:64], II2, II2[:, :].unsqueeze(1).to_broadcast([128, NCH, 64]),
                        ET_, ET_[:, :, :].to_broadcast([128, NCH, 64]), ALU.mult)
                if DBG < 2:
                    continue
                units = [(hl, ch) for hl in range(2) for ch in range(NCH)]
                groups = [units[g0:g0 + 4] for g0 in range(0, NU, 4)]
                ngl = len(GS)
                for gi, grp in enumerate(groups):
                    XMN, LTt = GS[gi % ngl]
                    b1 = [self.ps(hold=True), self.ps(hold=True)]
                    b2 = [self.ps(hold=True), self.ps(hold=True)]
                    for ui, (hl, ch) in enumerate(grp):
                        hs = slice(hl * 64, (hl + 1) * 64)
                        o1 = b1[ui // 2][0:64, (ui % 2) * 192:(ui % 2) * 192 + 192]
                        o2 = b2[ui // 2][0:64, (ui % 2) * 192:(ui % 2) * 192 + 192]
                        self.P(b1[ui // 2], o1, XL[hs, ch, 128:192], XR[hs, ch, 0:192], True, True, [XL, XR])
                        self.P(b2[ui // 2], o2, XR[hs, ch, 128:192], XL[hs, ch, 0:192], True, True, [XL, XR])
                    for hb_ in range(2):
                        self.tt(XMN, XMN[:, hb_ * 2:hb_ * 2 + 2, 0:192], b1[hb_], b1[hb_][0:64, 0:384].rearrange("p (u v) -> p u v", v=192),
                                M1, M1[:, dr:dr + 1, :].to_broadcast([64, 2, 192]), ALU.mult)
                        self.tt(LTt, LTt[:, hb_ * 2:hb_ * 2 + 2, :], b2[hb_], b2[hb_][0:64, 0:384].rearrange("p (u v) -> p u v", v=192),
                                M2, M2[:, dr:dr + 1, :].to_broadcast([64, 2, 192]), ALU.mult)
                    for b_ in b1 + b2:
                        self.release(b_)
                    self.act(XMN, XMN[:, :, 192:256], LTt, LTt[:, :, 128:192], AF.Copy)
                for gbase in range(0, len(groups), ngl):
                    gsel = list(range(gbase, min(gbase + ngl, len(groups))))
                    for lev in range(6):
                        banks = {}
                        for gi in gsel:
                            XMN, LTt = GS[gi % ngl]
                            bk = [self.ps(), self.ps()]
                            for ui in range(4):
                                bb_ = bk[ui // 2]
                                c0 = (ui % 2) * 256
                                if lev < 5:
                                    self.P(bb_, bb_[0:64, c0:c0 + 192], XMN[:, ui, 192:256], XMN[:, ui, 0:192], True, True, [XMN], fast=True)
                                    self.P(bb_, bb_[0:64, c0 + 192:c0 + 256], XMN[:, ui, 128:192], XMN[:, ui, 192:256], True, True, [XMN], fast=True)
                                else:
                                    self.P(bb_, bb_[0:64, c0:c0 + 128], XMN[:, ui, 192:256], XMN[:, ui, 0:128], True, True, [XMN], fast=True)
                            banks[gi] = bk
                        for gi in gsel:
                            XMN, LTt = GS[gi % ngl]
                            for hb_ in range(2):
                                src = banks[gi][hb_][0:64, :].rearrange("p (u v) -> p u v", v=256)
                                if lev < 5:
                                    self.act(XMN, XMN[:, hb_ * 2:hb_ * 2 + 2, 128:256], banks[gi][hb_], src[:, :, 128:256], AF.Copy)
                                self.tt(XMN, XMN[:, hb_ * 2:hb_ * 2 + 2, 0:128], XMN, XMN[:, hb_ * 2:hb_ * 2 + 2, 0:128],
                                        banks[gi][hb_], src[:, :, 0:128], ALU.add)
                    for gi in gsel:
                        XMN, LTt = GS[gi % ngl]
                        grp = groups[gi]
                        fbank = self.ps()
                        for ui, (hl, ch) in enumerate(grp):
                            hs = slice(hl * 64, (hl + 1) * 64)
                            self.P(fbank, fbank[:, ui * 128:(ui + 1) * 128], XL[hs, ch, 0:128], XR[hs, ch, 0:128], True, False, [XL, XR])
                            self.P(fbank, fbank[:, ui * 128:(ui + 1) * 128], LTt[:, ui, 0:128], XMN[:, ui, 0:128], False, True, [LTt, XMN],
                                   serial=True)
                        u0 = grp[0][0] * NCH + grp[0][1]
                        self.tt(FULL, FULL[:, di, u0:u0 + 4, :], fbank, fbank[:, :].rearrange("p (u v) -> p u v", v=128),
                                MF, MF[:, dr:dr + 1, :].to_broadcast([128, 4, 128]), ALU.mult)
            for di, dr in enumerate(dirs):
                for hl in range(2):
                    u_first = hl * NCH + (0 if dr == 0 else NCH - 1)
                    src = SST[:, hl * 2 + dr, :] if sst is None else sst(dr, hl)
                    self.cp(STK, STK[0:64, di, u_first, :], SST, src)
            for step in range(NCH):
                for di, dr in enumerate(dirs):
                    ch = step if dr == 0 else NCH - 1 - step
                    nxt = ch + 1 if dr == 0 else ch - 1
                    for hl in range(2):
                        u = hl * NCH + ch
                        hs = slice(hl * 64, (hl + 1) * 64)
                        cbank = self.ps()
                        self.P(cbank, cbank[0:64, 0:64], FULL[:, di, u, 0:64], STK[:, di, u, :], True, True, [FULL, STK], fast=True)
                        self.P(cbank, cbank[0:64, 64:128], STK[:, di, u, :], FULL[:, di, u, 64:128], True, True, [FULL, STK], fast=True)
                        if step < NCH - 1:
                            self.cp(STK, STK[0:64, di, hl * NCH + nxt, :], cbank, cbank[0:64, 0:64])
                        else:
                            dst = SST[:, hl * 2 + dr, :] if sst is None else sst(dr, hl)
                            self.cp(SST, dst, cbank, cbank[0:64, 0:64])
                        self.act(YD, YD[hs, di, ch * 64:(ch + 1) * 64], cbank, cbank[0:64, 64:128], AF.Copy)
            if DBG < 4:
                return
            if st_out is not None:
                s_, c = st_out
                o = self.dout["st_out"]
                for dr in range(2):
                    for hl in range(2):
                        self.dsp([o], [SST], o.h[s_, dr, c * 2 + hl], SST[:, hl * 2 + dr, :])
            if not final:
                self.dsp([YF], [YD], yf, YD[:, 0, :])
                return
            Y, MN_, VR = SG, AS, CS
            if ND == 2:
                self.tt(Y, Y[:, :], YD, YD[:, 0, :], YD, YD[:, 1, :], ALU.add)
            else:
                self.dsp([TM3], [YF], TM3[:, :], yf)
                self.tt(Y, Y[:, :], TM3, TM3[:, :], YD, YD[:, 0, :], ALU.add)
            b1 = self.ps()
            self.P(b1, b1[:, 0:NT], BD[:, :], Y[:, :], True, True, [BD, Y])
            self.act(TM2, TM2[:, :], Y, Y[:, :], AF.Square)
            b2 = self.ps()
            self.P(b2, b2[:, 0:NT], BD[:, :], TM2[:, :], True, True, [BD, TM2])
            self.act(MN_, MN_[:, :], b1, b1[:, 0:NT], AF.Identity, scale=1.0 / 64)
            self.tt(VR, VR[:, :], MN_, MN_[:, :], MN_, MN_[:, :], ALU.mult)
            self.stt(VR, VR[:, :], b2, b2[:, 0:NT], 1.0 / 64, VR, VR[:, :], ALU.mult, ALU.subtract)
            self.act(VR, VR[:, :], VR, VR[:, :], AF.Sqrt, scale=1.0, bias=self.EPS[:, 2:3], extra_reads=[self.EPS])
            self.recip(VR, VR[:, :], VR, VR[:, :])
            self.tt(Y, Y[:, :], Y, Y[:, :], MN_, MN_[:, :], ALU.subtract)
            self.tt(Y, Y[:, :], Y, Y[:, :], VR, VR[:, :], ALU.mult)
            self.act(Y, Y[:, :], Y, Y[:, :], AF.Identity, scale=vec[:, 7:8], bias=vec[:, 8:9], extra_reads=[VEC])
            self.tt(TMP, TMP[:, :], R, R[:, :], K, K[:, :], ALU.mult)
            self.stt(TMP, TMP[:, :], TMP, TMP[:, :], vec[:, 6:7], ATOT, ATOT[:, :], ALU.mult, ALU.mult, extra_reads=[VEC])
            b3 = self.ps()
            self.P(b3, b3[:, 0:NT], BD[:, :], TMP[:, :], True, True, [BD, TMP])
            self.tt(TMP, TMP[:, :], b3, b3[:, 0:NT], V, V[:, :], ALU.mult)
            self.tt(Y, Y[:, :], Y, Y[:, :], TMP, TMP[:, :], ALU.add)
            self.tt(OUT, out, Y, Y[:, :], ZS, ZS[:, :], ALU.mult)

    def epilogue(self):
        o = self.dout
        self.dsp([o["yp"]], [self.XP], o["yp"].h.ap().rearrange("(k p) t -> p k t", p=128), self.XP[:, :, :])
        for k in range(KC):
            self.dsp([o["ys"]], [self.XS], o["ys"].h[k * 128:(k + 1) * 128, :], self.XS[:, k, :])
        self.end_block("epilogue")


_PROG_CACHE = {}


def _get_prog(nl=DEPTH):
    if nl not in _PROG_CACHE:
        p = Prog(nl)
        p.build()
        _PROG_CACHE[nl] = p
    return _PROG_CACHE[nl]


def run_device(inputs, nl=DEPTH):
    prog = _get_prog(nl)
    maps = prep_inputs(inputs)
    names = set(prog.din.keys())
    maps = [{k_: v for k_, v in m.items() if k_ in names} for m in maps]
    res = run_bass_kernel_spmd(prog.nc, maps, core_ids=list(range(8)))
    return res.results


def assemble(results):
    yp = np.zeros((16, SEQ, D), np.float32)
    ys = np.zeros((2, TS, D), np.float32)
    for c in range(8):
        r = results[c]
        b, q = c // 4, c % 4
        yp[2 * c:2 * c + 2] = np.asarray(r["yp"]).T.reshape(2, SEQ, D)
        ys[b, q * 512:(q + 1) * 512] = np.asarray(r["ys"])[:, q * 512:(q + 1) * 512].T
    return yp, ys


def assemble_mla(results):
    ckv = np.zeros((16, 1, SEQ, 128), np.float32)
    kpe = np.zeros((16, 1, SEQ, 32), np.float32)
    for c in range(8):
        r = results[c]
        ckv[2 * c:2 * c + 2, 0] = np.asarray(r["ckv_out"]).T.reshape(2, SEQ, 128)
        kpe[2 * c:2 * c + 2, 0] = np.asarray(r["kpe_out"]).T.reshape(2, SEQ, 32)
    return ckv, kpe


def assemble_state(results):
    st = np.zeros((16, 1, 2, 16, 64, 64), np.float32)
    for c in range(8):
        r = np.asarray(results[c]["st_out"])
        st[2 * c:2 * c + 2, 0] = r.transpose(0, 1, 2, 4, 3)
    return st


def kernel(**inputs):
    results = run_device(inputs)
    yp, ys = assemble(results)
    ckv, kpe = assemble_mla(results)
    st = assemble_state(results)
    return yp, ys, ckv, kpe, st
```
